# Optimizing a Trainium2 kernel written in Bass

```python
import jax, jax.numpy as jnp
from jax import lax
import numpy as np

D_MODEL = 2048
BATCH = 32
SEQ = 256
DEPTH = 2
DEC_BATCH = 8
DEC_SEQ = 2048
PAST_LEN = 512

GRID_W = 64
BLOCK = 128
EPS = 1e-6
A_WIDTH = 2048
A_GROUPS = 16
A_GROUP_DIM = A_WIDTH // A_GROUPS
B_WIDTH = 2048
B_HEAD_DIM = 64
B_HEADS = B_WIDTH // B_HEAD_DIM
B_GROUPS = 8
B_STATE = 128
B_CONV = 5
B_CONV_CH = B_WIDTH + 2 * B_GROUPS * B_STATE
L0_IN = 2 * A_WIDTH + A_WIDTH + B_WIDTH + B_CONV_CH + 2 * B_HEADS
L0_MIX = A_WIDTH + B_WIDTH
C_HEADS = 16
C_KV_HEADS = 4
C_HEAD_DIM = 128
C_WIDTH = C_HEADS * C_HEAD_DIM
C_KV_WIDTH = C_KV_HEADS * C_HEAD_DIM
L1_IN = C_WIDTH + 2 * C_KV_WIDTH + C_WIDTH
ROPE_THETA = 10000.0

kernel_name = "hybrid_dit_gmlp_ssd_gqa_step"


def rms_norm(x, g):
    x32 = x.astype(jnp.float32)
    y = x32 * lax.rsqrt(jnp.mean(x32 * x32, axis=-1, keepdims=True) + EPS)
    return y.astype(x.dtype) * g


def layer_norm(x, g):
    x32 = x.astype(jnp.float32)
    xc = x32 - jnp.mean(x32, axis=-1, keepdims=True)
    y = xc * lax.rsqrt(jnp.mean(xc * xc, axis=-1, keepdims=True) + EPS)
    return y.astype(x.dtype) * g


def modulation(cond, w, b):
    m = jax.nn.silu(cond) @ w + b
    if m.ndim == 2:
        m = m[:, None, :]
    return jnp.split(m, 3, axis=-1)


def chunk_mlp(hu, hv, z, v_gain, w_s, b_s):
    b, n, _ = hu.shape
    v = layer_norm(hv, v_gain).reshape(b, n // BLOCK, BLOCK, A_GROUPS, A_GROUP_DIM)
    s = jnp.einsum('gts,bcsgd->bctgd', w_s, v) + b_s.T[None, None, :, :, None]
    return hu * s.reshape(b, n, A_WIDTH) * jax.nn.silu(z)


def depthwise_conv(x, w, bias):
    y = lax.conv_general_dilated(x, w[:, None, :], window_strides=(1,),
                                 padding=[(B_CONV // 2, B_CONV // 2)],
                                 dimension_numbers=('NWC', 'WIO', 'NWC'),
                                 feature_group_count=x.shape[-1])
    return y + bias


def ssd_scan(x, dt, a, bm, cm, h0):
    b, n = x.shape[:2]
    e = B_HEADS // B_GROUPS
    nc = n // BLOCK
    xdt = (x * dt[..., None]).reshape(b, nc, BLOCK, B_GROUPS, e, B_HEAD_DIM)
    da = (dt * a).reshape(b, nc, BLOCK, B_GROUPS, e)
    bm = bm.reshape(b, nc, BLOCK, B_GROUPS, B_STATE)
    cm = cm.reshape(b, nc, BLOCK, B_GROUPS, B_STATE)
    causal = jnp.tril(jnp.ones((BLOCK, BLOCK), dtype=bool))[None, :, :, None, None]

    def step(h, inp):
        xc, dac, bc, cc = inp
        cum = jnp.cumsum(dac, axis=1)
        seg = cum[:, :, None] - cum[:, None, :]
        lmat = jnp.exp(jnp.where(causal, seg, -jnp.inf))
        cb = jnp.einsum('blgn,bsgn->blsg', cc, bc)
        y = jnp.einsum('blsge,bsgep->blgep', cb[..., None] * lmat, xc)
        y = y + jnp.einsum('blgn,bgepn->blgep', cc, h) * jnp.exp(cum)[..., None]
        to_end = jnp.exp(cum[:, -1:] - cum)[..., None]
        h = h * jnp.exp(cum[:, -1])[..., None, None] + jnp.einsum('bsgn,bsgep->bgepn', bc, xc * to_end)
        return h, y

    inputs = tuple(jnp.moveaxis(t, 1, 0) for t in (xdt, da, bm, cm))
    h_fin, y = lax.scan(step, h0.reshape(b, B_GROUPS, e, B_HEAD_DIM, B_STATE), inputs)
    y = jnp.moveaxis(y, 0, 1).reshape(b, n, B_HEADS, B_HEAD_DIM)
    return y, h_fin.reshape(b, B_HEADS, B_HEAD_DIM, B_STATE)


def ssd_mixer(xbc, z, dt_raw, conv_w, conv_b, dt_bias, a_log, d_skip, norm_g, h0_f, h0_b):
    xbc = jax.nn.silu(depthwise_conv(xbc, conv_w, conv_b))
    b, n, _ = xbc.shape
    gn = B_GROUPS * B_STATE
    xs = xbc[..., :B_WIDTH].reshape(b, n, B_HEADS, B_HEAD_DIM)
    bm = xbc[..., B_WIDTH:B_WIDTH + gn].reshape(b, n, B_GROUPS, B_STATE)
    cm = xbc[..., B_WIDTH + gn:].reshape(b, n, B_GROUPS, B_STATE)
    dt = jax.nn.softplus(dt_raw.reshape(b, n, 2, B_HEADS) + dt_bias)
    a = -jnp.exp(a_log)
    flip = lambda t: jnp.flip(t, axis=1)
    y_f, h_f = ssd_scan(xs, dt[:, :, 0], a[0], bm, cm, h0_f)
    y_b, h_b = ssd_scan(flip(xs), flip(dt[:, :, 1]), a[1], flip(bm), flip(cm), h0_b)
    y = y_f + flip(y_b) + (d_skip[0] + d_skip[1])[:, None] * xs
    y = y.reshape(b, n, B_WIDTH) * jax.nn.silu(z)
    y = rms_norm(y.reshape(b, n, B_GROUPS, B_WIDTH // B_GROUPS),
                 norm_g.reshape(B_GROUPS, B_WIDTH // B_GROUPS)).reshape(b, n, B_WIDTH)
    return y, h_f, h_b


def layer0_mixer(h, h0_f, h0_b, w_in, v_gain, w_s, b_s, conv_w, conv_b,
                 dt_bias, a_log, d_skip, ssm_norm, w_out):
    p = h @ w_in
    o1 = 2 * A_WIDTH
    o2 = o1 + A_WIDTH
    o3 = o2 + B_WIDTH
    o4 = o3 + B_CONV_CH
    uv = jax.nn.gelu(p[..., :o1])
    a_out = chunk_mlp(uv[..., :A_WIDTH], uv[..., A_WIDTH:], p[..., o1:o2], v_gain, w_s, b_s)
    b_out, h_f, h_b = ssd_mixer(p[..., o3:o4], p[..., o2:o3], p[..., o4:], conv_w, conv_b,
                                dt_bias, a_log, d_skip, ssm_norm, h0_f, h0_b)
    out = jnp.concatenate([a_out, b_out], axis=-1) @ w_out
    return out, h_f, h_b


def rope_2d(x, n):
    rows = n // GRID_W
    t_row = jnp.repeat(jnp.arange(rows), GRID_W)
    t_col = jnp.tile(jnp.arange(GRID_W), rows)
    half = C_HEAD_DIM // 2
    inv = ROPE_THETA ** (-jnp.arange(0, half, 2, dtype=jnp.float32) / half)

    def rot(xh, pos):
        ang = pos.astype(jnp.float32)[:, None] * inv[None, :]
        cos = jnp.cos(ang)[None, :, None, :].astype(xh.dtype)
        sin = jnp.sin(ang)[None, :, None, :].astype(xh.dtype)
        x1, x2 = xh[..., :half // 2], xh[..., half // 2:]
        return jnp.concatenate([x1 * cos - x2 * sin, x1 * sin + x2 * cos], axis=-1)

    return jnp.concatenate([rot(x[..., :half], t_row), rot(x[..., half:], t_col)], axis=-1)


def blocked_attention(q, k, v):
    b, n = q.shape[:2]
    g = C_HEADS // C_KV_HEADS
    qb = q.reshape(b, n // BLOCK, BLOCK, C_KV_HEADS, g, C_HEAD_DIM).swapaxes(0, 1)
    scale = C_HEAD_DIM ** -0.5

    def one_block(qblk):
        s = jnp.einsum('bqkgd,bskd->bkgqs', qblk, k).astype(jnp.float32) * scale
        p = jax.nn.softmax(s, axis=-1).astype(v.dtype)
        return jnp.einsum('bkgqs,bskd->bqkgd', p, v)

    o = lax.map(one_block, qb)
    return o.swapaxes(0, 1).reshape(b, n, C_WIDTH)


def attn_project(h, w_in, q_norm, k_norm):
    b, n, _ = h.shape
    p = h @ w_in
    q = p[..., :C_WIDTH].reshape(b, n, C_HEADS, C_HEAD_DIM)
    k = p[..., C_WIDTH:C_WIDTH + C_KV_WIDTH].reshape(b, n, C_KV_HEADS, C_HEAD_DIM)
    v = p[..., C_WIDTH + C_KV_WIDTH:C_WIDTH + 2 * C_KV_WIDTH].reshape(b, n, C_KV_HEADS, C_HEAD_DIM)
    z = p[..., C_WIDTH + 2 * C_KV_WIDTH:]
    return rms_norm(q, q_norm), rms_norm(k, k_norm), v, z


def attn_context(h, w_in, q_norm, k_norm, w_out):
    q, k, v, z = attn_project(h, w_in, q_norm, k_norm)
    o = blocked_attention(q, k, v)
    return (o * jax.nn.silu(z)) @ w_out, k, v


def attn_latent(h, ctx_k, ctx_v, w_in, q_norm, k_norm, w_out):
    n = h.shape[1]
    q, k, v, z = attn_project(h, w_in, q_norm, k_norm)
    q, k = rope_2d(q, n), rope_2d(k, n)
    o = blocked_attention(q, jnp.concatenate([ctx_k, k], axis=1), jnp.concatenate([ctx_v, v], axis=1))
    return (o * jax.nn.silu(z)) @ w_out


def setup_inputs(seed: int = 0) -> dict:
    key = jax.random.key(seed)
    ks = iter(jax.random.split(key, 64))
    d = D_MODEL

    def nrm(shape, s):
        return jax.random.normal(next(ks), shape, jnp.float32) * s

    def uni(shape, lo, hi):
        return jax.random.uniform(next(ks), shape, jnp.float32, lo, hi)

    def dt_bias():
        dt = jnp.exp(uni((2, B_HEADS), float(np.log(1e-3)), float(np.log(1e-1))))
        return dt + jnp.log(-jnp.expm1(-dt))

    inp = {}
    inp['x_prompt'] = nrm((BATCH, SEQ, d), 1.0)
    inp['x_sample'] = nrm((DEC_BATCH, DEC_SEQ, d), 1.0)
    inp['state_l0_ssm_fwd'] = nrm((DEC_BATCH, B_HEADS, B_HEAD_DIM, B_STATE), 0.1)
    inp['state_l0_ssm_bwd'] = nrm((DEC_BATCH, B_HEADS, B_HEAD_DIM, B_STATE), 0.1)
    inp['cache_l1_k'] = nrm((DEC_BATCH, PAST_LEN, C_KV_HEADS, C_HEAD_DIM), 1.0)
    inp['cache_l1_v'] = nrm((DEC_BATCH, PAST_LEN, C_KV_HEADS, C_HEAD_DIM), 1.0)
    inp['c'] = nrm((DEC_BATCH, d), 1.0)
    inp['c_ctx'] = nrm((d,), 1.0)
    inp['mod_w0'] = nrm((d, 3 * d), d ** -0.5)
    inp['mod_b0'] = nrm((3 * d,), 0.01)
    inp['norm_pre0'] = 1.0 + nrm((d,), 0.02)
    inp['norm_post0'] = 1.0 + nrm((d,), 0.02)
    inp['l0_w_in'] = nrm((d, L0_IN), d ** -0.5)
    inp['l0_v_gain'] = 1.0 + nrm((A_WIDTH,), 0.02)
    inp['l0_w_s'] = nrm((A_GROUPS, BLOCK, BLOCK), BLOCK ** -0.5)
    inp['l0_b_s'] = 1.0 + nrm((A_GROUPS, BLOCK), 0.01)
    inp['l0_conv_w'] = nrm((B_CONV, B_CONV_CH), B_CONV ** -0.5)
    inp['l0_conv_b'] = nrm((B_CONV_CH,), 0.01)
    inp['l0_dt_bias'] = dt_bias()
    inp['l0_a_log'] = jnp.log(uni((2, B_HEADS), 1.0, 16.0))
    inp['l0_d_skip'] = 1.0 + nrm((2, B_HEADS), 0.1)
    inp['l0_ssm_norm'] = 1.0 + nrm((B_WIDTH,), 0.02)
    inp['l0_w_out'] = nrm((L0_MIX, d), L0_MIX ** -0.5)
    inp['mod_w1'] = nrm((d, 3 * d), d ** -0.5)
    inp['mod_b1'] = nrm((3 * d,), 0.01)
    inp['norm_pre1'] = 1.0 + nrm((d,), 0.02)
    inp['norm_post1'] = 1.0 + nrm((d,), 0.02)
    inp['l1_w_in'] = nrm((d, L1_IN), d ** -0.5)
    inp['l1_q_norm'] = 1.0 + nrm((C_HEAD_DIM,), 0.02)
    inp['l1_k_norm'] = 1.0 + nrm((C_HEAD_DIM,), 0.02)
    inp['l1_w_out'] = nrm((C_WIDTH, d), C_WIDTH ** -0.5)
    return inp


def reference(x_prompt, x_sample, state_l0_ssm_fwd, state_l0_ssm_bwd, cache_l1_k, cache_l1_v,
              c, c_ctx,
              mod_w0, mod_b0, norm_pre0, norm_post0, l0_w_in, l0_v_gain, l0_w_s, l0_b_s,
              l0_conv_w, l0_conv_b, l0_dt_bias, l0_a_log, l0_d_skip, l0_ssm_norm, l0_w_out,
              mod_w1, mod_b1, norm_pre1, norm_post1, l1_w_in, l1_q_norm, l1_k_norm, l1_w_out):
    mod_w = (mod_w0, mod_w1)
    mod_b = (mod_b0, mod_b1)
    norm_pre = (norm_pre0, norm_pre1)
    norm_post = (norm_post0, norm_post1)
    l0_params = (l0_w_in, l0_v_gain, l0_w_s, l0_b_s, l0_conv_w, l0_conv_b,
                 l0_dt_bias, l0_a_log, l0_d_skip, l0_ssm_norm, l0_w_out)

    xp, xs = x_prompt, x_sample
    for layer in range(DEPTH):
        sp, scp, gp = modulation(c_ctx, mod_w[layer], mod_b[layer])
        ss, scs, gs = modulation(c, mod_w[layer], mod_b[layer])
        hp = rms_norm(xp, norm_pre[layer]) * (1 + scp) + sp
        hs = rms_norm(xs, norm_pre[layer]) * (1 + scs) + ss
        if layer % 2 == 0:
            zero = jnp.zeros((xp.shape[0], B_HEADS, B_HEAD_DIM, B_STATE), xp.dtype)
            op, new_fwd, new_bwd = layer0_mixer(hp, zero, zero, *l0_params)
            os_, _, _ = layer0_mixer(hs, state_l0_ssm_fwd, state_l0_ssm_bwd, *l0_params)
        else:
            op, new_k, new_v = attn_context(hp, l1_w_in, l1_q_norm, l1_k_norm, l1_w_out)
            os_ = attn_latent(hs, cache_l1_k, cache_l1_v, l1_w_in, l1_q_norm, l1_k_norm, l1_w_out)
        xp = xp + gp * rms_norm(op, norm_post[layer])
        xs = xs + gs * rms_norm(os_, norm_post[layer])

    return (xp, xs, new_fwd, new_bwd, new_k, new_v)
```

```python
import numpy as np
from contextlib import ExitStack
import concourse.bass as bass
import concourse.mybir as mybir
from concourse.bass_utils import run_bass_kernel_spmd

F32 = mybir.dt.float32
BF16 = mybir.dt.bfloat16
AF = mybir.ActivationFunctionType
ALU = mybir.AluOpType
AX = mybir.AxisListType

NT = 3072
NCH = 24
D = 2048
EPS = 1e-6
L0_IN = 12352
SAME_ENG_SYNC = True


class Buf:
    __slots__ = ("name", "wr", "rds")

    def __init__(self, name=""):
        self.name = name
        self.wr = None
        self.rds = []


class Eng:
    def __init__(self, ctx, name, eng, ndma=0):
        self.name = name
        self.eng = eng
        self.sem = ctx.nc.alloc_semaphore("e_" + name)
        self.cnt = 0
        self.seen = {}
        self.pool = [[ctx.nc.alloc_semaphore("d_%s%d" % (name, i)), 0] for i in range(ndma)]
        self.pi = 0

    def wait(self, tok):
        sem, val = tok
        k = id(sem)
        if self.seen.get(k, 0) >= val:
            return
        self.eng.wait_ge(sem, val)
        self.seen[k] = val


class Ctx:
    def __init__(self, nc):
        self.nc = nc
        self.pe = Eng(self, "pe", nc.tensor)
        self.dve = Eng(self, "dve", nc.vector)
        self.act = Eng(self, "act", nc.scalar)
        self.pool = Eng(self, "pool", nc.gpsimd, ndma=16)
        self.sp = Eng(self, "sp", nc.sync, ndma=24)
        self.engs = [self.pe, self.dve, self.act, self.pool, self.sp]
        self.ninst = 0

    def _deps(self, e, reads, writes, so=False):
        deps = []
        for b in reads:
            if b.wr is not None:
                deps.append(b.wr)
        for b in writes:
            deps.extend(b.rds)
            if b.wr is not None:
                deps.append(b.wr)
        best = {}
        for sem, val in deps:
            k = id(sem)
            if k == id(e.sem) and (so or e is self.pe or not SAME_ENG_SYNC):
                continue
            if k not in best or best[k][1] < val:
                best[k] = (sem, val)
        for tok in best.values():
            e.wait(tok)

    def _record(self, tok, reads, writes):
        for b in writes:
            b.wr = tok
            b.rds = []
        for b in reads:
            if b not in writes:
                b.rds.append(tok)
                if len(b.rds) > 48:
                    best = {}
                    for s, v in b.rds:
                        if id(s) not in best or best[id(s)][1] < v:
                            best[id(s)] = (s, v)
                    b.rds = list(best.values())

    def op(self, e, fn, reads=(), writes=(), inc=True, so=False):
        self._deps(e, reads, writes, so)
        ins = fn()
        self.ninst += 1
        if inc:
            ins.then_inc(e.sem, 1)
            e.cnt += 1
            tok = (e.sem, e.cnt)
        else:
            tok = (e.sem, e.cnt + 1)
        self._record(tok, reads, writes)
        return tok

    def dma(self, e, out, in_, reads=(), writes=(), **kw):
        slot = e.pool[e.pi]
        e.pi = (e.pi + 1) % len(e.pool)
        if slot[1] > 0:
            e.wait((slot[0], 16 * slot[1]))
        self._deps(e, reads, writes)
        ins = e.eng.dma_start(out=out, in_=in_, **kw)
        ins.then_inc(slot[0], 16)
        slot[1] += 1
        self.ninst += 1
        tok = (slot[0], 16 * slot[1])
        self._record(tok, reads, writes)
        return tok

    def barrier(self):
        toks = []
        for e in self.engs:
            if e.cnt > 0:
                toks.append((e.sem, e.cnt))
            for s, c in e.pool:
                if c > 0:
                    toks.append((s, 16 * c))
        for e in self.engs:
            for t in toks:
                if id(t[0]) == id(e.sem):
                    continue
                e.wait(t)

    def finish(self):
        for e in self.engs:
            if e.cnt > 0:
                self.sp.wait((e.sem, e.cnt))
            for s, c in e.pool:
                if c > 0:
                    self.sp.wait((s, 16 * c))


class K:
    def __init__(self, upto=99, dbg=False):
        self.upto = upto
        self.dbg = dbg
        nc = self.nc = bass.Bass("TRN2", target_bir_lowering=False)
        self.cx = Ctx(nc)
        self.din = {}
        self.dout = {}
        self.psA = nc.alloc_psum_tensor("psA", [128, 2048], F32)
        self.psB = nc.alloc_psum_tensor("psB", [128, 2048], F32)
        self.bank = [(self.psA, i) for i in range(4)] + [(self.psB, i) for i in range(4)]
        self.bbuf = [Buf("bank%d" % i) for i in range(8)]
        self.bi = 0
        self.tid = 0

    def inp(self, name, shape, dt=F32):
        t = self.nc.dram_tensor(name, list(shape), dt, kind="ExternalInput").ap()
        self.din[name] = t
        return t

    def outp(self, name, shape, dt=F32):
        t = self.nc.dram_tensor(name, list(shape), dt, kind="ExternalOutput").ap()
        self.dout[name] = t
        return t

    def scr(self, name, shape, dt=BF16):
        if self.dbg:
            t = self.nc.dram_tensor(name, list(shape), dt, kind="ExternalOutput").ap()
            self.dout[name] = t
            return t
        return self.nc.dram_tensor(name, list(shape), dt).ap()

    def tile(self, st, shape, dt, name=None):
        self.tid += 1
        nm = "%s_%d" % (name or "t", self.tid)
        t = st.enter_context(self.nc.sbuf_tensor(nm, list(shape), dt))
        return t, Buf(nm)

    def ps(self, i):
        t, j = self.bank[i]
        return t[:, j * 512:(j + 1) * 512], self.bbuf[i]

    def nextbank(self, lo=0, hi=8):
        if self.bi < lo or self.bi >= hi:
            self.bi = lo
        i = self.bi
        self.bi += 1
        if self.bi >= hi:
            self.bi = lo
        return i

    def V(self, fn, reads=(), writes=(), so=False):
        return self.cx.op(self.cx.dve, fn, reads, writes, so=so)

    def A(self, fn, reads=(), writes=(), so=False):
        return self.cx.op(self.cx.act, fn, reads, writes, so=so)

    def G(self, fn, reads=(), writes=(), so=False):
        return self.cx.op(self.cx.pool, fn, reads, writes, so=so)

    def P(self, fn, reads=(), writes=(), inc=True):
        return self.cx.op(self.cx.pe, fn, reads, writes, inc=inc)

    def ld(self, out, in_, writes=(), reads=()):
        return self.cx.dma(self.cx.pool, out, in_, reads=reads, writes=writes)

    def stq(self, out, in_, reads=(), writes=()):
        return self.cx.dma(self.cx.sp, out, in_, reads=reads, writes=writes)

    def rstd(self, st, ss, bss, scale, n=1):
        nc = self.nc
        self.V(lambda: nc.vector.tensor_scalar(ss, ss, scale, EPS, ALU.mult, ALU.add), [bss], [bss])
        self.A(lambda: nc.scalar.activation(out=ss, in_=ss, func=AF.Ln), [bss], [bss])
        self.A(lambda: nc.scalar.activation(out=ss, in_=ss, func=AF.Exp, scale=-0.5), [bss], [bss])

    def build(self):
        nc = self.nc
        I = self.inp
        self.xtok = I("xtok", [NT, D])
        self.condT = I("condT", [128, 32])
        self.mod_w = [I("mod_w0", [D, 3 * D]), I("mod_w1", [D, 3 * D])]
        self.mod_b = [I("mod_b0", [128, 3 * D]), I("mod_b1", [128, 3 * D])]
        self.npre = [I("npre0", [128, D]), I("npre1", [128, D])]
        self.npost = [I("npost0", [128, D]), I("npost1", [128, D])]
        self.w_in0 = I("l0_w_in", [D, L0_IN])
        self.w_out0 = I("l0_w_out", [2 * D, D])
        self.w_in1 = I("l1_w_in", [D, 5120])
        self.w_out1 = I("l1_w_out", [D, D])
        self.vgain = I("vgain", [128, D])
        self.wsT_d = I("wsT", [128, 16, 128])
        self.bsrow_d = I("bsrow", [1, D])
        self.convw_d = I("convw", [128, 32, 5])
        self.convb_d = I("convb", [128, 32])
        self.dtb_d = I("dtb", [128, 64])
        self.alog_d = I("alog", [128, 64])
        self.dsk_d = I("dsk", [128, 2, D])
        self.ssmn_d = I("ssmn", [128, D])
        self.sf_d = I("sf", [D, 128])
        self.sb_d = I("sb", [D, 128])
        self.ck_d = I("ck", [512, 512])
        self.cv_d = I("cv", [512, 512])
        self.qkn_d = I("qkn", [128, 2, 128])
        self.cos_d = I("ropec", [2048, 128])
        self.sin_d = I("ropes", [2048, 128])
        self.cst_d = I("cst", [128, 6, 128])
        O = self.outp
        self.y_o = O("y", [NT, D])
        self.nf_o = O("nfwd", [4, D, 128])
        self.nb_o = O("nbwd", [4, D, 128])
        self.nk_o = O("nk", [1024, 512])
        self.nv_o = O("nv", [1024, 512])
        S = self.scr
        self.UT = S("UT", [16, 128, NT])
        self.ZAT = S("ZAT", [16, 128, NT])
        self.XBCT = S("XBCT", [32, 128, NT])
        self.Vs = S("Vs", [NT, D])
        self.ZB = S("ZB", [NT, D])
        self.DT = S("DTs", [NT, 64], F32)
        self.BCT = S("BCT", [16, 128, NT])
        self.XSB = S("XSB", [NT, 3072])
        self.HBS = S("HBS", [NCH, 128, D])
        self.MIXT = S("MIXT", [32, 128, NT])
        self.X1 = S("X1", [NT, D], F32)
        self.Qs = S("Qs", [NT, D])
        self.Ks = S("Ks", [NT + 512, 512])
        self.V1 = S("V1", [NT + 512, 512])
        self.Z1 = S("Z1", [NT, D])
        self.MS = S("MS", [2, 128, 3 * D], F32)

        with ExitStack() as g:
            self.cb, self.bcb = self.tile(g, [128, 6, 128], BF16, "cstb")
            self.cf, self.bcf = self.tile(g, [128, 6, 128], F32, "cstf")
            self.ld(self.cb[:], self.cst_d[:, :, :], [self.bcb])
            self.ld(self.cf[:], self.cst_d[:, :, :], [self.bcf])
            self.identb = self.cb[:, 0, :]
            self.identf = self.cf[:, 0, :]

            self.modulation(0)
            self.stage_in(0)
            if self.upto >= 2:
                self.stage_l0_front()
            if self.upto >= 4:
                self.stage_l0_ssd()
            if self.upto >= 5:
                self.stage_out(0)
            if self.upto >= 6:
                self.modulation(1)
                self.stage_in(1)
            if self.upto >= 7:
                self.stage_attn()
            if self.upto >= 8:
                self.stage_out(1)
            self.cx.finish()
        return nc

    def modulation(self, layer):
        nc = self.nc
        with ExitStack() as st:
            cond, bcond = self.tile(st, [128, 32], F32, "cond")
            lbc, blbc = self.tile(st, [128, 32, 128], BF16, "lbc")
            mb, bmb = self.tile(st, [128, 3 * D], F32, "mb")
            tA, btA = self.tile(st, [128, D], F32, "tA")
            tB, btB = self.tile(st, [128, D], F32, "tB")
            wsl = [self.tile(st, [128, 16, 512], BF16, "mw") for _ in range(2)]
            Ms = [self.tile(st, [128, 3 * D], F32, "M%d" % c) for c in range(2)]
            self.ld(cond[:], self.condT[:, :], [bcond])
            self.ld(mb[:], self.mod_b[layer][:, :], [bmb])
            self.ld(tA[:], self.npre[layer][:, :], [btA])
            self.ld(tB[:], self.npost[layer][:, :], [btB])
            self.A(lambda: nc.scalar.activation(out=cond[:], in_=cond[:], func=AF.Silu), [bcond], [bcond])
            self.V(lambda: nc.vector.tensor_copy(out=lbc[:], in_=cond[:].unsqueeze(2).to_broadcast([128, 32, 128])),
                   [bcond], [blbc])
            Wv = self.mod_w[layer].rearrange("(kc p) n -> p kc n", p=128)
            for s in range(12):
                w, bw = wsl[s % 2]
                self.ld(w[:], Wv[:, :, s * 512:(s + 1) * 512], [bw])
                for c in range(2):
                    bi = self.nextbank()
                    pa, pb = self.ps(bi)
                    for kc in range(16):
                        self.P(lambda kc=kc: nc.tensor.matmul(pa, lbc[:, c * 16 + kc, :], w[:, kc, :],
                                                              start=(kc == 0), stop=(kc == 15)),
                               [blbc, bw], [pb], inc=(kc == 15))
                    Mt, bM = Ms[c]
                    self.V(lambda: nc.vector.tensor_tensor(out=Mt[:, s * 512:(s + 1) * 512], in0=pa,
                                                           in1=mb[:, s * 512:(s + 1) * 512], op=ALU.add),
                           [pb, bmb], [bM])
            for c in range(2):
                Mt, bM = Ms[c]
                self.V(lambda: nc.vector.scalar_tensor_tensor(out=Mt[:, D:2 * D], in0=Mt[:, D:2 * D], scalar=1.0,
                                                              in1=tA[:], op0=ALU.add, op1=ALU.mult), [bM, btA], [bM])
                self.V(lambda: nc.vector.tensor_tensor(out=Mt[:, 2 * D:3 * D], in0=Mt[:, 2 * D:3 * D], in1=tB[:],
                                                       op=ALU.mult), [bM, btB], [bM])
                self.stq(self.MS[c, :, :], Mt[:], [bM])
            self.cx.barrier()

    def stage_in(self, layer):
        nc = self.nc
        xsrc = self.xtok if layer == 0 else self.X1
        W = self.w_in0 if layer == 0 else self.w_in1
        if layer == 0:
            slabs = []
            for i in range(4):
                slabs.append((i * 512, 512, "B", "u"))
            for i in range(4):
                slabs.append((2048 + i * 512, 512, "A", "v"))
            for i in range(4):
                slabs.append((4096 + i * 512, 512, "B", "za"))
            for i in range(4):
                slabs.append((6144 + i * 512, 512, "A", "zb"))
            for i in range(8):
                slabs.append((8192 + i * 512, 512, "B", "xbc"))
            slabs.append((12288, 64, "A", "dt"))
        else:
            slabs = []
            for i in range(4):
                slabs.append((i * 512, 512, "A", "q"))
            slabs.append((2048, 512, "A", "k"))
            slabs.append((2560, 512, "A", "vv"))
            for i in range(4):
                slabs.append((3072 + i * 512, 512, "A", "z"))
        Wv = W.rearrange("(kc p) n -> p kc n", p=128)
        TCH = 12
        with ExitStack() as st:
            hT, bhT = self.tile(st, [128, 16, TCH * 128], BF16, "hT")
            wsl = [self.tile(st, [128, 16, 512], BF16, "wsl") for _ in range(2)]
            xb = [self.tile(st, [128, D], F32, "xb") for _ in range(2)]
            hb = [self.tile(st, [128, D], BF16, "hb") for _ in range(2)]
            ss = [self.tile(st, [128, 1], F32, "ss") for _ in range(2)]
            NOB = 8
            ob = [self.tile(st, [128, 512], F32, "ob") for _ in range(NOB)]
            obh = [self.tile(st, [128, 512], BF16, "obh") for _ in range(NOB)]
            Mg = [self.tile(st, [128, 2 * D], F32, "Mg") for _ in range(2)]
            for c in range(2):
                self.ld(Mg[c][0][:], self.MS[c, :, 0:2 * D], [Mg[c][1]])
            if layer == 1:
                qkn, bqkn = self.tile(st, [128, 2, 128], F32, "qkn")
                self.ld(qkn[:], self.qkn_d[:, :, :], [bqkn])
                cs = [self.tile(st, [128, 2, 128], F32, "cs") for _ in range(3)]
                sqs = [self.tile(st, [128, 512], F32, "sq") for _ in range(3)]
                s4 = [self.tile(st, [128, 4], F32, "s4") for _ in range(4)]
                rts = [self.tile(st, [128, 512], F32, "rt") for _ in range(3)]
                for (src, dst) in ((self.ck_d, self.Ks), (self.cv_d, self.V1)):
                    for r in range(4):
                        t, bt = obh[r]
                        self.ld(t[:], src[r * 128:(r + 1) * 128, :], [bt])
                        self.stq(dst[NT + r * 128:NT + (r + 1) * 128, :], t[:], [bt])
            oi = 0
            for pas in range(2):
                c0 = pas * TCH
                for j in range(TCH):
                    c = c0 + j
                    cond = 0 if c < 16 else 1
                    Mt, bM = Mg[cond]
                    x, bx = xb[j % 2]
                    h, bh = hb[j % 2]
                    s1, bs1 = ss[j % 2]
                    self.ld(x[:], xsrc[c * 128:(c + 1) * 128, :], [bx])
                    self.A(lambda: nc.scalar.activation(out=h[:], in_=x[:], func=AF.Square, accum_out=s1[:]),
                           [bx], [bh, bs1])
                    self.rstd(st, s1[:], bs1, 1.0 / D)
                    self.V(lambda: nc.vector.scalar_tensor_tensor(out=x[:], in0=x[:], scalar=s1[:], in1=Mt[:, D:2 * D],
                                                                  op0=ALU.mult, op1=ALU.mult), [bx, bs1, bM], [bx])
                    self.V(lambda: nc.vector.tensor_tensor(out=h[:], in0=x[:], in1=Mt[:, 0:D], op=ALU.add),
                           [bx, bM], [bh])
                    for half in range(2):
                        bi = self.nextbank()
                        pa, pb = self.ps(bi)
                        pab = pa.bitcast(BF16)
                        for q in range(8):
                            kc = half * 8 + q
                            self.P(lambda kc=kc, q=q: nc.tensor.transpose(pab[:, q * 128:(q + 1) * 128],
                                                                           h[:, kc * 128:(kc + 1) * 128], self.identb),
                                   [bh, self.bcb], [pb], inc=(q == 7))
                        src = pab.rearrange("p (q t) -> p q t", q=8)
                        dst = hT[:, half * 8:(half + 1) * 8, j * 128:(j + 1) * 128]
                        if half == 0:
                            self.A(lambda: nc.scalar.copy(out=dst, in_=src), [pb], [bhT])
                        else:
                            self.V(lambda: nc.vector.tensor_copy(out=dst, in_=src), [pb], [bhT])
                for si, (col0, ncols, var, kind) in enumerate(slabs):
                    w, bw = wsl[si % 2]
                    self.ld(w[:, :, 0:ncols], Wv[:, :, col0:col0 + ncols], [bw])
                    if var == "A":
                        for j in range(TCH):
                            c = c0 + j
                            bi = self.nextbank()
                            pa, pb = self.ps(bi)
                            pa = pa[:, 0:ncols]
                            for kc in range(16):
                                self.P(lambda kc=kc: nc.tensor.matmul(pa, hT[:, kc, j * 128:(j + 1) * 128],
                                                                      w[:, kc, 0:ncols], start=(kc == 0), stop=(kc == 15)),
                                       [bhT, bw], [pb], inc=(kc == 15))
                            oi += 1
                            o, bo = ob[oi % NOB]
                            oh, boh = obh[oi % NOB]
                            rows = slice(c * 128, (c + 1) * 128)
                            if kind in ("v", "zb", "z"):
                                fn = AF.Gelu_apprx_tanh if kind == "v" else AF.Silu
                                dst = {"v": self.Vs, "zb": self.ZB, "z": self.Z1}[kind]
                                cc = col0 - {"v": 2048, "zb": 6144, "z": 3072}[kind]
                                self.A(lambda: nc.scalar.activation(out=oh[:], in_=pa, func=fn), [pb], [boh])
                                self.stq(dst[rows, cc:cc + 512], oh[:], [boh])
                            elif kind == "dt":
                                self.A(lambda: nc.scalar.copy(out=o[:, 0:64], in_=pa), [pb], [bo])
                                self.stq(self.DT[rows, :], o[:, 0:64], [bo])
                            elif kind == "vv":
                                self.A(lambda: nc.scalar.copy(out=o[:], in_=pa), [pb], [bo])
                                self.V(lambda: nc.vector.tensor_copy(out=oh[:], in_=o[:]), [bo], [boh])
                                self.stq(self.V1[rows, :], oh[:], [boh])
                                if c >= 16:
                                    self.stq(self.nv_o[(c - 16) * 128:(c - 15) * 128, :], o[:], [bo])
                            else:
                                which = 0 if kind == "q" else 1
                                s4t, bs4 = s4[oi % 4]
                                sq, bsq = sqs[oi % 3]
                                rt, brt = rts[oi % 3]
                                self.A(lambda: nc.scalar.copy(out=o[:], in_=pa), [pb], [bo])
                                for hh in range(4):
                                    self.A(lambda hh=hh: nc.scalar.activation(out=sq[:, hh * 128:(hh + 1) * 128],
                                                                              in_=o[:, hh * 128:(hh + 1) * 128], func=AF.Square,
                                                                              accum_out=s4t[:, hh:hh + 1]), [bo], [bsq, bs4], so=(hh > 0))
                                self.rstd(st, s4t[:], bs4, 1.0 / 128)
                                for hh in range(4):
                                    self.V(lambda hh=hh: nc.vector.scalar_tensor_tensor(out=o[:, hh * 128:(hh + 1) * 128],
                                                                                        in0=o[:, hh * 128:(hh + 1) * 128],
                                                                                        scalar=s4t[:, hh:hh + 1], in1=qkn[:, which, :],
                                                                                        op0=ALU.mult, op1=ALU.mult),
                                           [bo, bs4, bqkn], [bo], so=(hh > 0))
                                if kind == "k" and c >= 16:
                                    self.stq(self.nk_o[(c - 16) * 128:(c - 15) * 128, :], o[:], [bo])
                                if c < 16:
                                    cst, bcs = cs[oi % 3]
                                    self.ld(cst[:, 0, :], self.cos_d[rows, :], [bcs])
                                    self.ld(cst[:, 1, :], self.sin_d[rows, :], [bcs])
                                    o3 = o[:].rearrange("p (h d) -> p h d", h=4)
                                    o5 = o[:].rearrange("p (h a two i) -> p h a two i", h=4, a=2, two=2)
                                    r5 = rt[:].rearrange("p (h a two i) -> p h a two i", h=4, a=2, two=2)
                                    s4v = cst[:, 1, :].rearrange("p (a two i) -> p a two i", a=2, two=2)
                                    sin1 = s4v[:, :, 0, :].unsqueeze(1).to_broadcast([128, 4, 2, 32])
                                    sin2 = s4v[:, :, 1, :].unsqueeze(1).to_broadcast([128, 4, 2, 32])
                                    self.G(lambda: nc.gpsimd.tensor_tensor(out=r5[:, :, :, 0, :], in0=o5[:, :, :, 1, :], in1=sin1,
                                                                           op=ALU.mult), [bo, bcs], [brt])
                                    self.G(lambda: nc.gpsimd.tensor_tensor(out=r5[:, :, :, 1, :], in0=o5[:, :, :, 0, :], in1=sin2,
                                                                           op=ALU.mult), [bo, bcs], [brt])
                                    self.V(lambda: nc.vector.tensor_tensor(out=sq[:].rearrange("p (h d) -> p h d", h=4), in0=o3,
                                                                           in1=cst[:, 0:1, :].to_broadcast([128, 4, 128]),
                                                                           op=ALU.mult), [bo, bcs], [bsq])
                                    self.V(lambda: nc.vector.tensor_tensor(out=oh[:], in0=sq[:], in1=rt[:], op=ALU.add),
                                           [bsq, brt], [boh])
                                else:
                                    self.V(lambda: nc.vector.tensor_copy(out=oh[:], in_=o[:]), [bo], [boh])
                                if kind == "q":
                                    self.stq(self.Qs[rows, col0:col0 + 512], oh[:], [boh])
                                else:
                                    self.stq(self.Ks[rows, :], oh[:], [boh])
                    else:
                        for i in range(ncols // 128):
                            for tg in range(TCH // 4):
                                bi = self.nextbank()
                                pa, pb = self.ps(bi)
                                for kc in range(16):
                                    self.P(lambda kc=kc: nc.tensor.matmul(pa, w[:, kc, i * 128:(i + 1) * 128],
                                                                          hT[:, kc, tg * 512:(tg + 1) * 512],
                                                                          start=(kc == 0), stop=(kc == 15)),
                                           [bhT, bw], [pb], inc=(kc == 15))
                                oi += 1
                                oh, boh = obh[oi % NOB]
                                fn = {"u": AF.Gelu_apprx_tanh, "za": AF.Silu, "xbc": AF.Copy}[kind]
                                dst = {"u": self.UT, "za": self.ZAT, "xbc": self.XBCT}[kind]
                                cc = (col0 - {"u": 0, "za": 4096, "xbc": 8192}[kind]) // 128 + i
                                self.A(lambda: nc.scalar.activation(out=oh[:], in_=pa, func=fn), [pb], [boh])
                                t0 = c0 * 128 + tg * 512
                                self.stq(dst[cc, :, t0:t0 + 512], oh[:], [boh])
            self.cx.barrier()

    def stage_l0_front(self):
        nc = self.nc
        with ExitStack() as st:
            T = lambda shape, dt, nm: self.tile(st, shape, dt, nm)
            wsT, bwsT = T([128, 16, 128], BF16, "wsT")
            bsr, bbsr = T([1, D], BF16, "bsr")
            vg, bvg = T([128, D], F32, "vg")
            cw, bcw = T([128, 32, 5], F32, "cw")
            cbias, bcbias = T([128, 32], F32, "cbias")
            diag, bdiag = T([128, 160, 128], BF16, "diag")
            self.ld(wsT[:], self.wsT_d[:, :, :], [bwsT])
            self.ld(bsr[:], self.bsrow_d[:, :], [bbsr])
            self.ld(vg[:], self.vgain[:, :], [bvg])
            self.ld(cw[:], self.convw_d[:, :, :], [bcw])
            self.ld(cbias[:], self.convb_d[:, :], [bcbias])
            for kc in range(32):
                for j in range(5):
                    self.V(lambda kc=kc, j=j: nc.vector.tensor_scalar(diag[:, kc * 5 + j, :], self.identb,
                                                                       cw[:, kc, j:j + 1], None, ALU.mult),
                           [self.bcb, bcw], [bdiag])
            ones1 = self.cb[0:1, 5, :]
            xin = [T([128, 32, 260], BF16, "xin") for _ in range(1)]
            xcT = [T([128, 32, 256], BF16, "xcT") for _ in range(1)]
            ut = [T([128, 16, 256], BF16, "ut") for _ in range(1)]
            zat = [T([128, 16, 256], BF16, "zat") for _ in range(1)]
            vt = [T([128, D], BF16, "vt") for _ in range(2)]
            vn, bvn = T([128, D], BF16, "vn")
            tmp, btmp = T([128, D], F32, "tmp")
            mixa = [T([128, 16, 256], BF16, "mixa") for _ in range(1)]
            xtk = [T([128, 3072], BF16, "xtk") for _ in range(2)]
            st2 = [T([128, 4], F32, "st2") for _ in range(2)]
            for blk in range(12):
                t0 = blk * 256
                if blk < 8:
                    s0, s1 = 0, 2048
                else:
                    s0 = t0
                    s1 = t0 + 256
                xi, bxi = xin[0]
                xc, bxc = xcT[0]
                lo = max(s0, t0 - 2)
                hi = min(s1, t0 + 258)
                if lo > t0 - 2:
                    self.V(lambda: nc.vector.memset(xi[:, :, 0:2], 0.0), [], [bxi])
                if hi < t0 + 258:
                    self.V(lambda: nc.vector.memset(xi[:, :, 258:260], 0.0), [], [bxi])
                self.ld(xi[:, :, lo - (t0 - 2):hi - (t0 - 2)], self.XBCT[:, :, lo:hi].rearrange("k p t -> p k t"), [bxi])
                u, bu = ut[0]
                za, bza = zat[0]
                self.ld(u[:], self.UT[:, :, t0:t0 + 256].rearrange("k p t -> p k t"), [bu])
                self.ld(za[:], self.ZAT[:, :, t0:t0 + 256].rearrange("k p t -> p k t"), [bza])
                for kc in range(32):
                    bi = self.nextbank(4, 8)
                    pa, pb = self.ps(bi)
                    for j in range(5):
                        self.P(lambda kc=kc, j=j: nc.tensor.matmul(pa[:, 0:256], diag[:, kc * 5 + j, :], xi[:, kc, j:j + 256],
                                                                   start=(j == 0), stop=(j == 4)),
                               [bdiag, bxi], [pb], inc=(j == 4))
                    self.A(lambda kc=kc: nc.scalar.activation(out=xc[:, kc, :], in_=pa[:, 0:256], func=AF.Silu,
                                                              bias=cbias[:, kc:kc + 1]), [pb, bcbias], [bxc])
                self.stq(self.BCT[:, :, t0:t0 + 256].rearrange("k p t -> p k t"), xc[:, 16:32, :], [bxc])
                mx, bmx = mixa[0]
                for ch in range(2):
                    c = blk * 2 + ch
                    rows = slice(c * 128, (c + 1) * 128)
                    xk, bxk = xtk[c % 2]
                    for grp in range(3):
                        bi = self.nextbank(4, 8)
                        pa, pb = self.ps(bi)
                        pab = pa.bitcast(BF16)
                        for q in range(8):
                            kc = grp * 8 + q
                            self.P(lambda kc=kc, q=q: nc.tensor.transpose(pab[:, q * 128:(q + 1) * 128],
                                                                           xc[:, kc, ch * 128:(ch + 1) * 128], self.identb),
                                   [bxc, self.bcb], [pb], inc=(q == 7))
                        if grp == 1:
                            self.A(lambda: nc.scalar.copy(out=xk[:, grp * 1024:(grp + 1) * 1024], in_=pab), [pb], [bxk])
                        else:
                            self.V(lambda: nc.vector.tensor_copy(out=xk[:, grp * 1024:(grp + 1) * 1024], in_=pab), [pb], [bxk])
                    self.stq(self.XSB[rows, :], xk[:], [bxk])
                    v, bv = vt[c % 2]
                    s2, bs2 = st2[c % 2]
                    self.ld(v[:], self.Vs[rows, :], [bv])
                    self.A(lambda: nc.scalar.activation(out=tmp[:], in_=v[:], func=AF.Identity, accum_out=s2[:, 0:1]),
                           [bv], [btmp, bs2])
                    self.A(lambda: nc.scalar.activation(out=tmp[:], in_=v[:], func=AF.Square, accum_out=s2[:, 1:2]),
                           [bv], [btmp, bs2])
                    self.V(lambda: nc.vector.tensor_scalar(s2[:, 0:2], s2[:, 0:2], 1.0 / D, None, ALU.mult), [bs2], [bs2])
                    self.V(lambda: nc.vector.tensor_tensor(out=s2[:, 2:3], in0=s2[:, 0:1], in1=s2[:, 0:1], op=ALU.mult), [bs2], [bs2])
                    self.V(lambda: nc.vector.tensor_tensor(out=s2[:, 1:2], in0=s2[:, 1:2], in1=s2[:, 2:3], op=ALU.subtract), [bs2], [bs2])
                    self.rstd(st, s2[:, 1:2], bs2, 1.0)
                    self.V(lambda: nc.vector.scalar_tensor_tensor(out=s2[:, 3:4], in0=s2[:, 0:1], scalar=-1.0, in1=s2[:, 1:2],
                                                                  op0=ALU.mult, op1=ALU.mult), [bs2], [bs2])
                    self.A(lambda: nc.scalar.activation(out=tmp[:], in_=v[:], func=AF.Identity, scale=s2[:, 1:2],
                                                        bias=s2[:, 3:4]), [bv, bs2], [btmp])
                    self.V(lambda: nc.vector.tensor_tensor(out=vn[:], in0=tmp[:], in1=vg[:], op=ALU.mult), [btmp, bvg], [bvn])
                    for g in range(16):
                        bi = g // 4
                        pa, pb = self.ps(bi)
                        po = pa[:, (g % 4) * 128:(g % 4 + 1) * 128]
                        self.P(lambda g=g: nc.tensor.matmul(po, vn[:, g * 128:(g + 1) * 128], wsT[:, g, :], start=True, stop=False),
                               [bvn, bwsT], [pb], inc=False)
                        self.P(lambda g=g: nc.tensor.matmul(po, ones1, bsr[0:1, g * 128:(g + 1) * 128], start=False, stop=True),
                               [self.bcb, bbsr], [pb], inc=(g % 4 == 3))
                    pbs = [self.bbuf[i] for i in range(4)]
                    self.V(lambda: nc.vector.tensor_tensor(out=tmp[:].rearrange("p (g t) -> p g t", g=16),
                                                           in0=self.psA[:, :].rearrange("p (g t) -> p g t", g=16),
                                                           in1=u[:, :, ch * 128:(ch + 1) * 128], op=ALU.mult),
                           pbs + [bu], [btmp])
                    self.V(lambda: nc.vector.tensor_tensor(out=mx[:, :, ch * 128:(ch + 1) * 128],
                                                           in0=tmp[:].rearrange("p (g t) -> p g t", g=16),
                                                           in1=za[:, :, ch * 128:(ch + 1) * 128], op=ALU.mult),
                           [btmp, bza], [bmx])
                self.stq(self.MIXT[0:16, :, t0:t0 + 256].rearrange("k p t -> p k t"), mx[:], [bmx])
            self.cx.barrier()

    def load_state(self, H, bH, src, tmpf, btmpf):
        nc = self.nc
        self.ld(tmpf[:].rearrange("p (k n) -> p k n", k=16), src.rearrange("(k p) n -> p k n", p=128), [btmpf])
        for grp in range(4):
            bi = self.nextbank(0, 4)
            pa, pb = self.ps(bi)
            for q in range(4):
                k = grp * 4 + q
                self.P(lambda k=k, q=q: nc.tensor.transpose(pa[:, q * 128:(q + 1) * 128], tmpf[:, k * 128:(k + 1) * 128],
                                                             self.identf), [btmpf, self.bcf], [pb], inc=(q == 3))
            self.A(lambda: nc.scalar.copy(out=H[:, grp * 512:(grp + 1) * 512], in_=pa), [pb], [bH])

    def store_state(self, H, bH, dst, tmpf, btmpf):
        nc = self.nc
        for grp in range(4):
            bi = self.nextbank(0, 4)
            pa, pb = self.ps(bi)
            for q in range(4):
                k = grp * 4 + q
                self.P(lambda k=k, q=q: nc.tensor.transpose(pa[:, q * 128:(q + 1) * 128], H[:, k * 128:(k + 1) * 128],
                                                             self.identf), [bH, self.bcf], [pb], inc=(q == 3))
            self.A(lambda: nc.scalar.copy(out=tmpf[:, grp * 512:(grp + 1) * 512], in_=pa), [pb], [btmpf])
        self.stq(dst.rearrange("(k p) n -> p k n", p=128), tmpf[:].rearrange("p (k n) -> p k n", k=16), [btmpf])

    def state_update(self, H, bH, xw, bxw, xk, bxk, dec, be):
        nc = self.nc
        for g in range(8):
            pa, pb = self.ps(g // 2)
            self.P(lambda g=g: nc.tensor.matmul(pa[:, (g % 2) * 256:(g % 2 + 1) * 256],
                                                xk[:, 2048 + g * 128:2048 + (g + 1) * 128], xw[:, g * 256:(g + 1) * 256],
                                                start=True, stop=True), [bxk, bxw], [pb], inc=(g % 2 == 1))
        pbs = [self.bbuf[i] for i in range(4)]
        H3 = H[:].rearrange("p (h q) -> p h q", h=32)
        self.G(lambda: nc.gpsimd.tensor_tensor(out=H3, in0=H3, in1=dec.unsqueeze(2).to_broadcast([128, 32, 64]),
                                               op=ALU.mult), [bH, be], [bH])
        self.V(lambda: nc.vector.tensor_tensor(out=H[:], in0=H[:], in1=self.psA[:, :], op=ALU.add), [bH] + pbs, [bH])

    def stage_l0_ssd(self):
        nc = self.nc
        with ExitStack() as st:
            T = lambda shape, dt, nm: self.tile(st, shape, dt, nm)
            DT_, bDT = T([128, 24, 64], F32, "DTall")
            DA_, bDA = T([128, 24, 64], F32, "DAall")
            E_, bE = T([128, 24, 128], F32, "Eall")
            CU_, bCU = T([128, 24, 64], F32, "CUall")
            with ExitStack() as s2:
                T2 = lambda shape, dt, nm: self.tile(s2, shape, dt, nm)
                dtb, bdtb = T2([128, 64], F32, "dtb")
                a, ba = T2([128, 64], F32, "a")
                tl = [T2([128, 1536], F32, "tl") for _ in range(3)]
                self.ld(dtb[:], self.dtb_d[:, :], [bdtb])
                self.ld(a[:], self.alog_d[:, :], [ba])
                self.ld(DT_[:], self.DT.rearrange("(c p) k -> p c k", p=128), [bDT])
                self.A(lambda: nc.scalar.activation(out=a[:], in_=a[:], func=AF.Exp), [ba], [ba])
                self.V(lambda: nc.vector.tensor_scalar(a[:], a[:], -1.0, None, ALU.mult), [ba], [ba])
                self.V(lambda: nc.vector.tensor_tensor(out=DT_[:], in0=DT_[:], in1=dtb[:].unsqueeze(1).to_broadcast([128, 24, 64]),
                                                       op=ALU.add), [bDT, bdtb], [bDT])
                self.V(lambda: nc.vector.tensor_scalar(DT_[:], DT_[:], 40.0, None, ALU.min), [bDT], [bDT])
                self.A(lambda: nc.scalar.activation(out=DT_[:], in_=DT_[:], func=AF.Exp), [bDT], [bDT])
                self.A(lambda: nc.scalar.activation(out=DT_[:], in_=DT_[:], func=AF.Ln, bias=1.0), [bDT], [bDT])
                self.V(lambda: nc.vector.tensor_tensor(out=DA_[:], in0=DT_[:], in1=a[:].unsqueeze(1).to_broadcast([128, 24, 64]),
                                                       op=ALU.mult), [bDT, ba], [bDA])
                flat = DA_[:].rearrange("p c k -> p (c k)")
                for mi, (t_, bt_) in zip((1, 2, 5), tl):
                    for q in range(3):
                        pa, pb = self.ps(q)
                        self.P(lambda: nc.tensor.matmul(pa, self.cf[:, mi, :], flat[:, q * 512:(q + 1) * 512], start=True, stop=True),
                               [self.bcf, bDA], [pb])
                        self.A(lambda: nc.scalar.copy(out=t_[:, q * 512:(q + 1) * 512], in_=pa), [pb], [bt_])
                v3 = lambda t_: t_[:].rearrange("p (c k) -> p c k", c=24)
                self.V(lambda: nc.vector.tensor_copy(out=E_[:, :, 0:32], in_=v3(tl[0][0])[:, :, 0:32]), [tl[0][1]], [bE])
                self.V(lambda: nc.vector.tensor_copy(out=E_[:, :, 32:64], in_=v3(tl[1][0])[:, :, 32:64]), [tl[1][1]], [bE])
                self.V(lambda: nc.vector.tensor_copy(out=E_[:, :, 64:128], in_=v3(tl[2][0])), [tl[2][1]], [bE])
                self.V(lambda: nc.vector.tensor_tensor(out=CU_[:], in0=E_[:, :, 64:128], in1=E_[:, :, 0:64],
                                                       op=ALU.subtract), [bE], [bCU])
                self.A(lambda: nc.scalar.activation(out=CU_[:], in_=CU_[:], func=AF.Exp), [bCU], [bCU])
                self.V(lambda: nc.vector.tensor_tensor(out=CU_[:], in0=CU_[:], in1=DT_[:], op=ALU.mult),
                       [bCU, bDT], [bCU])
                self.A(lambda: nc.scalar.activation(out=E_[:], in_=E_[:], func=AF.Exp), [bE], [bE])
                self.cx.barrier()
            cst4 = (DT_, bDT, DA_, bDA, E_, bE, CU_, bCU)
            self.sweep_bwd(cst4)
            self.cx.barrier()
            self.sweep_fwd(cst4)
            self.cx.barrier()

    def sweep_bwd(self, cst4):
        nc = self.nc
        DT_, bDT, DA_, bDA, E_, bE, CU_, bCU = cst4
        with ExitStack() as st:
            T = lambda shape, dt, nm: self.tile(st, shape, dt, nm)
            H, bH = T([128, D], F32, "HB")
            tmpf, btmpf = T([128, D], F32, "tmpf")
            xtk = [T([128, 3072], BF16, "xtk") for _ in range(2)]
            xw = [T([128, D], BF16, "xw") for _ in range(2)]
            snap = [T([128, D], BF16, "snap") for _ in range(2)]
            seqs = [(0, 16, None)] + [(16 + 2 * p, 2, p) for p in range(4)]
            order = []
            for (c0, n, p) in seqs:
                for c in range(c0 + n - 1, c0 - 1, -1):
                    order.append((c, c == c0 + n - 1, c == c0, p))

            def loads(i):
                c = order[i][0]
                xk, bxk = xtk[i % 2]
                self.stq(xk[:], self.XSB[c * 128:(c + 1) * 128, :], writes=[bxk])
            loads(0)
            for i, (c, first, last, p) in enumerate(order):
                if i + 1 < len(order):
                    loads(i + 1)
                if first:
                    if p is None:
                        self.load_state(H, bH, self.sb_d, tmpf, btmpf)
                    else:
                        self.V(lambda: nc.vector.memset(H[:], 0.0), [], [bH])
                k = i % 2
                xk, bxk = xtk[k]
                sn, bsn = snap[k]
                self.A(lambda: nc.scalar.copy(out=sn[:], in_=H[:]), [bH], [bsn])
                self.stq(self.HBS[c, :, :], sn[:], [bsn])
                w_, bw_ = xw[k]
                self.G(lambda: nc.gpsimd.tensor_tensor(out=w_[:].rearrange("p (h q) -> p h q", h=32),
                                                       in0=xk[:, 0:D].rearrange("p (h q) -> p h q", h=32),
                                                       in1=CU_[:, c, 32:64].unsqueeze(2).to_broadcast([128, 32, 64]),
                                                       op=ALU.mult), [bxk, bCU], [bw_])
                self.state_update(H, bH, w_, bw_, xk, bxk, E_[:, c, 96:128], bE)
                if last and p is not None:
                    self.store_state(H, bH, self.nb_o[p, :, :], tmpf, btmpf)

    def sweep_fwd(self, cst4):
        nc = self.nc
        DT_, bDT, DA_, bDA, E_, bE, CU_, bCU = cst4
        with ExitStack() as st:
            T = lambda shape, dt, nm: self.tile(st, shape, dt, nm)
            H, bH = T([128, D], F32, "HF")
            tmpf, btmpf = T([128, D], F32, "tmpf")
            tmpg, btmpg = T([128, D], F32, "tmpg")
            yacc, byacc = T([128, D], F32, "yacc")
            Dbc, bDbc = T([128, D], F32, "Dbc")
            sn_, bsn_ = T([128, D], F32, "ssmn")
            self.ld(Dbc[:], self.dsk_d[:, 0, :], [bDbc])
            self.ld(tmpf[:], self.dsk_d[:, 1, :], [btmpf])
            self.ld(sn_[:], self.ssmn_d[:, :], [bsn_])
            self.V(lambda: nc.vector.tensor_tensor(out=Dbc[:], in0=Dbc[:], in1=tmpf[:], op=ALU.add), [bDbc, btmpf], [bDbc])
            xtk = [T([128, 3072], BF16, "xtk") for _ in range(3)]
            zb = [T([128, D], BF16, "zb") for _ in range(2)]
            hbb = [T([128, D], BF16, "hbb") for _ in range(2)]
            bct = [T([128, 16, 128], BF16, "bct") for _ in range(3)]
            hfb, bhfb = T([128, D], BF16, "hfb")
            R = [T([128, 32, 128], BF16, "R") for _ in range(1)]
            MmP = [[T([128, 32, 128], BF16, "Mm") for _ in range(2)] for _ in range(2)]
            CB = [T([128, 8, 128], BF16, "CB") for _ in range(2)]
            LT = [T([128, 512], BF16, "LT") for _ in range(2)]
            xdt = [T([128, D], BF16, "xdt") for _ in range(2)]
            xw_, bxw_ = T([128, D], BF16, "xwf")
            ybf, bybf = T([128, D], BF16, "ybf")
            ybT = [T([128, 16, 128], BF16, "ybT") for _ in range(1)]
            s8, bs8 = T([128, 8], F32, "s8")
            LEb = self.cb[:, 1, :]
            GEb = self.cb[:, 2, :]
            GTb = self.cb[:, 3, :]
            LTb = self.cb[:, 4, :]
            seqs = [(0, 16, None)] + [(16 + 2 * p, 2, p) for p in range(4)]
            order = []
            for (c0, n, p) in seqs:
                for c in range(c0, c0 + n):
                    order.append((c, c == c0, c == c0 + n - 1, p))

            def loads(i):
                c = order[i][0]
                k = i % 2
                rows = slice(c * 128, (c + 1) * 128)
                k3 = i % 3
                self.stq(bct[k3][0][:], self.BCT[:, :, c * 128:(c + 1) * 128].rearrange("k p t -> p k t"), writes=[bct[k3][1]])
                self.stq(xtk[k3][0][:], self.XSB[rows, :], writes=[xtk[k3][1]])

            def loadsB(i):
                c = order[i][0]
                k = i % 2
                rows = slice(c * 128, (c + 1) * 128)
                self.stq(zb[k][0][:], self.ZB[rows, :], writes=[zb[k][1]])
                self.stq(hbb[k][0][:], self.HBS[c, :, :], writes=[hbb[k][1]])
            def front(i):
                c = order[i][0]
                k = i % 2
                bc, bbc = bct[i % 3]
                Mm = MmP[k]
                pbs45 = [self.bbuf[4], self.bbuf[5]]
                for g in range(8):
                    pa, pb = self.ps(4 + g // 4)
                    self.P(lambda g=g: nc.tensor.matmul(pa[:, (g % 4) * 128:(g % 4 + 1) * 128], bc[:, g, :], bc[:, 8 + g, :],
                                                        start=True, stop=True), [bbc], [pb], inc=(g % 4 == 3))
                cbps = self.psB[:, 0:1024].rearrange("p (g l) -> p g l", g=8)
                self.V(lambda: nc.vector.tensor_tensor(out=CB[0][0][:], in0=cbps,
                                                       in1=LEb.unsqueeze(1).to_broadcast([128, 8, 128]), op=ALU.mult),
                       pbs45 + [self.bcb], [CB[0][1]])
                self.V(lambda: nc.vector.tensor_tensor(out=CB[1][0][:], in0=cbps,
                                                       in1=GEb.unsqueeze(1).to_broadcast([128, 8, 128]), op=ALU.mult),
                       pbs45 + [self.bcb], [CB[1][1]])
                for d in range(2):
                    Rt, bR = R[0]
                    Mt, bMm = Mm[d]
                    msk = LEb if d == 0 else GEb
                    lm = GTb if d == 0 else LTb
                    for h in range(32):
                        self.V(lambda h=h: nc.vector.tensor_scalar(Rt[:, h, :], msk, DA_[:, c, d * 32 + h:d * 32 + h + 1], None,
                                                                    ALU.mult), [self.bcb, bDA], [bR], so=(h > 0))
                    for q in range(8):
                        bi = 6 + (q % 2)
                        pa, pb = self.ps(bi)
                        self.P(lambda q=q: nc.tensor.matmul(pa, lm, Rt[:, 4 * q:4 * q + 4, :].rearrange("p h l -> p (h l)"),
                                                            start=True, stop=True), [self.bcb, bR], [pb])
                        lt, blt = LT[q % 2]
                        self.A(lambda: nc.scalar.activation(out=lt[:], in_=pa, func=AF.Exp), [pb], [blt])
                        self.V(lambda q=q: nc.vector.tensor_tensor(out=Mt[:, 4 * q:4 * q + 4, :],
                                                                   in0=lt[:].rearrange("p (h l) -> p h l", h=4),
                                                                   in1=CB[d][0][:, q:q + 1, :].to_broadcast([128, 4, 128]),
                                                                   op=ALU.mult), [blt, CB[d][1]], [bMm], so=(q > 0))
            loads(0)
            loads(1)
            loadsB(0)
            front(0)
            for i, (c, first, last, p) in enumerate(order):
                if i + 2 < len(order):
                    loads(i + 2)
                if i + 1 < len(order):
                    loadsB(i + 1)
                    front(i + 1)
                Mm = MmP[i % 2]
                if first:
                    if p is None:
                        self.load_state(H, bH, self.sf_d, tmpf, btmpf)
                    else:
                        self.V(lambda: nc.vector.memset(H[:], 0.0), [], [bH])
                k = i % 2
                xk, bxk = xtk[i % 3]
                z, bz = zb[k]
                hb_, bhb_ = hbb[k]
                bc, bbc = bct[i % 3]
                x3 = xk[:, 0:D].rearrange("p (h q) -> p h q", h=32)
                self.A(lambda: nc.scalar.copy(out=hfb[:], in_=H[:]), [bH], [bhfb])
                for d in range(2):
                    xd, bxd = xdt[d]
                    self.G(lambda d=d: nc.gpsimd.tensor_tensor(out=xd[:].rearrange("p (h q) -> p h q", h=32), in0=x3,
                                                               in1=DT_[:, c, d * 32:(d + 1) * 32].unsqueeze(2).to_broadcast([128, 32, 64]),
                                                               op=ALU.mult), [bxk, bDT], [bxd])
                self.G(lambda: nc.gpsimd.tensor_tensor(out=tmpg[:], in0=xk[:, 0:D], in1=Dbc[:], op=ALU.mult), [bxk, bDbc], [btmpg])
                self.G(lambda: nc.gpsimd.tensor_tensor(out=xw_[:].rearrange("p (h q) -> p h q", h=32), in0=x3,
                                                       in1=CU_[:, c, 0:32].unsqueeze(2).to_broadcast([128, 32, 64]),
                                                       op=ALU.mult), [bxk, bCU], [bxw_])
                for h in range(32):
                    pa, pb = self.ps(h // 8)
                    po = pa[:, (h % 8) * 64:(h % 8 + 1) * 64]
                    self.P(lambda h=h: nc.tensor.matmul(po, Mm[0][0][:, h, :], xdt[0][0][:, h * 64:(h + 1) * 64],
                                                        start=True, stop=False), [Mm[0][1], xdt[0][1]], [pb], inc=False)
                    self.P(lambda h=h: nc.tensor.matmul(po, Mm[1][0][:, h, :], xdt[1][0][:, h * 64:(h + 1) * 64],
                                                        start=False, stop=True), [Mm[1][1], xdt[1][1]], [pb],
                           inc=(h % 8 == 7))
                pbsA = [self.bbuf[i] for i in range(4)]
                self.V(lambda: nc.vector.tensor_tensor(out=yacc[:], in0=tmpg[:], in1=self.psA[:, :], op=ALU.add),
                       [btmpg] + pbsA, [byacc])
                for d in range(2):
                    Hs, bHs = (hfb, bhfb) if d == 0 else (hb_, bhb_)
                    tt, btt = (tmpf, btmpf) if d == 0 else (tmpg, btmpg)
                    for g in range(8):
                        pa, pb = self.ps(g // 2)
                        self.P(lambda g=g: nc.tensor.matmul(pa[:, (g % 2) * 256:(g % 2 + 1) * 256], bc[:, 8 + g, :],
                                                            Hs[:, g * 256:(g + 1) * 256], start=True, stop=True),
                               [bbc, bHs], [pb], inc=(g % 2 == 1))
                    self.V(lambda d=d: nc.vector.tensor_tensor(out=tt[:].rearrange("p (h q) -> p h q", h=32),
                                                               in0=self.psA[:, :].rearrange("p (h q) -> p h q", h=32),
                                                               in1=E_[:, c, d * 32:(d + 1) * 32].unsqueeze(2).to_broadcast([128, 32, 64]),
                                                               op=ALU.mult), pbsA + [bE], [btt])
                    self.V(lambda: nc.vector.tensor_tensor(out=yacc[:], in0=yacc[:], in1=tt[:], op=ALU.add),
                           [byacc, btt], [byacc])
                self.state_update(H, bH, xw_, bxw_, xk, bxk, E_[:, c, 64:96], bE)
                self.V(lambda: nc.vector.tensor_tensor(out=yacc[:], in0=yacc[:], in1=z[:], op=ALU.mult), [byacc, bz], [byacc])
                for g in range(8):
                    self.A(lambda g=g: nc.scalar.activation(out=ybf[:, g * 256:(g + 1) * 256], in_=yacc[:, g * 256:(g + 1) * 256],
                                                            func=AF.Square, accum_out=s8[:, g:g + 1]), [byacc], [bybf, bs8], so=(g > 0))
                self.rstd(st, s8[:], bs8, 1.0 / 256)
                for g in range(8):
                    self.V(lambda g=g: nc.vector.scalar_tensor_tensor(out=ybf[:, g * 256:(g + 1) * 256], in0=yacc[:, g * 256:(g + 1) * 256],
                                                                      scalar=s8[:, g:g + 1], in1=sn_[:, g * 256:(g + 1) * 256],
                                                                      op0=ALU.mult, op1=ALU.mult), [byacc, bs8, bsn_], [bybf], so=(g > 0))
                yT, byT = ybT[0]
                for half in range(2):
                    bi = 4 + half
                    pa, pb = self.ps(bi)
                    pab = pa.bitcast(BF16)
                    for q in range(8):
                        kc = half * 8 + q
                        self.P(lambda kc=kc, q=q: nc.tensor.transpose(pab[:, q * 128:(q + 1) * 128],
                                                                       ybf[:, kc * 128:(kc + 1) * 128], self.identb),
                               [bybf, self.bcb], [pb], inc=(q == 7))
                    self.A(lambda: nc.scalar.copy(out=yT[:, half * 8:(half + 1) * 8, :],
                                                  in_=pab.rearrange("p (q t) -> p q t", q=8)), [pb], [byT])
                self.stq(self.MIXT[16:32, :, c * 128:(c + 1) * 128].rearrange("k p t -> p k t"), yT[:], [byT])
                if last and p is not None:
                    self.store_state(H, bH, self.nf_o[p, :, :], tmpf, btmpf)

    def stage_out(self, layer):
        nc = self.nc
        KC = 32 if layer == 0 else 16
        W = self.w_out0 if layer == 0 else self.w_out1
        xsrc = self.xtok if layer == 0 else self.X1
        dst = self.X1 if layer == 0 else self.y_o
        Wv = W.rearrange("(kc p) n -> p kc n", p=128)
        TCH = 6
        SW = 256 if KC == 32 else 512
        with ExitStack() as st:
            T = lambda shape, dt, nm: self.tile(st, shape, dt, nm)
            mT, bmT = T([128, KC, TCH * 128], BF16, "mT")
            wsl = [T([128, KC, SW], BF16, "wo") for _ in range(2)]
            G2 = [T([128, D], F32, "G2") for _ in range(2)]
            for c in range(2):
                self.ld(G2[c][0][:], self.MS[c, :, 2 * D:3 * D], [G2[c][1]])
            oacc = [T([128, D], F32, "oacc") for _ in range(TCH)]
            xb = [T([128, D], F32, "xb") for _ in range(2)]
            junk, bjunk = T([128, D], BF16, "junk")
            ss = [T([128, 1], F32, "ss") for _ in range(2)]
            for pas in range(NCH // TCH):
                c0 = pas * TCH
                self.ld(mT[:], self.MIXT[0:KC, :, c0 * 128:(c0 + TCH) * 128].rearrange("k p t -> p k t"), [bmT])
                for s in range(D // SW):
                    w, bw = wsl[s % 2]
                    self.ld(w[:], Wv[:, :, s * SW:(s + 1) * SW], [bw])
                    for j in range(TCH):
                        bi = self.nextbank()
                        pa, pb = self.ps(bi)
                        pa = pa[:, 0:SW]
                        for kc in range(KC):
                            self.P(lambda kc=kc: nc.tensor.matmul(pa, mT[:, kc, j * 128:(j + 1) * 128], w[:, kc, :],
                                                                  start=(kc == 0), stop=(kc == KC - 1)),
                                   [bmT, bw], [pb], inc=(kc == KC - 1))
                        o, bo = oacc[j]
                        self.A(lambda: nc.scalar.copy(out=o[:, s * SW:(s + 1) * SW], in_=pa), [pb], [bo])
                for j in range(TCH):
                    c = c0 + j
                    cond = 0 if c < 16 else 1
                    Mt, bM = G2[cond]
                    o, bo = oacc[j]
                    x, bx = xb[j % 2]
                    s1, bs1 = ss[j % 2]
                    rows = slice(c * 128, (c + 1) * 128)
                    self.ld(x[:], xsrc[rows, :], [bx])
                    self.A(lambda: nc.scalar.activation(out=junk[:], in_=o[:], func=AF.Square, accum_out=s1[:]),
                           [bo], [bjunk, bs1])
                    self.rstd(st, s1[:], bs1, 1.0 / D)
                    self.V(lambda: nc.vector.scalar_tensor_tensor(out=o[:], in0=o[:], scalar=s1[:], in1=Mt[:],
                                                                  op0=ALU.mult, op1=ALU.mult), [bo, bs1, bM], [bo])
                    self.V(lambda: nc.vector.tensor_tensor(out=o[:], in0=o[:], in1=x[:], op=ALU.add), [bo, bx], [bo])
                    self.stq(dst[rows, :], o[:], [bo])
            self.cx.barrier()

    def stage_attn(self):
        nc = self.nc
        SCALE = 128 ** -0.5
        with ExitStack() as st:
            T = lambda shape, dt, nm: self.tile(st, shape, dt, nm)
            kT, bkT = T([128, 4, 2560], BF16, "kT")
            va, bva = T([128, 20, 4, 132], BF16, "va")
            kt = [T([128, 512], BF16, "kt") for _ in range(2)]
            qt = [T([128, D], BF16, "qt") for _ in range(2)]
            qT, bqT = T([128, 16, 512], BF16, "qT")
            zt = [T([128, D], BF16, "zt") for _ in range(4)]
            pt = [T([128, 512], BF16, "pt") for _ in range(3)]
            mix = [T([128, D], BF16, "mix") for _ in range(4)]
            mxT = [T([128, 16, 128], BF16, "mxT") for _ in range(2)]
            rc = [T([128, 1], F32, "rc") for _ in range(4)]
            of = [T([128, 128], F32, "of") for _ in range(2)]
            self.V(lambda: nc.vector.memset(va[:], 1.0), [], [bva])
            seqs = [(0, 16, None)] + [(16 + 2 * p, 2, p) for p in range(4)]
            it = 0
            for (c0, n, p) in seqs:
                kch = ([NT // 128 + r for r in range(4)] if p is None else []) + list(range(c0, c0 + n))
                nk = len(kch)
                for i, kc_ in enumerate(kch):
                    it += 1
                    t, bt = kt[it % 2]
                    rows = slice(kc_ * 128, (kc_ + 1) * 128)
                    self.ld(t[:], self.Ks[rows, :], [bt])
                    self.ld(va[:, i, :, 0:128], self.V1[rows, :].rearrange("t (h d) -> t h d", h=4), [bva])
                    bi = self.nextbank(6, 8)
                    pa, pb = self.ps(bi)
                    pab = pa.bitcast(BF16)
                    for h in range(4):
                        self.P(lambda h=h: nc.tensor.transpose(pab[:, h * 128:(h + 1) * 128], t[:, h * 128:(h + 1) * 128],
                                                               self.identb), [bt, self.bcb], [pb], inc=(h == 3))
                    self.V(lambda: nc.vector.tensor_copy(out=kT[:, :, i * 128:(i + 1) * 128],
                                                         in_=pab[:, 0:512].rearrange("p (h t) -> p h t", h=4)), [pb], [bkT])
                nqc = min(4, n)
                for qt0 in range(c0, c0 + n, nqc):
                    nq = nqc * 128
                    for j in range(nqc):
                        c = qt0 + j
                        it += 1
                        q_, bq_ = qt[it % 2]
                        rows = slice(c * 128, (c + 1) * 128)
                        self.ld(q_[:], self.Qs[rows, :], [bq_])
                        self.ld(zt[j][0][:], self.Z1[rows, :], [zt[j][1]])
                        for half in range(2):
                            bi = self.nextbank(6, 8)
                            pa, pb = self.ps(bi)
                            pab = pa.bitcast(BF16)
                            for q in range(8):
                                hh = half * 8 + q
                                self.P(lambda hh=hh, q=q: nc.tensor.transpose(pab[:, q * 128:(q + 1) * 128],
                                                                               q_[:, hh * 128:(hh + 1) * 128], self.identb),
                                       [bq_, self.bcb], [pb], inc=(q == 7))
                            self.V(lambda: nc.vector.tensor_copy(out=qT[:, half * 8:(half + 1) * 8, j * 128:(j + 1) * 128],
                                                                 in_=pab.rearrange("p (q t) -> p q t", q=8)), [pb], [bqT])
                    iters = [(h, i) for h in range(16) for i in range(nk)]

                    def qk(t):
                        h, i = iters[t]
                        pa, pb = self.ps(4 + t % 2)
                        self.P(lambda: nc.tensor.matmul(pa[:, 0:nq], kT[:, h // 4, i * 128:(i + 1) * 128], qT[:, h, 0:nq],
                                                        start=True, stop=True), [bkT, bqT], [pb])
                    qk(0)
                    for t, (h, i) in enumerate(iters):
                        kh = h // 4
                        if t + 1 < len(iters):
                            qk(t + 1)
                        pa, pb = self.ps(4 + t % 2)
                        p_, bp_ = pt[t % 3]
                        self.A(lambda: nc.scalar.activation(out=p_[:, 0:nq], in_=pa[:, 0:nq], func=AF.Exp, scale=SCALE),
                               [pb], [bp_])
                        for j in range(nqc):
                            po, pob = self.ps(j)
                            self.P(lambda j=j, i=i: nc.tensor.matmul(po[:, 0:129], p_[:, j * 128:(j + 1) * 128],
                                                                      va[:, i, kh, 0:129], start=(i == 0), stop=(i == nk - 1)),
                                   [bp_, bva], [pob], inc=(i == nk - 1))
                        if i == nk - 1:
                            for j in range(nqc):
                                po, pob = self.ps(j)
                                r_, br_ = rc[j]
                                self.V(lambda: nc.vector.reciprocal(r_[:], po[:, 128:129]), [pob], [br_])
                                self.V(lambda j=j: nc.vector.scalar_tensor_tensor(out=mix[j][0][:, h * 128:(h + 1) * 128],
                                                                                  in0=po[:, 0:128], scalar=r_[:],
                                                                                  in1=zt[j][0][:, h * 128:(h + 1) * 128],
                                                                                  op0=ALU.mult, op1=ALU.mult),
                                       [pob, br_, zt[j][1]], [mix[j][1]])
                    for j in range(nqc):
                        c = qt0 + j
                        it += 1
                        yT, byT = mxT[it % 2]
                        for half in range(2):
                            bi = self.nextbank(6, 8)
                            pa, pb = self.ps(bi)
                            pab = pa.bitcast(BF16)
                            for q in range(8):
                                kc = half * 8 + q
                                self.P(lambda kc=kc, q=q: nc.tensor.transpose(pab[:, q * 128:(q + 1) * 128],
                                                                               mix[j][0][:, kc * 128:(kc + 1) * 128], self.identb),
                                       [mix[j][1], self.bcb], [pb], inc=(q == 7))
                            self.V(lambda: nc.vector.tensor_copy(out=yT[:, half * 8:(half + 1) * 8, :],
                                                                 in_=pab.rearrange("p (q t) -> p q t", q=8)), [pb], [byT])
                        self.stq(self.MIXT[0:16, :, c * 128:(c + 1) * 128].rearrange("k p t -> p k t"), yT[:], [byT])
            self.cx.barrier()


def _consts():
    k = np.arange(128)[:, None]
    m = np.arange(128)[None, :]
    cst = np.stack([np.eye(128), k <= m, k >= m, k > m, k < m, np.ones((128, 128))], axis=1).astype(np.float32)
    n = 2048
    rows = n // 64
    t_row = np.repeat(np.arange(rows), 64).astype(np.float32)
    t_col = np.tile(np.arange(64), rows).astype(np.float32)
    half = 64
    inv = (10000.0 ** (-np.arange(0, half, 2, dtype=np.float32) / half)).astype(np.float32)
    ar = t_row[:, None] * inv[None, :]
    ac = t_col[:, None] * inv[None, :]
    cos = np.concatenate([np.cos(ar), np.cos(ar), np.cos(ac), np.cos(ac)], axis=1).astype(np.float32)
    sin = np.concatenate([-np.sin(ar), np.sin(ar), -np.sin(ac), np.sin(ac)], axis=1).astype(np.float32)
    return np.ascontiguousarray(cst), cos, sin


def _bc(v, n=128):
    v = np.asarray(v, np.float32).reshape(1, -1)
    return np.ascontiguousarray(np.broadcast_to(v, (n, v.shape[1])))


def prep_core(inp, i, shared):
    f = lambda a: np.ascontiguousarray(np.asarray(a, np.float32))
    m = dict(shared)
    xs = np.asarray(inp["x_sample"][i], np.float32)
    xp = np.asarray(inp["x_prompt"][4 * i:4 * i + 4], np.float32).reshape(1024, D)
    m["xtok"] = np.ascontiguousarray(np.concatenate([xs, xp], axis=0))
    c = np.asarray(inp["c"][i], np.float32).reshape(16, 128).T
    cc = np.asarray(inp["c_ctx"], np.float32).reshape(16, 128).T
    m["condT"] = np.ascontiguousarray(np.concatenate([c, cc], axis=1))
    m["sf"] = f(np.asarray(inp["state_l0_ssm_fwd"][i]).reshape(D, 128))
    m["sb"] = f(np.asarray(inp["state_l0_ssm_bwd"][i]).reshape(D, 128))
    m["ck"] = f(np.asarray(inp["cache_l1_k"][i]).reshape(512, 512))
    m["cv"] = f(np.asarray(inp["cache_l1_v"][i]).reshape(512, 512))
    return m


def prep_shared(inp):
    f = lambda a: np.ascontiguousarray(np.asarray(a, np.float32))
    cst, cos, sin = _consts()
    s = {}
    s["mod_w0"] = f(inp["mod_w0"])
    s["mod_w1"] = f(inp["mod_w1"])
    s["mod_b0"] = _bc(inp["mod_b0"])
    s["mod_b1"] = _bc(inp["mod_b1"])
    s["npre0"] = _bc(inp["norm_pre0"])
    s["npre1"] = _bc(inp["norm_pre1"])
    s["npost0"] = _bc(inp["norm_post0"])
    s["npost1"] = _bc(inp["norm_post1"])
    s["l0_w_in"] = f(inp["l0_w_in"])
    s["l0_w_out"] = f(inp["l0_w_out"])
    s["l1_w_in"] = f(inp["l1_w_in"])
    s["l1_w_out"] = f(inp["l1_w_out"])
    s["vgain"] = _bc(inp["l0_v_gain"])
    s["wsT"] = f(np.transpose(np.asarray(inp["l0_w_s"], np.float32), (2, 0, 1)))
    s["bsrow"] = f(np.asarray(inp["l0_b_s"], np.float32).reshape(1, D))
    cw = np.asarray(inp["l0_conv_w"], np.float32)
    s["convw"] = f(np.transpose(cw.reshape(5, 32, 128), (2, 1, 0)))
    s["convb"] = f(np.asarray(inp["l0_conv_b"], np.float32).reshape(32, 128).T)
    s["dtb"] = _bc(np.asarray(inp["l0_dt_bias"], np.float32).reshape(-1))
    s["alog"] = _bc(np.asarray(inp["l0_a_log"], np.float32).reshape(-1))
    dsk = np.repeat(np.asarray(inp["l0_d_skip"], np.float32)[:, :, None], 64, axis=2).reshape(2, D)
    s["dsk"] = f(np.broadcast_to(dsk[None], (128, 2, D)))
    s["ssmn"] = _bc(inp["l0_ssm_norm"])
    qk = np.stack([np.asarray(inp["l1_q_norm"], np.float32), np.asarray(inp["l1_k_norm"], np.float32)], axis=0)
    s["qkn"] = f(np.broadcast_to(qk[None], (128, 2, 128)))
    s["ropec"] = cos
    s["ropes"] = sin
    s["cst"] = cst
    return s


def assemble(res, ncores=8):
    yp = np.zeros((4 * ncores, 256, D), np.float32)
    ys = np.zeros((ncores, 2048, D), np.float32)
    nf = np.zeros((4 * ncores, 32, 64, 128), np.float32)
    nb = np.zeros((4 * ncores, 32, 64, 128), np.float32)
    nk = np.zeros((4 * ncores, 256, 4, 128), np.float32)
    nv = np.zeros((4 * ncores, 256, 4, 128), np.float32)
    for i in range(ncores):
        r = res[i]
        y = np.asarray(r["y"])
        ys[i] = y[0:2048]
        yp[4 * i:4 * i + 4] = y[2048:].reshape(4, 256, D)
        nf[4 * i:4 * i + 4] = np.asarray(r["nfwd"]).reshape(4, 32, 64, 128)
        nb[4 * i:4 * i + 4] = np.asarray(r["nbwd"]).reshape(4, 32, 64, 128)
        nk[4 * i:4 * i + 4] = np.asarray(r["nk"]).reshape(4, 256, 4, 128)
        nv[4 * i:4 * i + 4] = np.asarray(r["nv"]).reshape(4, 256, 4, 128)
    return yp, ys, nf, nb, nk, nv


def kernel(**inputs):
    shared = prep_shared(inputs)
    in_maps = [prep_core(inputs, i, shared) for i in range(8)]
    nc = K().build()
    res = run_bass_kernel_spmd(nc, in_maps, core_ids=list(range(8)))
    return assemble(res.results, 8)
```

```python
import numpy as np
from contextlib import ExitStack
import concourse.bass as bass
import concourse.mybir as mybir
from concourse.bass_utils import run_bass_kernel_spmd

F32 = mybir.dt.float32
BF16 = mybir.dt.bfloat16
AF = mybir.ActivationFunctionType
ALU = mybir.AluOpType
AX = mybir.AxisListType

NT = 3072
NCH = 24
D = 2048
EPS = 1e-6
L0_IN = 12352
SAME_ENG_SYNC = True


class Buf:
    __slots__ = ("name", "wr", "rds")

    def __init__(self, name=""):
        self.name = name
        self.wr = None
        self.rds = []


class Eng:
    def __init__(self, ctx, name, eng, ndma=0):
        self.name = name
        self.eng = eng
        self.sem = ctx.nc.alloc_semaphore("e_" + name)
        self.cnt = 0
        self.seen = {}
        self.pool = [[ctx.nc.alloc_semaphore("d_%s%d" % (name, i)), 0] for i in range(ndma)]
        self.pi = 0

    def wait(self, tok):
        sem, val = tok
        k = id(sem)
        if self.seen.get(k, 0) >= val:
            return
        self.eng.wait_ge(sem, val)
        self.seen[k] = val


class Ctx:
    def __init__(self, nc):
        self.nc = nc
        self.pe = Eng(self, "pe", nc.tensor)
        self.dve = Eng(self, "dve", nc.vector)
        self.act = Eng(self, "act", nc.scalar)
        self.pool = Eng(self, "pool", nc.gpsimd, ndma=16)
        self.sp = Eng(self, "sp", nc.sync, ndma=24)
        self.engs = [self.pe, self.dve, self.act, self.pool, self.sp]
        self.ninst = 0

    def _deps(self, e, reads, writes, so=False):
        deps = []
        for b in reads:
            if b.wr is not None:
                deps.append(b.wr)
        for b in writes:
            deps.extend(b.rds)
            if b.wr is not None:
                deps.append(b.wr)
        best = {}
        for sem, val in deps:
            k = id(sem)
            if k == id(e.sem) and (so or e is self.pe or not SAME_ENG_SYNC):
                continue
            if k not in best or best[k][1] < val:
                best[k] = (sem, val)
        for tok in best.values():
            e.wait(tok)

    def _record(self, tok, reads, writes):
        for b in writes:
            b.wr = tok
            b.rds = []
        for b in reads:
            if b not in writes:
                b.rds.append(tok)
                if len(b.rds) > 48:
                    best = {}
                    for s, v in b.rds:
                        if id(s) not in best or best[id(s)][1] < v:
                            best[id(s)] = (s, v)
                    b.rds = list(best.values())

    def op(self, e, fn, reads=(), writes=(), inc=True, so=False):
        self._deps(e, reads, writes, so)
        ins = fn()
        self.ninst += 1
        if inc:
            ins.then_inc(e.sem, 1)
            e.cnt += 1
            tok = (e.sem, e.cnt)
        else:
            tok = (e.sem, e.cnt + 1)
        self._record(tok, reads, writes)
        return tok

    def dma(self, e, out, in_, reads=(), writes=(), **kw):
        slot = e.pool[e.pi]
        e.pi = (e.pi + 1) % len(e.pool)
        if slot[1] > 0:
            e.wait((slot[0], 16 * slot[1]))
        self._deps(e, reads, writes)
        ins = e.eng.dma_start(out=out, in_=in_, **kw)
        ins.then_inc(slot[0], 16)
        slot[1] += 1
        self.ninst += 1
        tok = (slot[0], 16 * slot[1])
        self._record(tok, reads, writes)
        return tok

    def barrier(self):
        toks = []
        for e in self.engs:
            if e.cnt > 0:
                toks.append((e.sem, e.cnt))
            for s, c in e.pool:
                if c > 0:
                    toks.append((s, 16 * c))
        for e in self.engs:
            for t in toks:
                if id(t[0]) == id(e.sem):
                    continue
                e.wait(t)

    def finish(self):
        for e in self.engs:
            if e.cnt > 0:
                self.sp.wait((e.sem, e.cnt))
            for s, c in e.pool:
                if c > 0:
                    self.sp.wait((s, 16 * c))


class K:
    def __init__(self, upto=99, dbg=False):
        self.upto = upto
        self.dbg = dbg
        nc = self.nc = bass.Bass("TRN2", target_bir_lowering=False)
        self.cx = Ctx(nc)
        self.din = {}
        self.dout = {}
        self.psA = nc.alloc_psum_tensor("psA", [128, 2048], F32)
        self.psB = nc.alloc_psum_tensor("psB", [128, 2048], F32)
        self.bank = [(self.psA, i) for i in range(4)] + [(self.psB, i) for i in range(4)]
        self.bbuf = [Buf("bank%d" % i) for i in range(8)]
        self.bi = 0
        self.tid = 0

    def inp(self, name, shape, dt=F32):
        t = self.nc.dram_tensor(name, list(shape), dt, kind="ExternalInput").ap()
        self.din[name] = t
        return t

    def outp(self, name, shape, dt=F32):
        t = self.nc.dram_tensor(name, list(shape), dt, kind="ExternalOutput").ap()
        self.dout[name] = t
        return t

    def scr(self, name, shape, dt=BF16):
        if self.dbg:
            t = self.nc.dram_tensor(name, list(shape), dt, kind="ExternalOutput").ap()
            self.dout[name] = t
            return t
        return self.nc.dram_tensor(name, list(shape), dt).ap()

    def tile(self, st, shape, dt, name=None):
        self.tid += 1
        nm = "%s_%d" % (name or "t", self.tid)
        t = st.enter_context(self.nc.sbuf_tensor(nm, list(shape), dt))
        return t, Buf(nm)

    def ps(self, i):
        t, j = self.bank[i]
        return t[:, j * 512:(j + 1) * 512], self.bbuf[i]

    def nextbank(self, lo=0, hi=8):
        if self.bi < lo or self.bi >= hi:
            self.bi = lo
        i = self.bi
        self.bi += 1
        if self.bi >= hi:
            self.bi = lo
        return i

    def V(self, fn, reads=(), writes=(), so=False):
        return self.cx.op(self.cx.dve, fn, reads, writes, so=so)

    def A(self, fn, reads=(), writes=(), so=False):
        return self.cx.op(self.cx.act, fn, reads, writes, so=so)

    def G(self, fn, reads=(), writes=(), so=False):
        return self.cx.op(self.cx.pool, fn, reads, writes, so=so)

    def P(self, fn, reads=(), writes=(), inc=True):
        return self.cx.op(self.cx.pe, fn, reads, writes, inc=inc)

    def ld(self, out, in_, writes=(), reads=()):
        return self.cx.dma(self.cx.pool, out, in_, reads=reads, writes=writes)

    def stq(self, out, in_, reads=(), writes=()):
        return self.cx.dma(self.cx.sp, out, in_, reads=reads, writes=writes)

    def rstd(self, st, ss, bss, scale, n=1):
        nc = self.nc
        self.V(lambda: nc.vector.tensor_scalar(ss, ss, scale, EPS, ALU.mult, ALU.add), [bss], [bss])
        self.A(lambda: nc.scalar.activation(out=ss, in_=ss, func=AF.Ln), [bss], [bss])
        self.A(lambda: nc.scalar.activation(out=ss, in_=ss, func=AF.Exp, scale=-0.5), [bss], [bss])

    def build(self):
        nc = self.nc
        I = self.inp
        self.xtok = I("xtok", [NT, D])
        self.condT = I("condT", [128, 32])
        self.mod_w = [I("mod_w0", [D, 3 * D]), I("mod_w1", [D, 3 * D])]
        self.mod_b = [I("mod_b0", [128, 3 * D]), I("mod_b1", [128, 3 * D])]
        self.npre = [I("npre0", [128, D]), I("npre1", [128, D])]
        self.npost = [I("npost0", [128, D]), I("npost1", [128, D])]
        self.w_in0 = I("l0_w_in", [D, L0_IN])
        self.w_out0 = I("l0_w_out", [2 * D, D])
        self.w_in1 = I("l1_w_in", [D, 5120])
        self.w_out1 = I("l1_w_out", [D, D])
        self.vgain = I("vgain", [128, D])
        self.wsT_d = I("wsT", [128, 16, 128])
        self.bsrow_d = I("bsrow", [1, D])
        self.convw_d = I("convw", [128, 32, 5])
        self.convb_d = I("convb", [128, 32])
        self.dtb_d = I("dtb", [128, 64])
        self.alog_d = I("alog", [128, 64])
        self.dsk_d = I("dsk", [128, 2, D])
        self.ssmn_d = I("ssmn", [128, D])
        self.sf_d = I("sf", [D, 128])
        self.sb_d = I("sb", [D, 128])
        self.ck_d = I("ck", [512, 512])
        self.cv_d = I("cv", [512, 512])
        self.qkn_d = I("qkn", [128, 2, 128])
        self.cos_d = I("ropec", [2048, 128])
        self.sin_d = I("ropes", [2048, 128])
        self.cst_d = I("cst", [128, 6, 128])
        O = self.outp
        self.y_o = O("y", [NT, D])
        self.nf_o = O("nfwd", [4, D, 128])
        self.nb_o = O("nbwd", [4, D, 128])
        self.nk_o = O("nk", [1024, 512])
        self.nv_o = O("nv", [1024, 512])
        S = self.scr
        self.UT = S("UT", [16, 128, NT])
        self.ZAT = S("ZAT", [16, 128, NT])
        self.XBCT = S("XBCT", [32, 128, NT])
        self.Vs = S("Vs", [NT, D])
        self.ZB = S("ZB", [NT, D])
        self.DT = S("DTs", [NT, 64], F32)
        self.BCT = S("BCT", [16, 128, NT])
        self.XSB = S("XSB", [NT, 3072])
        self.HBS = S("HBS", [NCH, 128, D])
        self.MIXT = S("MIXT", [32, 128, NT])
        self.X1 = S("X1", [NT, D], F32)
        self.Qs = S("Qs", [NT, D])
        self.Ks = S("Ks", [NT + 512, 512])
        self.V1 = S("V1", [NT + 512, 512])
        self.Z1 = S("Z1", [NT, D])
        self.MS = S("MS", [2, 128, 3 * D], F32)

        with ExitStack() as g:
            self.cb, self.bcb = self.tile(g, [128, 6, 128], BF16, "cstb")
            self.cf, self.bcf = self.tile(g, [128, 6, 128], F32, "cstf")
            self.ld(self.cb[:], self.cst_d[:, :, :], [self.bcb])
            self.ld(self.cf[:], self.cst_d[:, :, :], [self.bcf])
            self.identb = self.cb[:, 0, :]
            self.identf = self.cf[:, 0, :]

            self.modulation(0)
            self.stage_in(0)
            if self.upto >= 2:
                self.stage_l0_front()
            if self.upto >= 4:
                self.stage_l0_ssd()
            if self.upto >= 5:
                self.stage_out(0)
            if self.upto >= 6:
                self.modulation(1)
                self.stage_in(1)
            if self.upto >= 7:
                self.stage_attn()
            if self.upto >= 8:
                self.stage_out(1)
            self.cx.finish()
        return nc

    def modulation(self, layer):
        nc = self.nc
        with ExitStack() as st:
            cond, bcond = self.tile(st, [128, 32], F32, "cond")
            lbc, blbc = self.tile(st, [128, 32, 128], BF16, "lbc")
            mb, bmb = self.tile(st, [128, 3 * D], F32, "mb")
            tA, btA = self.tile(st, [128, D], F32, "tA")
            tB, btB = self.tile(st, [128, D], F32, "tB")
            wsl = [self.tile(st, [128, 16, 512], BF16, "mw") for _ in range(2)]
            Ms = [self.tile(st, [128, 3 * D], F32, "M%d" % c) for c in range(2)]
            self.ld(cond[:], self.condT[:, :], [bcond])
            self.ld(mb[:], self.mod_b[layer][:, :], [bmb])
            self.ld(tA[:], self.npre[layer][:, :], [btA])
            self.ld(tB[:], self.npost[layer][:, :], [btB])
            self.A(lambda: nc.scalar.activation(out=cond[:], in_=cond[:], func=AF.Silu), [bcond], [bcond])
            self.V(lambda: nc.vector.tensor_copy(out=lbc[:], in_=cond[:].unsqueeze(2).to_broadcast([128, 32, 128])),
                   [bcond], [blbc])
            Wv = self.mod_w[layer].rearrange("(kc p) n -> p kc n", p=128)
            for s in range(12):
                w, bw = wsl[s % 2]
                self.ld(w[:], Wv[:, :, s * 512:(s + 1) * 512], [bw])
                for c in range(2):
                    bi = self.nextbank()
                    pa, pb = self.ps(bi)
                    for kc in range(16):
                        self.P(lambda kc=kc: nc.tensor.matmul(pa, lbc[:, c * 16 + kc, :], w[:, kc, :],
                                                              start=(kc == 0), stop=(kc == 15)),
                               [blbc, bw], [pb], inc=(kc == 15))
                    Mt, bM = Ms[c]
                    self.V(lambda: nc.vector.tensor_tensor(out=Mt[:, s * 512:(s + 1) * 512], in0=pa,
                                                           in1=mb[:, s * 512:(s + 1) * 512], op=ALU.add),
                           [pb, bmb], [bM])
            for c in range(2):
                Mt, bM = Ms[c]
                self.V(lambda: nc.vector.scalar_tensor_tensor(out=Mt[:, D:2 * D], in0=Mt[:, D:2 * D], scalar=1.0,
                                                              in1=tA[:], op0=ALU.add, op1=ALU.mult), [bM, btA], [bM])
                self.V(lambda: nc.vector.tensor_tensor(out=Mt[:, 2 * D:3 * D], in0=Mt[:, 2 * D:3 * D], in1=tB[:],
                                                       op=ALU.mult), [bM, btB], [bM])
                self.stq(self.MS[c, :, :], Mt[:], [bM])
            self.cx.barrier()

    def stage_in(self, layer):
        nc = self.nc
        xsrc = self.xtok if layer == 0 else self.X1
        W = self.w_in0 if layer == 0 else self.w_in1
        if layer == 0:
            slabs = []
            for i in range(4):
                slabs.append((i * 512, 512, "B", "u"))
            for i in range(4):
                slabs.append((2048 + i * 512, 512, "A", "v"))
            for i in range(4):
                slabs.append((4096 + i * 512, 512, "B", "za"))
            for i in range(4):
                slabs.append((6144 + i * 512, 512, "A", "zb"))
            for i in range(8):
                slabs.append((8192 + i * 512, 512, "B", "xbc"))
            slabs.append((12288, 64, "A", "dt"))
        else:
            slabs = []
            for i in range(4):
                slabs.append((i * 512, 512, "A", "q"))
            slabs.append((2048, 512, "A", "k"))
            slabs.append((2560, 512, "A", "vv"))
            for i in range(4):
                slabs.append((3072 + i * 512, 512, "A", "z"))
        Wv = W.rearrange("(kc p) n -> p kc n", p=128)
        TCH = 12
        with ExitStack() as st:
            hT, bhT = self.tile(st, [128, 16, TCH * 128], BF16, "hT")
            wsl = [self.tile(st, [128, 16, 512], BF16, "wsl") for _ in range(2)]
            xb = [self.tile(st, [128, D], F32, "xb") for _ in range(2)]
            hb = [self.tile(st, [128, D], BF16, "hb") for _ in range(2)]
            ss = [self.tile(st, [128, 1], F32, "ss") for _ in range(2)]
            NOB = 8
            ob = [self.tile(st, [128, 512], F32, "ob") for _ in range(NOB)]
            obh = [self.tile(st, [128, 512], BF16, "obh") for _ in range(NOB)]
            Mg = [self.tile(st, [128, 2 * D], F32, "Mg") for _ in range(2)]
            for c in range(2):
                self.ld(Mg[c][0][:], self.MS[c, :, 0:2 * D], [Mg[c][1]])
            if layer == 1:
                qkn, bqkn = self.tile(st, [128, 2, 128], F32, "qkn")
                self.ld(qkn[:], self.qkn_d[:, :, :], [bqkn])
                cs = [self.tile(st, [128, 2, 128], F32, "cs") for _ in range(3)]
                sqs = [self.tile(st, [128, 512], F32, "sq") for _ in range(3)]
                s4 = [self.tile(st, [128, 4], F32, "s4") for _ in range(4)]
                rts = [self.tile(st, [128, 512], F32, "rt") for _ in range(3)]
                for (src, dst) in ((self.ck_d, self.Ks), (self.cv_d, self.V1)):
                    for r in range(4):
                        t, bt = obh[r]
                        self.ld(t[:], src[r * 128:(r + 1) * 128, :], [bt])
                        self.stq(dst[NT + r * 128:NT + (r + 1) * 128, :], t[:], [bt])
            oi = 0
            for pas in range(2):
                c0 = pas * TCH
                for j in range(TCH):
                    c = c0 + j
                    cond = 0 if c < 16 else 1
                    Mt, bM = Mg[cond]
                    x, bx = xb[j % 2]
                    h, bh = hb[j % 2]
                    s1, bs1 = ss[j % 2]
                    self.ld(x[:], xsrc[c * 128:(c + 1) * 128, :], [bx])
                    self.A(lambda: nc.scalar.activation(out=h[:], in_=x[:], func=AF.Square, accum_out=s1[:]),
                           [bx], [bh, bs1])
                    self.rstd(st, s1[:], bs1, 1.0 / D)
                    self.V(lambda: nc.vector.scalar_tensor_tensor(out=x[:], in0=x[:], scalar=s1[:], in1=Mt[:, D:2 * D],
                                                                  op0=ALU.mult, op1=ALU.mult), [bx, bs1, bM], [bx])
                    self.V(lambda: nc.vector.tensor_tensor(out=h[:], in0=x[:], in1=Mt[:, 0:D], op=ALU.add),
                           [bx, bM], [bh])
                    for half in range(2):
                        bi = self.nextbank()
                        pa, pb = self.ps(bi)
                        pab = pa.bitcast(BF16)
                        for q in range(8):
                            kc = half * 8 + q
                            self.P(lambda kc=kc, q=q: nc.tensor.transpose(pab[:, q * 128:(q + 1) * 128],
                                                                           h[:, kc * 128:(kc + 1) * 128], self.identb),
                                   [bh, self.bcb], [pb], inc=(q == 7))
                        src = pab.rearrange("p (q t) -> p q t", q=8)
                        dst = hT[:, half * 8:(half + 1) * 8, j * 128:(j + 1) * 128]
                        if half == 0:
                            self.A(lambda: nc.scalar.copy(out=dst, in_=src), [pb], [bhT])
                        else:
                            self.V(lambda: nc.vector.tensor_copy(out=dst, in_=src), [pb], [bhT])
                for si, (col0, ncols, var, kind) in enumerate(slabs):
                    w, bw = wsl[si % 2]
                    self.ld(w[:, :, 0:ncols], Wv[:, :, col0:col0 + ncols], [bw])
                    if var == "A":
                        for j in range(TCH):
                            c = c0 + j
                            bi = self.nextbank()
                            pa, pb = self.ps(bi)
                            pa = pa[:, 0:ncols]
                            for kc in range(16):
                                self.P(lambda kc=kc: nc.tensor.matmul(pa, hT[:, kc, j * 128:(j + 1) * 128],
                                                                      w[:, kc, 0:ncols], start=(kc == 0), stop=(kc == 15)),
                                       [bhT, bw], [pb], inc=(kc == 15))
                            oi += 1
                            o, bo = ob[oi % NOB]
                            oh, boh = obh[oi % NOB]
                            rows = slice(c * 128, (c + 1) * 128)
                            if kind in ("v", "zb", "z"):
                                fn = AF.Gelu_apprx_tanh if kind == "v" else AF.Silu
                                dst = {"v": self.Vs, "zb": self.ZB, "z": self.Z1}[kind]
                                cc = col0 - {"v": 2048, "zb": 6144, "z": 3072}[kind]
                                self.A(lambda: nc.scalar.activation(out=oh[:], in_=pa, func=fn), [pb], [boh])
                                self.stq(dst[rows, cc:cc + 512], oh[:], [boh])
                            elif kind == "dt":
                                self.A(lambda: nc.scalar.copy(out=o[:, 0:64], in_=pa), [pb], [bo])
                                self.stq(self.DT[rows, :], o[:, 0:64], [bo])
                            elif kind == "vv":
                                self.A(lambda: nc.scalar.copy(out=o[:], in_=pa), [pb], [bo])
                                self.V(lambda: nc.vector.tensor_copy(out=oh[:], in_=o[:]), [bo], [boh])
                                self.stq(self.V1[rows, :], oh[:], [boh])
                                if c >= 16:
                                    self.stq(self.nv_o[(c - 16) * 128:(c - 15) * 128, :], o[:], [bo])
                            else:
                                which = 0 if kind == "q" else 1
                                s4t, bs4 = s4[oi % 4]
                                sq, bsq = sqs[oi % 3]
                                rt, brt = rts[oi % 3]
                                self.A(lambda: nc.scalar.copy(out=o[:], in_=pa), [pb], [bo])
                                for hh in range(4):
                                    self.A(lambda hh=hh: nc.scalar.activation(out=sq[:, hh * 128:(hh + 1) * 128],
                                                                              in_=o[:, hh * 128:(hh + 1) * 128], func=AF.Square,
                                                                              accum_out=s4t[:, hh:hh + 1]), [bo], [bsq, bs4], so=(hh > 0))
                                self.rstd(st, s4t[:], bs4, 1.0 / 128)
                                for hh in range(4):
                                    self.V(lambda hh=hh: nc.vector.scalar_tensor_tensor(out=o[:, hh * 128:(hh + 1) * 128],
                                                                                        in0=o[:, hh * 128:(hh + 1) * 128],
                                                                                        scalar=s4t[:, hh:hh + 1], in1=qkn[:, which, :],
                                                                                        op0=ALU.mult, op1=ALU.mult),
                                           [bo, bs4, bqkn], [bo], so=(hh > 0))
                                if kind == "k" and c >= 16:
                                    self.stq(self.nk_o[(c - 16) * 128:(c - 15) * 128, :], o[:], [bo])
                                if c < 16:
                                    cst, bcs = cs[oi % 3]
                                    self.ld(cst[:, 0, :], self.cos_d[rows, :], [bcs])
                                    self.ld(cst[:, 1, :], self.sin_d[rows, :], [bcs])
                                    o3 = o[:].rearrange("p (h d) -> p h d", h=4)
                                    o5 = o[:].rearrange("p (h a two i) -> p h a two i", h=4, a=2, two=2)
                                    r5 = rt[:].rearrange("p (h a two i) -> p h a two i", h=4, a=2, two=2)
                                    s4v = cst[:, 1, :].rearrange("p (a two i) -> p a two i", a=2, two=2)
                                    sin1 = s4v[:, :, 0, :].unsqueeze(1).to_broadcast([128, 4, 2, 32])
                                    sin2 = s4v[:, :, 1, :].unsqueeze(1).to_broadcast([128, 4, 2, 32])
                                    self.G(lambda: nc.gpsimd.tensor_tensor(out=r5[:, :, :, 0, :], in0=o5[:, :, :, 1, :], in1=sin1,
                                                                           op=ALU.mult), [bo, bcs], [brt])
                                    self.G(lambda: nc.gpsimd.tensor_tensor(out=r5[:, :, :, 1, :], in0=o5[:, :, :, 0, :], in1=sin2,
                                                                           op=ALU.mult), [bo, bcs], [brt])
                                    self.V(lambda: nc.vector.tensor_tensor(out=sq[:].rearrange("p (h d) -> p h d", h=4), in0=o3,
                                                                           in1=cst[:, 0:1, :].to_broadcast([128, 4, 128]),
                                                                           op=ALU.mult), [bo, bcs], [bsq])
                                    self.V(lambda: nc.vector.tensor_tensor(out=oh[:], in0=sq[:], in1=rt[:], op=ALU.add),
                                           [bsq, brt], [boh])
                                else:
                                    self.V(lambda: nc.vector.tensor_copy(out=oh[:], in_=o[:]), [bo], [boh])
                                if kind == "q":
                                    self.stq(self.Qs[rows, col0:col0 + 512], oh[:], [boh])
                                else:
                                    self.stq(self.Ks[rows, :], oh[:], [boh])
                    else:
                        for i in range(ncols // 128):
                            for tg in range(TCH // 4):
                                bi = self.nextbank()
                                pa, pb = self.ps(bi)
                                for kc in range(16):
                                    self.P(lambda kc=kc: nc.tensor.matmul(pa, w[:, kc, i * 128:(i + 1) * 128],
                                                                          hT[:, kc, tg * 512:(tg + 1) * 512],
                                                                          start=(kc == 0), stop=(kc == 15)),
                                           [bhT, bw], [pb], inc=(kc == 15))
                                oi += 1
                                oh, boh = obh[oi % NOB]
                                fn = {"u": AF.Gelu_apprx_tanh, "za": AF.Silu, "xbc": AF.Copy}[kind]
                                dst = {"u": self.UT, "za": self.ZAT, "xbc": self.XBCT}[kind]
                                cc = (col0 - {"u": 0, "za": 4096, "xbc": 8192}[kind]) // 128 + i
                                self.A(lambda: nc.scalar.activation(out=oh[:], in_=pa, func=fn), [pb], [boh])
                                t0 = c0 * 128 + tg * 512
                                self.stq(dst[cc, :, t0:t0 + 512], oh[:], [boh])
            self.cx.barrier()

    def stage_l0_front(self):
        nc = self.nc
        with ExitStack() as st:
            T = lambda shape, dt, nm: self.tile(st, shape, dt, nm)
            wsT, bwsT = T([128, 16, 128], BF16, "wsT")
            bsr, bbsr = T([1, D], BF16, "bsr")
            vg, bvg = T([128, D], F32, "vg")
            cw, bcw = T([128, 32, 5], F32, "cw")
            cbias, bcbias = T([128, 32], F32, "cbias")
            diag, bdiag = T([128, 160, 128], BF16, "diag")
            self.ld(wsT[:], self.wsT_d[:, :, :], [bwsT])
            self.ld(bsr[:], self.bsrow_d[:, :], [bbsr])
            self.ld(vg[:], self.vgain[:, :], [bvg])
            self.ld(cw[:], self.convw_d[:, :, :], [bcw])
            self.ld(cbias[:], self.convb_d[:, :], [bcbias])
            for kc in range(32):
                for j in range(5):
                    self.V(lambda kc=kc, j=j: nc.vector.tensor_scalar(diag[:, kc * 5 + j, :], self.identb,
                                                                       cw[:, kc, j:j + 1], None, ALU.mult),
                           [self.bcb, bcw], [bdiag])
            ones1 = self.cb[0:1, 5, :]
            xin = [T([128, 32, 260], BF16, "xin") for _ in range(1)]
            xcT = [T([128, 32, 256], BF16, "xcT") for _ in range(1)]
            ut = [T([128, 16, 256], BF16, "ut") for _ in range(1)]
            zat = [T([128, 16, 256], BF16, "zat") for _ in range(1)]
            vt = [T([128, D], BF16, "vt") for _ in range(2)]
            vn, bvn = T([128, D], BF16, "vn")
            tmp, btmp = T([128, D], F32, "tmp")
            mixa = [T([128, 16, 256], BF16, "mixa") for _ in range(1)]
            xtk = [T([128, 3072], BF16, "xtk") for _ in range(2)]
            st2 = [T([128, 4], F32, "st2") for _ in range(2)]
            for blk in range(12):
                t0 = blk * 256
                if blk < 8:
                    s0, s1 = 0, 2048
                else:
                    s0 = t0
                    s1 = t0 + 256
                xi, bxi = xin[0]
                xc, bxc = xcT[0]
                lo = max(s0, t0 - 2)
                hi = min(s1, t0 + 258)
                if lo > t0 - 2:
                    self.V(lambda: nc.vector.memset(xi[:, :, 0:2], 0.0), [], [bxi])
                if hi < t0 + 258:
                    self.V(lambda: nc.vector.memset(xi[:, :, 258:260], 0.0), [], [bxi])
                self.ld(xi[:, :, lo - (t0 - 2):hi - (t0 - 2)], self.XBCT[:, :, lo:hi].rearrange("k p t -> p k t"), [bxi])
                u, bu = ut[0]
                za, bza = zat[0]
                self.ld(u[:], self.UT[:, :, t0:t0 + 256].rearrange("k p t -> p k t"), [bu])
                self.ld(za[:], self.ZAT[:, :, t0:t0 + 256].rearrange("k p t -> p k t"), [bza])
                for kc in range(32):
                    bi = self.nextbank(4, 8)
                    pa, pb = self.ps(bi)
                    for j in range(5):
                        self.P(lambda kc=kc, j=j: nc.tensor.matmul(pa[:, 0:256], diag[:, kc * 5 + j, :], xi[:, kc, j:j + 256],
                                                                   start=(j == 0), stop=(j == 4)),
                               [bdiag, bxi], [pb], inc=(j == 4))
                    self.A(lambda kc=kc: nc.scalar.activation(out=xc[:, kc, :], in_=pa[:, 0:256], func=AF.Silu,
                                                              bias=cbias[:, kc:kc + 1]), [pb, bcbias], [bxc])
                self.stq(self.BCT[:, :, t0:t0 + 256].rearrange("k p t -> p k t"), xc[:, 16:32, :], [bxc])
                mx, bmx = mixa[0]
                for ch in range(2):
                    c = blk * 2 + ch
                    rows = slice(c * 128, (c + 1) * 128)
                    xk, bxk = xtk[c % 2]
                    for grp in range(3):
                        bi = self.nextbank(4, 8)
                        pa, pb = self.ps(bi)
                        pab = pa.bitcast(BF16)
                        for q in range(8):
                            kc = grp * 8 + q
                            self.P(lambda kc=kc, q=q: nc.tensor.transpose(pab[:, q * 128:(q + 1) * 128],
                                                                           xc[:, kc, ch * 128:(ch + 1) * 128], self.identb),
                                   [bxc, self.bcb], [pb], inc=(q == 7))
                        if grp == 1:
                            self.A(lambda: nc.scalar.copy(out=xk[:, grp * 1024:(grp + 1) * 1024], in_=pab), [pb], [bxk])
                        else:
                            self.V(lambda: nc.vector.tensor_copy(out=xk[:, grp * 1024:(grp + 1) * 1024], in_=pab), [pb], [bxk])
                    self.stq(self.XSB[rows, :], xk[:], [bxk])
                    v, bv = vt[c % 2]
                    s2, bs2 = st2[c % 2]
                    self.ld(v[:], self.Vs[rows, :], [bv])
                    self.A(lambda: nc.scalar.activation(out=tmp[:], in_=v[:], func=AF.Identity, accum_out=s2[:, 0:1]),
                           [bv], [btmp, bs2])
                    self.A(lambda: nc.scalar.activation(out=tmp[:], in_=v[:], func=AF.Square, accum_out=s2[:, 1:2]),
                           [bv], [btmp, bs2])
                    self.V(lambda: nc.vector.tensor_scalar(s2[:, 0:2], s2[:, 0:2], 1.0 / D, None, ALU.mult), [bs2], [bs2])
                    self.V(lambda: nc.vector.tensor_tensor(out=s2[:, 2:3], in0=s2[:, 0:1], in1=s2[:, 0:1], op=ALU.mult), [bs2], [bs2])
                    self.V(lambda: nc.vector.tensor_tensor(out=s2[:, 1:2], in0=s2[:, 1:2], in1=s2[:, 2:3], op=ALU.subtract), [bs2], [bs2])
                    self.rstd(st, s2[:, 1:2], bs2, 1.0)
                    self.V(lambda: nc.vector.scalar_tensor_tensor(out=s2[:, 3:4], in0=s2[:, 0:1], scalar=-1.0, in1=s2[:, 1:2],
                                                                  op0=ALU.mult, op1=ALU.mult), [bs2], [bs2])
                    self.A(lambda: nc.scalar.activation(out=tmp[:], in_=v[:], func=AF.Identity, scale=s2[:, 1:2],
                                                        bias=s2[:, 3:4]), [bv, bs2], [btmp])
                    self.V(lambda: nc.vector.tensor_tensor(out=vn[:], in0=tmp[:], in1=vg[:], op=ALU.mult), [btmp, bvg], [bvn])
                    for g in range(16):
                        bi = g // 4
                        pa, pb = self.ps(bi)
                        po = pa[:, (g % 4) * 128:(g % 4 + 1) * 128]
                        self.P(lambda g=g: nc.tensor.matmul(po, vn[:, g * 128:(g + 1) * 128], wsT[:, g, :], start=True, stop=False),
                               [bvn, bwsT], [pb], inc=False)
                        self.P(lambda g=g: nc.tensor.matmul(po, ones1, bsr[0:1, g * 128:(g + 1) * 128], start=False, stop=True),
                               [self.bcb, bbsr], [pb], inc=(g % 4 == 3))
                    pbs = [self.bbuf[i] for i in range(4)]
                    self.V(lambda: nc.vector.tensor_tensor(out=tmp[:].rearrange("p (g t) -> p g t", g=16),
                                                           in0=self.psA[:, :].rearrange("p (g t) -> p g t", g=16),
                                                           in1=u[:, :, ch * 128:(ch + 1) * 128], op=ALU.mult),
                           pbs + [bu], [btmp])
                    self.V(lambda: nc.vector.tensor_tensor(out=mx[:, :, ch * 128:(ch + 1) * 128],
                                                           in0=tmp[:].rearrange("p (g t) -> p g t", g=16),
                                                           in1=za[:, :, ch * 128:(ch + 1) * 128], op=ALU.mult),
                           [btmp, bza], [bmx])
                self.stq(self.MIXT[0:16, :, t0:t0 + 256].rearrange("k p t -> p k t"), mx[:], [bmx])
            self.cx.barrier()

    def load_state(self, H, bH, src, tmpf, btmpf):
        nc = self.nc
        self.ld(tmpf[:].rearrange("p (k n) -> p k n", k=16), src.rearrange("(k p) n -> p k n", p=128), [btmpf])
        for grp in range(4):
            bi = self.nextbank(0, 4)
            pa, pb = self.ps(bi)
            for q in range(4):
                k = grp * 4 + q
                self.P(lambda k=k, q=q: nc.tensor.transpose(pa[:, q * 128:(q + 1) * 128], tmpf[:, k * 128:(k + 1) * 128],
                                                             self.identf), [btmpf, self.bcf], [pb], inc=(q == 3))
            self.A(lambda: nc.scalar.copy(out=H[:, grp * 512:(grp + 1) * 512], in_=pa), [pb], [bH])

    def store_state(self, H, bH, dst, tmpf, btmpf):
        nc = self.nc
        for grp in range(4):
            bi = self.nextbank(0, 4)
            pa, pb = self.ps(bi)
            for q in range(4):
                k = grp * 4 + q
                self.P(lambda k=k, q=q: nc.tensor.transpose(pa[:, q * 128:(q + 1) * 128], H[:, k * 128:(k + 1) * 128],
                                                             self.identf), [bH, self.bcf], [pb], inc=(q == 3))
            self.A(lambda: nc.scalar.copy(out=tmpf[:, grp * 512:(grp + 1) * 512], in_=pa), [pb], [btmpf])
        self.stq(dst.rearrange("(k p) n -> p k n", p=128), tmpf[:].rearrange("p (k n) -> p k n", k=16), [btmpf])

    def state_update(self, H, bH, xw, bxw, xk, bxk, dec, be):
        nc = self.nc
        for g in range(8):
            pa, pb = self.ps(g // 2)
            self.P(lambda g=g: nc.tensor.matmul(pa[:, (g % 2) * 256:(g % 2 + 1) * 256],
                                                xk[:, 2048 + g * 128:2048 + (g + 1) * 128], xw[:, g * 256:(g + 1) * 256],
                                                start=True, stop=True), [bxk, bxw], [pb], inc=(g % 2 == 1))
        pbs = [self.bbuf[i] for i in range(4)]
        H3 = H[:].rearrange("p (h q) -> p h q", h=32)
        self.G(lambda: nc.gpsimd.tensor_tensor(out=H3, in0=H3, in1=dec.unsqueeze(2).to_broadcast([128, 32, 64]),
                                               op=ALU.mult), [bH, be], [bH])
        self.V(lambda: nc.vector.tensor_tensor(out=H[:], in0=H[:], in1=self.psA[:, :], op=ALU.add), [bH] + pbs, [bH])

    def stage_l0_ssd(self):
        nc = self.nc
        with ExitStack() as st:
            T = lambda shape, dt, nm: self.tile(st, shape, dt, nm)
            DT_, bDT = T([128, 24, 64], F32, "DTall")
            DA_, bDA = T([128, 24, 64], F32, "DAall")
            E_, bE = T([128, 24, 128], F32, "Eall")
            CU_, bCU = T([128, 24, 64], F32, "CUall")
            with ExitStack() as s2:
                T2 = lambda shape, dt, nm: self.tile(s2, shape, dt, nm)
                dtb, bdtb = T2([128, 64], F32, "dtb")
                a, ba = T2([128, 64], F32, "a")
                tl = [T2([128, 1536], F32, "tl") for _ in range(3)]
                self.ld(dtb[:], self.dtb_d[:, :], [bdtb])
                self.ld(a[:], self.alog_d[:, :], [ba])
                self.ld(DT_[:], self.DT.rearrange("(c p) k -> p c k", p=128), [bDT])
                self.A(lambda: nc.scalar.activation(out=a[:], in_=a[:], func=AF.Exp), [ba], [ba])
                self.V(lambda: nc.vector.tensor_scalar(a[:], a[:], -1.0, None, ALU.mult), [ba], [ba])
                self.V(lambda: nc.vector.tensor_tensor(out=DT_[:], in0=DT_[:], in1=dtb[:].unsqueeze(1).to_broadcast([128, 24, 64]),
                                                       op=ALU.add), [bDT, bdtb], [bDT])
                self.V(lambda: nc.vector.tensor_scalar(DT_[:], DT_[:], 40.0, None, ALU.min), [bDT], [bDT])
                self.A(lambda: nc.scalar.activation(out=DT_[:], in_=DT_[:], func=AF.Exp), [bDT], [bDT])
                self.A(lambda: nc.scalar.activation(out=DT_[:], in_=DT_[:], func=AF.Ln, bias=1.0), [bDT], [bDT])
                self.V(lambda: nc.vector.tensor_tensor(out=DA_[:], in0=DT_[:], in1=a[:].unsqueeze(1).to_broadcast([128, 24, 64]),
                                                       op=ALU.mult), [bDT, ba], [bDA])
                flat = DA_[:].rearrange("p c k -> p (c k)")
                for mi, (t_, bt_) in zip((1, 2, 5), tl):
                    for q in range(3):
                        pa, pb = self.ps(q)
                        self.P(lambda: nc.tensor.matmul(pa, self.cf[:, mi, :], flat[:, q * 512:(q + 1) * 512], start=True, stop=True),
                               [self.bcf, bDA], [pb])
                        self.A(lambda: nc.scalar.copy(out=t_[:, q * 512:(q + 1) * 512], in_=pa), [pb], [bt_])
                v3 = lambda t_: t_[:].rearrange("p (c k) -> p c k", c=24)
                self.V(lambda: nc.vector.tensor_copy(out=E_[:, :, 0:32], in_=v3(tl[0][0])[:, :, 0:32]), [tl[0][1]], [bE])
                self.V(lambda: nc.vector.tensor_copy(out=E_[:, :, 32:64], in_=v3(tl[1][0])[:, :, 32:64]), [tl[1][1]], [bE])
                self.V(lambda: nc.vector.tensor_copy(out=E_[:, :, 64:128], in_=v3(tl[2][0])), [tl[2][1]], [bE])
                self.V(lambda: nc.vector.tensor_tensor(out=CU_[:], in0=E_[:, :, 64:128], in1=E_[:, :, 0:64],
                                                       op=ALU.subtract), [bE], [bCU])
                self.A(lambda: nc.scalar.activation(out=CU_[:], in_=CU_[:], func=AF.Exp), [bCU], [bCU])
                self.V(lambda: nc.vector.tensor_tensor(out=CU_[:], in0=CU_[:], in1=DT_[:], op=ALU.mult),
                       [bCU, bDT], [bCU])
                self.A(lambda: nc.scalar.activation(out=E_[:], in_=E_[:], func=AF.Exp), [bE], [bE])
                self.cx.barrier()
            cst4 = (DT_, bDT, DA_, bDA, E_, bE, CU_, bCU)
            self.sweep_bwd(cst4)
            self.cx.barrier()
            self.sweep_fwd(cst4)
            self.cx.barrier()

    def sweep_bwd(self, cst4):
        nc = self.nc
        DT_, bDT, DA_, bDA, E_, bE, CU_, bCU = cst4
        with ExitStack() as st:
            T = lambda shape, dt, nm: self.tile(st, shape, dt, nm)
            H, bH = T([128, D], F32, "HB")
            tmpf, btmpf = T([128, D], F32, "tmpf")
            xtk = [T([128, 3072], BF16, "xtk") for _ in range(2)]
            xw = [T([128, D], BF16, "xw") for _ in range(2)]
            snap = [T([128, D], BF16, "snap") for _ in range(2)]
            seqs = [(0, 16, None)] + [(16 + 2 * p, 2, p) for p in range(4)]
            order = []
            for (c0, n, p) in seqs:
                for c in range(c0 + n - 1, c0 - 1, -1):
                    order.append((c, c == c0 + n - 1, c == c0, p))

            def loads(i):
                c = order[i][0]
                xk, bxk = xtk[i % 2]
                self.stq(xk[:], self.XSB[c * 128:(c + 1) * 128, :], writes=[bxk])
            loads(0)
            for i, (c, first, last, p) in enumerate(order):
                if i + 1 < len(order):
                    loads(i + 1)
                if first:
                    if p is None:
                        self.load_state(H, bH, self.sb_d, tmpf, btmpf)
                    else:
                        self.V(lambda: nc.vector.memset(H[:], 0.0), [], [bH])
                k = i % 2
                xk, bxk = xtk[k]
                sn, bsn = snap[k]
                self.A(lambda: nc.scalar.copy(out=sn[:], in_=H[:]), [bH], [bsn])
                self.stq(self.HBS[c, :, :], sn[:], [bsn])
                w_, bw_ = xw[k]
                self.G(lambda: nc.gpsimd.tensor_tensor(out=w_[:].rearrange("p (h q) -> p h q", h=32),
                                                       in0=xk[:, 0:D].rearrange("p (h q) -> p h q", h=32),
                                                       in1=CU_[:, c, 32:64].unsqueeze(2).to_broadcast([128, 32, 64]),
                                                       op=ALU.mult), [bxk, bCU], [bw_])
                self.state_update(H, bH, w_, bw_, xk, bxk, E_[:, c, 96:128], bE)
                if last and p is not None:
                    self.store_state(H, bH, self.nb_o[p, :, :], tmpf, btmpf)

    def sweep_fwd(self, cst4):
        nc = self.nc
        DT_, bDT, DA_, bDA, E_, bE, CU_, bCU = cst4
        with ExitStack() as st:
            T = lambda shape, dt, nm: self.tile(st, shape, dt, nm)
            H, bH = T([128, D], F32, "HF")
            tmpf, btmpf = T([128, D], F32, "tmpf")
            tmpg, btmpg = T([128, D], F32, "tmpg")
            yacc, byacc = T([128, D], F32, "yacc")
            Dbc, bDbc = T([128, D], F32, "Dbc")
            sn_, bsn_ = T([128, D], F32, "ssmn")
            self.ld(Dbc[:], self.dsk_d[:, 0, :], [bDbc])
            self.ld(tmpf[:], self.dsk_d[:, 1, :], [btmpf])
            self.ld(sn_[:], self.ssmn_d[:, :], [bsn_])
            self.V(lambda: nc.vector.tensor_tensor(out=Dbc[:], in0=Dbc[:], in1=tmpf[:], op=ALU.add), [bDbc, btmpf], [bDbc])
            xtk = [T([128, 3072], BF16, "xtk") for _ in range(3)]
            zb = [T([128, D], BF16, "zb") for _ in range(2)]
            hbb = [T([128, D], BF16, "hbb") for _ in range(2)]
            bct = [T([128, 16, 128], BF16, "bct") for _ in range(3)]
            hfb, bhfb = T([128, D], BF16, "hfb")
            R = [T([128, 32, 128], BF16, "R") for _ in range(1)]
            MmP = [[T([128, 32, 128], BF16, "Mm") for _ in range(2)] for _ in range(2)]
            CB = [T([128, 8, 128], BF16, "CB") for _ in range(2)]
            LT = [T([128, 512], BF16, "LT") for _ in range(2)]
            xdt = [T([128, D], BF16, "xdt") for _ in range(2)]
            xw_, bxw_ = T([128, D], BF16, "xwf")
            ybf, bybf = T([128, D], BF16, "ybf")
            ybT = [T([128, 16, 128], BF16, "ybT") for _ in range(1)]
            s8, bs8 = T([128, 8], F32, "s8")
            LEb = self.cb[:, 1, :]
            GEb = self.cb[:, 2, :]
            GTb = self.cb[:, 3, :]
            LTb = self.cb[:, 4, :]
            seqs = [(0, 16, None)] + [(16 + 2 * p, 2, p) for p in range(4)]
            order = []
            for (c0, n, p) in seqs:
                for c in range(c0, c0 + n):
                    order.append((c, c == c0, c == c0 + n - 1, p))

            def loads(i):
                c = order[i][0]
                k = i % 2
                rows = slice(c * 128, (c + 1) * 128)
                k3 = i % 3
                self.stq(bct[k3][0][:], self.BCT[:, :, c * 128:(c + 1) * 128].rearrange("k p t -> p k t"), writes=[bct[k3][1]])
                self.stq(xtk[k3][0][:], self.XSB[rows, :], writes=[xtk[k3][1]])

            def loadsB(i):
                c = order[i][0]
                k = i % 2
                rows = slice(c * 128, (c + 1) * 128)
                self.stq(zb[k][0][:], self.ZB[rows, :], writes=[zb[k][1]])
                self.stq(hbb[k][0][:], self.HBS[c, :, :], writes=[hbb[k][1]])
            def front(i):
                c = order[i][0]
                k = i % 2
                bc, bbc = bct[i % 3]
                Mm = MmP[k]
                pbs45 = [self.bbuf[4], self.bbuf[5]]
                for g in range(8):
                    pa, pb = self.ps(4 + g // 4)
                    self.P(lambda g=g: nc.tensor.matmul(pa[:, (g % 4) * 128:(g % 4 + 1) * 128], bc[:, g, :], bc[:, 8 + g, :],
                                                        start=True, stop=True), [bbc], [pb], inc=(g % 4 == 3))
                cbps = self.psB[:, 0:1024].rearrange("p (g l) -> p g l", g=8)
                self.V(lambda: nc.vector.tensor_tensor(out=CB[0][0][:], in0=cbps,
                                                       in1=LEb.unsqueeze(1).to_broadcast([128, 8, 128]), op=ALU.mult),
                       pbs45 + [self.bcb], [CB[0][1]])
                self.V(lambda: nc.vector.tensor_tensor(out=CB[1][0][:], in0=cbps,
                                                       in1=GEb.unsqueeze(1).to_broadcast([128, 8, 128]), op=ALU.mult),
                       pbs45 + [self.bcb], [CB[1][1]])
                for d in range(2):
                    Rt, bR = R[0]
                    Mt, bMm = Mm[d]
                    msk = LEb if d == 0 else GEb
                    lm = GTb if d == 0 else LTb
                    for h in range(32):
                        if d == 0:
                            self.V(lambda h=h: nc.vector.tensor_scalar(Rt[:, h, :], msk, DA_[:, c, d * 32 + h:d * 32 + h + 1], None,
                                                                        ALU.mult), [self.bcb, bDA], [bR], so=(h > 0))
                        else:
                            self.A(lambda h=h: nc.scalar.activation(out=Rt[:, h, :], in_=msk, func=AF.Copy,
                                                                    scale=DA_[:, c, d * 32 + h:d * 32 + h + 1]),
                                   [self.bcb, bDA], [bR], so=(h > 0))
                    for q in range(8):
                        bi = 6 + (q % 2)
                        pa, pb = self.ps(bi)
                        self.P(lambda q=q: nc.tensor.matmul(pa, lm, Rt[:, 4 * q:4 * q + 4, :].rearrange("p h l -> p (h l)"),
                                                            start=True, stop=True), [self.bcb, bR], [pb])
                        lt, blt = LT[q % 2]
                        self.A(lambda: nc.scalar.activation(out=lt[:], in_=pa, func=AF.Exp), [pb], [blt])
                        self.V(lambda q=q: nc.vector.tensor_tensor(out=Mt[:, 4 * q:4 * q + 4, :],
                                                                   in0=lt[:].rearrange("p (h l) -> p h l", h=4),
                                                                   in1=CB[d][0][:, q:q + 1, :].to_broadcast([128, 4, 128]),
                                                                   op=ALU.mult), [blt, CB[d][1]], [bMm], so=(q > 0))
            loads(0)
            loads(1)
            loadsB(0)
            front(0)
            for i, (c, first, last, p) in enumerate(order):
                if i + 2 < len(order):
                    loads(i + 2)
                if i + 1 < len(order):
                    loadsB(i + 1)
                    front(i + 1)
                Mm = MmP[i % 2]
                if first:
                    if p is None:
                        self.load_state(H, bH, self.sf_d, tmpf, btmpf)
                    else:
                        self.V(lambda: nc.vector.memset(H[:], 0.0), [], [bH])
                k = i % 2
                xk, bxk = xtk[i % 3]
                z, bz = zb[k]
                hb_, bhb_ = hbb[k]
                bc, bbc = bct[i % 3]
                x3 = xk[:, 0:D].rearrange("p (h q) -> p h q", h=32)
                self.A(lambda: nc.scalar.copy(out=hfb[:], in_=H[:]), [bH], [bhfb])
                for d in range(2):
                    xd, bxd = xdt[d]
                    self.G(lambda d=d: nc.gpsimd.tensor_tensor(out=xd[:].rearrange("p (h q) -> p h q", h=32), in0=x3,
                                                               in1=DT_[:, c, d * 32:(d + 1) * 32].unsqueeze(2).to_broadcast([128, 32, 64]),
                                                               op=ALU.mult), [bxk, bDT], [bxd])
                self.G(lambda: nc.gpsimd.tensor_tensor(out=tmpg[:], in0=xk[:, 0:D], in1=Dbc[:], op=ALU.mult), [bxk, bDbc], [btmpg])
                self.G(lambda: nc.gpsimd.tensor_tensor(out=xw_[:].rearrange("p (h q) -> p h q", h=32), in0=x3,
                                                       in1=CU_[:, c, 0:32].unsqueeze(2).to_broadcast([128, 32, 64]),
                                                       op=ALU.mult), [bxk, bCU], [bxw_])
                for h in range(32):
                    pa, pb = self.ps(h // 8)
                    po = pa[:, (h % 8) * 64:(h % 8 + 1) * 64]
                    self.P(lambda h=h: nc.tensor.matmul(po, Mm[0][0][:, h, :], xdt[0][0][:, h * 64:(h + 1) * 64],
                                                        start=True, stop=False), [Mm[0][1], xdt[0][1]], [pb], inc=False)
                    self.P(lambda h=h: nc.tensor.matmul(po, Mm[1][0][:, h, :], xdt[1][0][:, h * 64:(h + 1) * 64],
                                                        start=False, stop=True), [Mm[1][1], xdt[1][1]], [pb],
                           inc=(h % 8 == 7))
                pbsA = [self.bbuf[i] for i in range(4)]
                self.V(lambda: nc.vector.tensor_tensor(out=yacc[:], in0=tmpg[:], in1=self.psA[:, :], op=ALU.add),
                       [btmpg] + pbsA, [byacc])
                for d in range(2):
                    Hs, bHs = (hfb, bhfb) if d == 0 else (hb_, bhb_)
                    tt, btt = (tmpf, btmpf) if d == 0 else (tmpg, btmpg)
                    for g in range(8):
                        pa, pb = self.ps(g // 2)
                        self.P(lambda g=g: nc.tensor.matmul(pa[:, (g % 2) * 256:(g % 2 + 1) * 256], bc[:, 8 + g, :],
                                                            Hs[:, g * 256:(g + 1) * 256], start=True, stop=True),
                               [bbc, bHs], [pb], inc=(g % 2 == 1))
                    self.V(lambda d=d: nc.vector.tensor_tensor(out=tt[:].rearrange("p (h q) -> p h q", h=32),
                                                               in0=self.psA[:, :].rearrange("p (h q) -> p h q", h=32),
                                                               in1=E_[:, c, d * 32:(d + 1) * 32].unsqueeze(2).to_broadcast([128, 32, 64]),
                                                               op=ALU.mult), pbsA + [bE], [btt])
                    self.V(lambda: nc.vector.tensor_tensor(out=yacc[:], in0=yacc[:], in1=tt[:], op=ALU.add),
                           [byacc, btt], [byacc])
                self.state_update(H, bH, xw_, bxw_, xk, bxk, E_[:, c, 64:96], bE)
                self.V(lambda: nc.vector.tensor_tensor(out=yacc[:], in0=yacc[:], in1=z[:], op=ALU.mult), [byacc, bz], [byacc])
                for g in range(8):
                    self.A(lambda g=g: nc.scalar.activation(out=ybf[:, g * 256:(g + 1) * 256], in_=yacc[:, g * 256:(g + 1) * 256],
                                                            func=AF.Square, accum_out=s8[:, g:g + 1]), [byacc], [bybf, bs8], so=(g > 0))
                self.rstd(st, s8[:], bs8, 1.0 / 256)
                for g in range(8):
                    self.V(lambda g=g: nc.vector.scalar_tensor_tensor(out=ybf[:, g * 256:(g + 1) * 256], in0=yacc[:, g * 256:(g + 1) * 256],
                                                                      scalar=s8[:, g:g + 1], in1=sn_[:, g * 256:(g + 1) * 256],
                                                                      op0=ALU.mult, op1=ALU.mult), [byacc, bs8, bsn_], [bybf], so=(g > 0))
                yT, byT = ybT[0]
                for half in range(2):
                    bi = 4 + half
                    pa, pb = self.ps(bi)
                    pab = pa.bitcast(BF16)
                    for q in range(8):
                        kc = half * 8 + q
                        self.P(lambda kc=kc, q=q: nc.tensor.transpose(pab[:, q * 128:(q + 1) * 128],
                                                                       ybf[:, kc * 128:(kc + 1) * 128], self.identb),
                               [bybf, self.bcb], [pb], inc=(q == 7))
                    self.A(lambda: nc.scalar.copy(out=yT[:, half * 8:(half + 1) * 8, :],
                                                  in_=pab.rearrange("p (q t) -> p q t", q=8)), [pb], [byT])
                self.stq(self.MIXT[16:32, :, c * 128:(c + 1) * 128].rearrange("k p t -> p k t"), yT[:], [byT])
                if last and p is not None:
                    self.store_state(H, bH, self.nf_o[p, :, :], tmpf, btmpf)

    def stage_out(self, layer):
        nc = self.nc
        KC = 32 if layer == 0 else 16
        W = self.w_out0 if layer == 0 else self.w_out1
        xsrc = self.xtok if layer == 0 else self.X1
        dst = self.X1 if layer == 0 else self.y_o
        Wv = W.rearrange("(kc p) n -> p kc n", p=128)
        TCH = 6
        SW = 256 if KC == 32 else 512
        with ExitStack() as st:
            T = lambda shape, dt, nm: self.tile(st, shape, dt, nm)
            mT, bmT = T([128, KC, TCH * 128], BF16, "mT")
            wsl = [T([128, KC, SW], BF16, "wo") for _ in range(2)]
            G2 = [T([128, D], F32, "G2") for _ in range(2)]
            for c in range(2):
                self.ld(G2[c][0][:], self.MS[c, :, 2 * D:3 * D], [G2[c][1]])
            oacc = [T([128, D], F32, "oacc") for _ in range(TCH)]
            xb = [T([128, D], F32, "xb") for _ in range(2)]
            junk, bjunk = T([128, D], BF16, "junk")
            ss = [T([128, 1], F32, "ss") for _ in range(2)]
            for pas in range(NCH // TCH):
                c0 = pas * TCH
                self.ld(mT[:], self.MIXT[0:KC, :, c0 * 128:(c0 + TCH) * 128].rearrange("k p t -> p k t"), [bmT])
                for s in range(D // SW):
                    w, bw = wsl[s % 2]
                    self.ld(w[:], Wv[:, :, s * SW:(s + 1) * SW], [bw])
                    for j in range(TCH):
                        bi = self.nextbank()
                        pa, pb = self.ps(bi)
                        pa = pa[:, 0:SW]
                        for kc in range(KC):
                            self.P(lambda kc=kc: nc.tensor.matmul(pa, mT[:, kc, j * 128:(j + 1) * 128], w[:, kc, :],
                                                                  start=(kc == 0), stop=(kc == KC - 1)),
                                   [bmT, bw], [pb], inc=(kc == KC - 1))
                        o, bo = oacc[j]
                        self.A(lambda: nc.scalar.copy(out=o[:, s * SW:(s + 1) * SW], in_=pa), [pb], [bo])
                for j in range(TCH):
                    c = c0 + j
                    cond = 0 if c < 16 else 1
                    Mt, bM = G2[cond]
                    o, bo = oacc[j]
                    x, bx = xb[j % 2]
                    s1, bs1 = ss[j % 2]
                    rows = slice(c * 128, (c + 1) * 128)
                    self.ld(x[:], xsrc[rows, :], [bx])
                    self.A(lambda: nc.scalar.activation(out=junk[:], in_=o[:], func=AF.Square, accum_out=s1[:]),
                           [bo], [bjunk, bs1])
                    self.rstd(st, s1[:], bs1, 1.0 / D)
                    self.V(lambda: nc.vector.scalar_tensor_tensor(out=o[:], in0=o[:], scalar=s1[:], in1=Mt[:],
                                                                  op0=ALU.mult, op1=ALU.mult), [bo, bs1, bM], [bo])
                    self.V(lambda: nc.vector.tensor_tensor(out=o[:], in0=o[:], in1=x[:], op=ALU.add), [bo, bx], [bo])
                    self.stq(dst[rows, :], o[:], [bo])
            self.cx.barrier()

    def stage_attn(self):
        nc = self.nc
        SCALE = 128 ** -0.5
        with ExitStack() as st:
            T = lambda shape, dt, nm: self.tile(st, shape, dt, nm)
            kT, bkT = T([128, 4, 2560], BF16, "kT")
            va, bva = T([128, 20, 4, 132], BF16, "va")
            kt = [T([128, 512], BF16, "kt") for _ in range(2)]
            qt = [T([128, D], BF16, "qt") for _ in range(2)]
            qT, bqT = T([128, 16, 512], BF16, "qT")
            zt = [T([128, D], BF16, "zt") for _ in range(4)]
            pt = [T([128, 512], BF16, "pt") for _ in range(3)]
            mix = [T([128, D], BF16, "mix") for _ in range(4)]
            mxT = [T([128, 16, 128], BF16, "mxT") for _ in range(2)]
            rc = [T([128, 1], F32, "rc") for _ in range(4)]
            of = [T([128, 128], F32, "of") for _ in range(2)]
            self.V(lambda: nc.vector.memset(va[:], 1.0), [], [bva])
            seqs = [(0, 16, None)] + [(16 + 2 * p, 2, p) for p in range(4)]
            it = 0
            for (c0, n, p) in seqs:
                kch = ([NT // 128 + r for r in range(4)] if p is None else []) + list(range(c0, c0 + n))
                nk = len(kch)
                for i, kc_ in enumerate(kch):
                    it += 1
                    t, bt = kt[it % 2]
                    rows = slice(kc_ * 128, (kc_ + 1) * 128)
                    self.ld(t[:], self.Ks[rows, :], [bt])
                    self.ld(va[:, i, :, 0:128], self.V1[rows, :].rearrange("t (h d) -> t h d", h=4), [bva])
                    bi = self.nextbank(6, 8)
                    pa, pb = self.ps(bi)
                    pab = pa.bitcast(BF16)
                    for h in range(4):
                        self.P(lambda h=h: nc.tensor.transpose(pab[:, h * 128:(h + 1) * 128], t[:, h * 128:(h + 1) * 128],
                                                               self.identb), [bt, self.bcb], [pb], inc=(h == 3))
                    self.V(lambda: nc.vector.tensor_copy(out=kT[:, :, i * 128:(i + 1) * 128],
                                                         in_=pab[:, 0:512].rearrange("p (h t) -> p h t", h=4)), [pb], [bkT])
                nqc = min(4, n)
                for qt0 in range(c0, c0 + n, nqc):
                    nq = nqc * 128
                    for j in range(nqc):
                        c = qt0 + j
                        it += 1
                        q_, bq_ = qt[it % 2]
                        rows = slice(c * 128, (c + 1) * 128)
                        self.ld(q_[:], self.Qs[rows, :], [bq_])
                        self.ld(zt[j][0][:], self.Z1[rows, :], [zt[j][1]])
                        for half in range(2):
                            bi = self.nextbank(6, 8)
                            pa, pb = self.ps(bi)
                            pab = pa.bitcast(BF16)
                            for q in range(8):
                                hh = half * 8 + q
                                self.P(lambda hh=hh, q=q: nc.tensor.transpose(pab[:, q * 128:(q + 1) * 128],
                                                                               q_[:, hh * 128:(hh + 1) * 128], self.identb),
                                       [bq_, self.bcb], [pb], inc=(q == 7))
                            self.V(lambda: nc.vector.tensor_copy(out=qT[:, half * 8:(half + 1) * 8, j * 128:(j + 1) * 128],
                                                                 in_=pab.rearrange("p (q t) -> p q t", q=8)), [pb], [bqT])
                    iters = [(h, i) for h in range(16) for i in range(nk)]

                    def qk(t):
                        h, i = iters[t]
                        pa, pb = self.ps(4 + t % 2)
                        self.P(lambda: nc.tensor.matmul(pa[:, 0:nq], kT[:, h // 4, i * 128:(i + 1) * 128], qT[:, h, 0:nq],
                                                        start=True, stop=True), [bkT, bqT], [pb])
                    qk(0)
                    for t, (h, i) in enumerate(iters):
                        kh = h // 4
                        if t + 1 < len(iters):
                            qk(t + 1)
                        pa, pb = self.ps(4 + t % 2)
                        p_, bp_ = pt[t % 3]
                        self.A(lambda: nc.scalar.activation(out=p_[:, 0:nq], in_=pa[:, 0:nq], func=AF.Exp, scale=SCALE),
                               [pb], [bp_])
                        for j in range(nqc):
                            po, pob = self.ps(j)
                            self.P(lambda j=j, i=i: nc.tensor.matmul(po[:, 0:129], p_[:, j * 128:(j + 1) * 128],
                                                                      va[:, i, kh, 0:129], start=(i == 0), stop=(i == nk - 1)),
                                   [bp_, bva], [pob], inc=(i == nk - 1))
                        if i == nk - 1:
                            for j in range(nqc):
                                po, pob = self.ps(j)
                                r_, br_ = rc[j]
                                self.V(lambda: nc.vector.reciprocal(r_[:], po[:, 128:129]), [pob], [br_])
                                self.V(lambda j=j: nc.vector.scalar_tensor_tensor(out=mix[j][0][:, h * 128:(h + 1) * 128],
                                                                                  in0=po[:, 0:128], scalar=r_[:],
                                                                                  in1=zt[j][0][:, h * 128:(h + 1) * 128],
                                                                                  op0=ALU.mult, op1=ALU.mult),
                                       [pob, br_, zt[j][1]], [mix[j][1]])
                    for j in range(nqc):
                        c = qt0 + j
                        it += 1
                        yT, byT = mxT[it % 2]
                        for half in range(2):
                            bi = self.nextbank(6, 8)
                            pa, pb = self.ps(bi)
                            pab = pa.bitcast(BF16)
                            for q in range(8):
                                kc = half * 8 + q
                                self.P(lambda kc=kc, q=q: nc.tensor.transpose(pab[:, q * 128:(q + 1) * 128],
                                                                               mix[j][0][:, kc * 128:(kc + 1) * 128], self.identb),
                                       [mix[j][1], self.bcb], [pb], inc=(q == 7))
                            self.V(lambda: nc.vector.tensor_copy(out=yT[:, half * 8:(half + 1) * 8, :],
                                                                 in_=pab.rearrange("p (q t) -> p q t", q=8)), [pb], [byT])
                        self.stq(self.MIXT[0:16, :, c * 128:(c + 1) * 128].rearrange("k p t -> p k t"), yT[:], [byT])
            self.cx.barrier()


def _consts():
    k = np.arange(128)[:, None]
    m = np.arange(128)[None, :]
    cst = np.stack([np.eye(128), k <= m, k >= m, k > m, k < m, np.ones((128, 128))], axis=1).astype(np.float32)
    n = 2048
    rows = n // 64
    t_row = np.repeat(np.arange(rows), 64).astype(np.float32)
    t_col = np.tile(np.arange(64), rows).astype(np.float32)
    half = 64
    inv = (10000.0 ** (-np.arange(0, half, 2, dtype=np.float32) / half)).astype(np.float32)
    ar = t_row[:, None] * inv[None, :]
    ac = t_col[:, None] * inv[None, :]
    cos = np.concatenate([np.cos(ar), np.cos(ar), np.cos(ac), np.cos(ac)], axis=1).astype(np.float32)
    sin = np.concatenate([-np.sin(ar), np.sin(ar), -np.sin(ac), np.sin(ac)], axis=1).astype(np.float32)
    return np.ascontiguousarray(cst), cos, sin


def _bc(v, n=128):
    v = np.asarray(v, np.float32).reshape(1, -1)
    return np.ascontiguousarray(np.broadcast_to(v, (n, v.shape[1])))


def prep_core(inp, i, shared):
    f = lambda a: np.ascontiguousarray(np.asarray(a, np.float32))
    m = dict(shared)
    xs = np.asarray(inp["x_sample"][i], np.float32)
    xp = np.asarray(inp["x_prompt"][4 * i:4 * i + 4], np.float32).reshape(1024, D)
    m["xtok"] = np.ascontiguousarray(np.concatenate([xs, xp], axis=0))
    c = np.asarray(inp["c"][i], np.float32).reshape(16, 128).T
    cc = np.asarray(inp["c_ctx"], np.float32).reshape(16, 128).T
    m["condT"] = np.ascontiguousarray(np.concatenate([c, cc], axis=1))
    m["sf"] = f(np.asarray(inp["state_l0_ssm_fwd"][i]).reshape(D, 128))
    m["sb"] = f(np.asarray(inp["state_l0_ssm_bwd"][i]).reshape(D, 128))
    m["ck"] = f(np.asarray(inp["cache_l1_k"][i]).reshape(512, 512))
    m["cv"] = f(np.asarray(inp["cache_l1_v"][i]).reshape(512, 512))
    return m


def prep_shared(inp):
    f = lambda a: np.ascontiguousarray(np.asarray(a, np.float32))
    cst, cos, sin = _consts()
    s = {}
    s["mod_w0"] = f(inp["mod_w0"])
    s["mod_w1"] = f(inp["mod_w1"])
    s["mod_b0"] = _bc(inp["mod_b0"])
    s["mod_b1"] = _bc(inp["mod_b1"])
    s["npre0"] = _bc(inp["norm_pre0"])
    s["npre1"] = _bc(inp["norm_pre1"])
    s["npost0"] = _bc(inp["norm_post0"])
    s["npost1"] = _bc(inp["norm_post1"])
    s["l0_w_in"] = f(inp["l0_w_in"])
    s["l0_w_out"] = f(inp["l0_w_out"])
    s["l1_w_in"] = f(inp["l1_w_in"])
    s["l1_w_out"] = f(inp["l1_w_out"])
    s["vgain"] = _bc(inp["l0_v_gain"])
    s["wsT"] = f(np.transpose(np.asarray(inp["l0_w_s"], np.float32), (2, 0, 1)))
    s["bsrow"] = f(np.asarray(inp["l0_b_s"], np.float32).reshape(1, D))
    cw = np.asarray(inp["l0_conv_w"], np.float32)
    s["convw"] = f(np.transpose(cw.reshape(5, 32, 128), (2, 1, 0)))
    s["convb"] = f(np.asarray(inp["l0_conv_b"], np.float32).reshape(32, 128).T)
    s["dtb"] = _bc(np.asarray(inp["l0_dt_bias"], np.float32).reshape(-1))
    s["alog"] = _bc(np.asarray(inp["l0_a_log"], np.float32).reshape(-1))
    dsk = np.repeat(np.asarray(inp["l0_d_skip"], np.float32)[:, :, None], 64, axis=2).reshape(2, D)
    s["dsk"] = f(np.broadcast_to(dsk[None], (128, 2, D)))
    s["ssmn"] = _bc(inp["l0_ssm_norm"])
    qk = np.stack([np.asarray(inp["l1_q_norm"], np.float32), np.asarray(inp["l1_k_norm"], np.float32)], axis=0)
    s["qkn"] = f(np.broadcast_to(qk[None], (128, 2, 128)))
    s["ropec"] = cos
    s["ropes"] = sin
    s["cst"] = cst
    return s


def assemble(res, ncores=8):
    yp = np.zeros((4 * ncores, 256, D), np.float32)
    ys = np.zeros((ncores, 2048, D), np.float32)
    nf = np.zeros((4 * ncores, 32, 64, 128), np.float32)
    nb = np.zeros((4 * ncores, 32, 64, 128), np.float32)
    nk = np.zeros((4 * ncores, 256, 4, 128), np.float32)
    nv = np.zeros((4 * ncores, 256, 4, 128), np.float32)
    for i in range(ncores):
        r = res[i]
        y = np.asarray(r["y"])
        ys[i] = y[0:2048]
        yp[4 * i:4 * i + 4] = y[2048:].reshape(4, 256, D)
        nf[4 * i:4 * i + 4] = np.asarray(r["nfwd"]).reshape(4, 32, 64, 128)
        nb[4 * i:4 * i + 4] = np.asarray(r["nbwd"]).reshape(4, 32, 64, 128)
        nk[4 * i:4 * i + 4] = np.asarray(r["nk"]).reshape(4, 256, 4, 128)
        nv[4 * i:4 * i + 4] = np.asarray(r["nv"]).reshape(4, 256, 4, 128)
    return yp, ys, nf, nb, nk, nv


def kernel(**inputs):
    shared = prep_shared(inputs)
    in_maps = [prep_core(inputs, i, shared) for i in range(8)]
    nc = K().build()
    res = run_bass_kernel_spmd(nc, in_maps, core_ids=list(range(8)))
    return assemble(res.results, 8)
```

```python
import numpy as np
from contextlib import ExitStack
import concourse.bass as bass
import concourse.mybir as mybir
from concourse.bass_utils import run_bass_kernel_spmd

F32 = mybir.dt.float32
BF16 = mybir.dt.bfloat16
AF = mybir.ActivationFunctionType
ALU = mybir.AluOpType
AX = mybir.AxisListType

NT = 3072
NCH = 24
D = 2048
EPS = 1e-6
L0_IN = 12352
SAME_ENG_SYNC = True


class Buf:
    __slots__ = ("name", "wr", "rds")

    def __init__(self, name=""):
        self.name = name
        self.wr = None
        self.rds = []


class Eng:
    def __init__(self, ctx, name, eng, ndma=0):
        self.name = name
        self.eng = eng
        self.sem = ctx.nc.alloc_semaphore("e_" + name)
        self.cnt = 0
        self.seen = {}
        self.pool = [[ctx.nc.alloc_semaphore("d_%s%d" % (name, i)), 0] for i in range(ndma)]
        self.pi = 0

    def wait(self, tok):
        sem, val = tok
        k = id(sem)
        if self.seen.get(k, 0) >= val:
            return
        self.eng.wait_ge(sem, val)
        self.seen[k] = val


class Ctx:
    def __init__(self, nc):
        self.nc = nc
        self.pe = Eng(self, "pe", nc.tensor)
        self.dve = Eng(self, "dve", nc.vector)
        self.act = Eng(self, "act", nc.scalar)
        self.pool = Eng(self, "pool", nc.gpsimd, ndma=16)
        self.sp = Eng(self, "sp", nc.sync, ndma=24)
        self.engs = [self.pe, self.dve, self.act, self.pool, self.sp]
        self.ninst = 0

    def _deps(self, e, reads, writes, so=False):
        deps = []
        for b in reads:
            if b.wr is not None:
                deps.append(b.wr)
        for b in writes:
            deps.extend(b.rds)
            if b.wr is not None:
                deps.append(b.wr)
        best = {}
        for sem, val in deps:
            k = id(sem)
            if k == id(e.sem) and (so or e is self.pe or not SAME_ENG_SYNC):
                continue
            if k not in best or best[k][1] < val:
                best[k] = (sem, val)
        for tok in best.values():
            e.wait(tok)

    def _record(self, tok, reads, writes):
        for b in writes:
            b.wr = tok
            b.rds = []
        for b in reads:
            if b not in writes:
                b.rds.append(tok)
                if len(b.rds) > 48:
                    best = {}
                    for s, v in b.rds:
                        if id(s) not in best or best[id(s)][1] < v:
                            best[id(s)] = (s, v)
                    b.rds = list(best.values())

    def op(self, e, fn, reads=(), writes=(), inc=True, so=False):
        self._deps(e, reads, writes, so)
        ins = fn()
        self.ninst += 1
        if inc:
            ins.then_inc(e.sem, 1)
            e.cnt += 1
            tok = (e.sem, e.cnt)
        else:
            tok = (e.sem, e.cnt + 1)
        self._record(tok, reads, writes)
        return tok

    def dma(self, e, out, in_, reads=(), writes=(), **kw):
        slot = e.pool[e.pi]
        e.pi = (e.pi + 1) % len(e.pool)
        if slot[1] > 0:
            e.wait((slot[0], 16 * slot[1]))
        self._deps(e, reads, writes)
        ins = e.eng.dma_start(out=out, in_=in_, **kw)
        ins.then_inc(slot[0], 16)
        slot[1] += 1
        self.ninst += 1
        tok = (slot[0], 16 * slot[1])
        self._record(tok, reads, writes)
        return tok

    def barrier(self):
        toks = []
        for e in self.engs:
            if e.cnt > 0:
                toks.append((e.sem, e.cnt))
            for s, c in e.pool:
                if c > 0:
                    toks.append((s, 16 * c))
        for e in self.engs:
            for t in toks:
                if id(t[0]) == id(e.sem):
                    continue
                e.wait(t)

    def finish(self):
        for e in self.engs:
            if e.cnt > 0:
                self.sp.wait((e.sem, e.cnt))
            for s, c in e.pool:
                if c > 0:
                    self.sp.wait((s, 16 * c))


class K:
    def __init__(self, upto=99, dbg=False):
        self.upto = upto
        self.dbg = dbg
        nc = self.nc = bass.Bass("TRN2", target_bir_lowering=False)
        self.cx = Ctx(nc)
        self.din = {}
        self.dout = {}
        self.psA = nc.alloc_psum_tensor("psA", [128, 2048], F32)
        self.psB = nc.alloc_psum_tensor("psB", [128, 2048], F32)
        self.bank = [(self.psA, i) for i in range(4)] + [(self.psB, i) for i in range(4)]
        self.bbuf = [Buf("bank%d" % i) for i in range(8)]
        self.bi = 0
        self.tid = 0

    def inp(self, name, shape, dt=F32):
        t = self.nc.dram_tensor(name, list(shape), dt, kind="ExternalInput").ap()
        self.din[name] = t
        return t

    def outp(self, name, shape, dt=F32):
        t = self.nc.dram_tensor(name, list(shape), dt, kind="ExternalOutput").ap()
        self.dout[name] = t
        return t

    def scr(self, name, shape, dt=BF16):
        if self.dbg:
            t = self.nc.dram_tensor(name, list(shape), dt, kind="ExternalOutput").ap()
            self.dout[name] = t
            return t
        return self.nc.dram_tensor(name, list(shape), dt).ap()

    def tile(self, st, shape, dt, name=None):
        self.tid += 1
        nm = "%s_%d" % (name or "t", self.tid)
        t = st.enter_context(self.nc.sbuf_tensor(nm, list(shape), dt))
        return t, Buf(nm)

    def ps(self, i):
        t, j = self.bank[i]
        return t[:, j * 512:(j + 1) * 512], self.bbuf[i]

    def nextbank(self, lo=0, hi=8):
        if self.bi < lo or self.bi >= hi:
            self.bi = lo
        i = self.bi
        self.bi += 1
        if self.bi >= hi:
            self.bi = lo
        return i

    def V(self, fn, reads=(), writes=(), so=False):
        return self.cx.op(self.cx.dve, fn, reads, writes, so=so)

    def A(self, fn, reads=(), writes=(), so=False):
        return self.cx.op(self.cx.act, fn, reads, writes, so=so)

    def G(self, fn, reads=(), writes=(), so=False):
        return self.cx.op(self.cx.pool, fn, reads, writes, so=so)

    def P(self, fn, reads=(), writes=(), inc=True):
        return self.cx.op(self.cx.pe, fn, reads, writes, inc=inc)

    def ld(self, out, in_, writes=(), reads=()):
        return self.cx.dma(self.cx.pool, out, in_, reads=reads, writes=writes)

    def stq(self, out, in_, reads=(), writes=()):
        return self.cx.dma(self.cx.sp, out, in_, reads=reads, writes=writes)

    def rstd(self, st, ss, bss, scale, n=1):
        nc = self.nc
        self.V(lambda: nc.vector.tensor_scalar(ss, ss, scale, EPS, ALU.mult, ALU.add), [bss], [bss])
        self.A(lambda: nc.scalar.activation(out=ss, in_=ss, func=AF.Ln), [bss], [bss])
        self.A(lambda: nc.scalar.activation(out=ss, in_=ss, func=AF.Exp, scale=-0.5), [bss], [bss])

    def build(self):
        nc = self.nc
        I = self.inp
        self.xtok = I("xtok", [NT, D])
        self.condT = I("condT", [128, 32])
        self.mod_w = [I("mod_w0", [D, 3 * D]), I("mod_w1", [D, 3 * D])]
        self.mod_b = [I("mod_b0", [128, 3 * D]), I("mod_b1", [128, 3 * D])]
        self.npre = [I("npre0", [128, D]), I("npre1", [128, D])]
        self.npost = [I("npost0", [128, D]), I("npost1", [128, D])]
        self.w_in0 = I("l0_w_in", [D, L0_IN])
        self.w_out0 = I("l0_w_out", [2 * D, D])
        self.w_in1 = I("l1_w_in", [D, 5120])
        self.w_out1 = I("l1_w_out", [D, D])
        self.vgain = I("vgain", [128, D])
        self.wsT_d = I("wsT", [128, 16, 128])
        self.bsrow_d = I("bsrow", [1, D])
        self.convw_d = I("convw", [128, 32, 5])
        self.convb_d = I("convb", [128, 32])
        self.dtb_d = I("dtb", [128, 64])
        self.alog_d = I("alog", [128, 64])
        self.dsk_d = I("dsk", [128, 2, D])
        self.ssmn_d = I("ssmn", [128, D])
        self.sf_d = I("sf", [D, 128])
        self.sb_d = I("sb", [D, 128])
        self.ck_d = I("ck", [512, 512])
        self.cv_d = I("cv", [512, 512])
        self.qkn_d = I("qkn", [128, 2, 128])
        self.cos_d = I("ropec", [2048, 128])
        self.sin_d = I("ropes", [2048, 128])
        self.cst_d = I("cst", [128, 6, 128])
        O = self.outp
        self.y_o = O("y", [NT, D])
        self.nf_o = O("nfwd", [4, D, 128])
        self.nb_o = O("nbwd", [4, D, 128])
        self.nk_o = O("nk", [1024, 512])
        self.nv_o = O("nv", [1024, 512])
        S = self.scr
        self.UT = S("UT", [16, 128, NT])
        self.ZAT = S("ZAT", [16, 128, NT])
        self.XBCT = S("XBCT", [32, 128, NT])
        self.Vs = S("Vs", [NT, D])
        self.ZB = S("ZB", [NT, D])
        self.DT = S("DTs", [NT, 64], F32)
        self.BCT = S("BCT", [16, 128, NT])
        self.XSB = S("XSB", [NT, 3072])
        self.HBS = S("HBS", [NCH, 128, D])
        self.MIXT = S("MIXT", [32, 128, NT])
        self.X1 = S("X1", [NT, D], F32)
        self.Qs = S("Qs", [NT, D])
        self.Ks = S("Ks", [NT + 512, 512])
        self.V1 = S("V1", [NT + 512, 512])
        self.Z1 = S("Z1", [NT, D])
        self.MS = S("MS", [2, 128, 3 * D], F32)

        with ExitStack() as g:
            self.cb, self.bcb = self.tile(g, [128, 6, 128], BF16, "cstb")
            self.cf, self.bcf = self.tile(g, [128, 6, 128], F32, "cstf")
            self.ld(self.cb[:], self.cst_d[:, :, :], [self.bcb])
            self.ld(self.cf[:], self.cst_d[:, :, :], [self.bcf])
            self.identb = self.cb[:, 0, :]
            self.identf = self.cf[:, 0, :]

            self.modulation(0)
            self.stage_in(0)
            if self.upto >= 2:
                self.stage_l0_front()
            if self.upto >= 4:
                self.stage_l0_ssd()
            if self.upto >= 5:
                self.stage_out(0)
            if self.upto >= 6:
                self.modulation(1)
                self.stage_in(1)
            if self.upto >= 7:
                self.stage_attn()
            if self.upto >= 8:
                self.stage_out(1)
            self.cx.finish()
        return nc

    def modulation(self, layer):
        nc = self.nc
        with ExitStack() as st:
            cond, bcond = self.tile(st, [128, 32], F32, "cond")
            lbc, blbc = self.tile(st, [128, 32, 128], BF16, "lbc")
            mb, bmb = self.tile(st, [128, 3 * D], F32, "mb")
            tA, btA = self.tile(st, [128, D], F32, "tA")
            tB, btB = self.tile(st, [128, D], F32, "tB")
            wsl = [self.tile(st, [128, 16, 512], BF16, "mw") for _ in range(2)]
            Ms = [self.tile(st, [128, 3 * D], F32, "M%d" % c) for c in range(2)]
            self.ld(cond[:], self.condT[:, :], [bcond])
            self.ld(mb[:], self.mod_b[layer][:, :], [bmb])
            self.ld(tA[:], self.npre[layer][:, :], [btA])
            self.ld(tB[:], self.npost[layer][:, :], [btB])
            self.A(lambda: nc.scalar.activation(out=cond[:], in_=cond[:], func=AF.Silu), [bcond], [bcond])
            self.V(lambda: nc.vector.tensor_copy(out=lbc[:], in_=cond[:].unsqueeze(2).to_broadcast([128, 32, 128])),
                   [bcond], [blbc])
            Wv = self.mod_w[layer].rearrange("(kc p) n -> p kc n", p=128)
            for s in range(12):
                w, bw = wsl[s % 2]
                self.ld(w[:], Wv[:, :, s * 512:(s + 1) * 512], [bw])
                for c in range(2):
                    bi = self.nextbank()
                    pa, pb = self.ps(bi)
                    for kc in range(16):
                        self.P(lambda kc=kc: nc.tensor.matmul(pa, lbc[:, c * 16 + kc, :], w[:, kc, :],
                                                              start=(kc == 0), stop=(kc == 15)),
                               [blbc, bw], [pb], inc=(kc == 15))
                    Mt, bM = Ms[c]
                    self.V(lambda: nc.vector.tensor_tensor(out=Mt[:, s * 512:(s + 1) * 512], in0=pa,
                                                           in1=mb[:, s * 512:(s + 1) * 512], op=ALU.add),
                           [pb, bmb], [bM])
            for c in range(2):
                Mt, bM = Ms[c]
                self.V(lambda: nc.vector.scalar_tensor_tensor(out=Mt[:, D:2 * D], in0=Mt[:, D:2 * D], scalar=1.0,
                                                              in1=tA[:], op0=ALU.add, op1=ALU.mult), [bM, btA], [bM])
                self.V(lambda: nc.vector.tensor_tensor(out=Mt[:, 2 * D:3 * D], in0=Mt[:, 2 * D:3 * D], in1=tB[:],
                                                       op=ALU.mult), [bM, btB], [bM])
                self.stq(self.MS[c, :, :], Mt[:], [bM])
            self.cx.barrier()

    def stage_in(self, layer):
        nc = self.nc
        xsrc = self.xtok if layer == 0 else self.X1
        W = self.w_in0 if layer == 0 else self.w_in1
        if layer == 0:
            slabs = []
            for i in range(4):
                slabs.append((i * 512, 512, "B", "u"))
            for i in range(4):
                slabs.append((2048 + i * 512, 512, "A", "v"))
            for i in range(4):
                slabs.append((4096 + i * 512, 512, "B", "za"))
            for i in range(4):
                slabs.append((6144 + i * 512, 512, "A", "zb"))
            for i in range(8):
                slabs.append((8192 + i * 512, 512, "B", "xbc"))
            slabs.append((12288, 64, "A", "dt"))
        else:
            slabs = []
            for i in range(4):
                slabs.append((i * 512, 512, "A", "q"))
            slabs.append((2048, 512, "A", "k"))
            slabs.append((2560, 512, "A", "vv"))
            for i in range(4):
                slabs.append((3072 + i * 512, 512, "A", "z"))
        Wv = W.rearrange("(kc p) n -> p kc n", p=128)
        TCH = 12
        with ExitStack() as st:
            hT, bhT = self.tile(st, [128, 16, TCH * 128], BF16, "hT")
            wsl = [self.tile(st, [128, 16, 512], BF16, "wsl") for _ in range(2)]
            xb = [self.tile(st, [128, D], F32, "xb") for _ in range(2)]
            hb = [self.tile(st, [128, D], BF16, "hb") for _ in range(2)]
            ss = [self.tile(st, [128, 1], F32, "ss") for _ in range(2)]
            NOB = 8
            ob = [self.tile(st, [128, 512], F32, "ob") for _ in range(NOB)]
            obh = [self.tile(st, [128, 512], BF16, "obh") for _ in range(NOB)]
            Mg = [self.tile(st, [128, 2 * D], F32, "Mg") for _ in range(2)]
            for c in range(2):
                self.ld(Mg[c][0][:], self.MS[c, :, 0:2 * D], [Mg[c][1]])
            if layer == 1:
                qkn, bqkn = self.tile(st, [128, 2, 128], F32, "qkn")
                self.ld(qkn[:], self.qkn_d[:, :, :], [bqkn])
                cs = [self.tile(st, [128, 2, 128], F32, "cs") for _ in range(3)]
                sqs = [self.tile(st, [128, 512], F32, "sq") for _ in range(3)]
                s4 = [self.tile(st, [128, 4], F32, "s4") for _ in range(4)]
                rts = [self.tile(st, [128, 512], F32, "rt") for _ in range(3)]
                for (src, dst) in ((self.ck_d, self.Ks), (self.cv_d, self.V1)):
                    for r in range(4):
                        t, bt = obh[r]
                        self.ld(t[:], src[r * 128:(r + 1) * 128, :], [bt])
                        self.stq(dst[NT + r * 128:NT + (r + 1) * 128, :], t[:], [bt])
            oi = 0
            for pas in range(2):
                c0 = pas * TCH
                for j in range(TCH):
                    c = c0 + j
                    cond = 0 if c < 16 else 1
                    Mt, bM = Mg[cond]
                    x, bx = xb[j % 2]
                    h, bh = hb[j % 2]
                    s1, bs1 = ss[j % 2]
                    self.ld(x[:], xsrc[c * 128:(c + 1) * 128, :], [bx])
                    self.A(lambda: nc.scalar.activation(out=h[:], in_=x[:], func=AF.Square, accum_out=s1[:]),
                           [bx], [bh, bs1])
                    self.rstd(st, s1[:], bs1, 1.0 / D)
                    self.V(lambda: nc.vector.scalar_tensor_tensor(out=x[:], in0=x[:], scalar=s1[:], in1=Mt[:, D:2 * D],
                                                                  op0=ALU.mult, op1=ALU.mult), [bx, bs1, bM], [bx])
                    self.V(lambda: nc.vector.tensor_tensor(out=h[:], in0=x[:], in1=Mt[:, 0:D], op=ALU.add),
                           [bx, bM], [bh])
                    for half in range(2):
                        bi = self.nextbank()
                        pa, pb = self.ps(bi)
                        pab = pa.bitcast(BF16)
                        for q in range(8):
                            kc = half * 8 + q
                            self.P(lambda kc=kc, q=q: nc.tensor.transpose(pab[:, q * 128:(q + 1) * 128],
                                                                           h[:, kc * 128:(kc + 1) * 128], self.identb),
                                   [bh, self.bcb], [pb], inc=(q == 7))
                        src = pab.rearrange("p (q t) -> p q t", q=8)
                        dst = hT[:, half * 8:(half + 1) * 8, j * 128:(j + 1) * 128]
                        if half == 0:
                            self.A(lambda: nc.scalar.copy(out=dst, in_=src), [pb], [bhT])
                        else:
                            self.V(lambda: nc.vector.tensor_copy(out=dst, in_=src), [pb], [bhT])
                for si, (col0, ncols, var, kind) in enumerate(slabs):
                    w, bw = wsl[si % 2]
                    self.ld(w[:, :, 0:ncols], Wv[:, :, col0:col0 + ncols], [bw])
                    if var == "A":
                        for j in range(TCH):
                            c = c0 + j
                            bi = self.nextbank()
                            pa, pb = self.ps(bi)
                            pa = pa[:, 0:ncols]
                            for kc in range(16):
                                self.P(lambda kc=kc: nc.tensor.matmul(pa, hT[:, kc, j * 128:(j + 1) * 128],
                                                                      w[:, kc, 0:ncols], start=(kc == 0), stop=(kc == 15)),
                                       [bhT, bw], [pb], inc=(kc == 15))
                            oi += 1
                            o, bo = ob[oi % NOB]
                            oh, boh = obh[oi % NOB]
                            rows = slice(c * 128, (c + 1) * 128)
                            if kind in ("v", "zb", "z"):
                                fn = AF.Gelu_apprx_tanh if kind == "v" else AF.Silu
                                dst = {"v": self.Vs, "zb": self.ZB, "z": self.Z1}[kind]
                                cc = col0 - {"v": 2048, "zb": 6144, "z": 3072}[kind]
                                self.A(lambda: nc.scalar.activation(out=oh[:], in_=pa, func=fn), [pb], [boh])
                                self.stq(dst[rows, cc:cc + 512], oh[:], [boh])
                            elif kind == "dt":
                                self.A(lambda: nc.scalar.copy(out=o[:, 0:64], in_=pa), [pb], [bo])
                                self.stq(self.DT[rows, :], o[:, 0:64], [bo])
                            elif kind == "vv":
                                self.A(lambda: nc.scalar.copy(out=o[:], in_=pa), [pb], [bo])
                                self.V(lambda: nc.vector.tensor_copy(out=oh[:], in_=o[:]), [bo], [boh])
                                self.stq(self.V1[rows, :], oh[:], [boh])
                                if c >= 16:
                                    self.stq(self.nv_o[(c - 16) * 128:(c - 15) * 128, :], o[:], [bo])
                            else:
                                which = 0 if kind == "q" else 1
                                s4t, bs4 = s4[oi % 4]
                                sq, bsq = sqs[oi % 3]
                                rt, brt = rts[oi % 3]
                                self.A(lambda: nc.scalar.copy(out=o[:], in_=pa), [pb], [bo])
                                for hh in range(4):
                                    self.A(lambda hh=hh: nc.scalar.activation(out=sq[:, hh * 128:(hh + 1) * 128],
                                                                              in_=o[:, hh * 128:(hh + 1) * 128], func=AF.Square,
                                                                              accum_out=s4t[:, hh:hh + 1]), [bo], [bsq, bs4], so=(hh > 0))
                                self.rstd(st, s4t[:], bs4, 1.0 / 128)
                                for hh in range(4):
                                    self.V(lambda hh=hh: nc.vector.scalar_tensor_tensor(out=o[:, hh * 128:(hh + 1) * 128],
                                                                                        in0=o[:, hh * 128:(hh + 1) * 128],
                                                                                        scalar=s4t[:, hh:hh + 1], in1=qkn[:, which, :],
                                                                                        op0=ALU.mult, op1=ALU.mult),
                                           [bo, bs4, bqkn], [bo], so=(hh > 0))
                                if kind == "k" and c >= 16:
                                    self.stq(self.nk_o[(c - 16) * 128:(c - 15) * 128, :], o[:], [bo])
                                if c < 16:
                                    cst, bcs = cs[oi % 3]
                                    self.ld(cst[:, 0, :], self.cos_d[rows, :], [bcs])
                                    self.ld(cst[:, 1, :], self.sin_d[rows, :], [bcs])
                                    o3 = o[:].rearrange("p (h d) -> p h d", h=4)
                                    o5 = o[:].rearrange("p (h a two i) -> p h a two i", h=4, a=2, two=2)
                                    r5 = rt[:].rearrange("p (h a two i) -> p h a two i", h=4, a=2, two=2)
                                    s4v = cst[:, 1, :].rearrange("p (a two i) -> p a two i", a=2, two=2)
                                    sin1 = s4v[:, :, 0, :].unsqueeze(1).to_broadcast([128, 4, 2, 32])
                                    sin2 = s4v[:, :, 1, :].unsqueeze(1).to_broadcast([128, 4, 2, 32])
                                    self.G(lambda: nc.gpsimd.tensor_tensor(out=r5[:, :, :, 0, :], in0=o5[:, :, :, 1, :], in1=sin1,
                                                                           op=ALU.mult), [bo, bcs], [brt])
                                    self.G(lambda: nc.gpsimd.tensor_tensor(out=r5[:, :, :, 1, :], in0=o5[:, :, :, 0, :], in1=sin2,
                                                                           op=ALU.mult), [bo, bcs], [brt])
                                    self.V(lambda: nc.vector.tensor_tensor(out=sq[:].rearrange("p (h d) -> p h d", h=4), in0=o3,
                                                                           in1=cst[:, 0:1, :].to_broadcast([128, 4, 128]),
                                                                           op=ALU.mult), [bo, bcs], [bsq])
                                    self.V(lambda: nc.vector.tensor_tensor(out=oh[:], in0=sq[:], in1=rt[:], op=ALU.add),
                                           [bsq, brt], [boh])
                                else:
                                    self.V(lambda: nc.vector.tensor_copy(out=oh[:], in_=o[:]), [bo], [boh])
                                if kind == "q":
                                    self.stq(self.Qs[rows, col0:col0 + 512], oh[:], [boh])
                                else:
                                    self.stq(self.Ks[rows, :], oh[:], [boh])
                    else:
                        for i in range(ncols // 128):
                            for tg in range(TCH // 4):
                                bi = self.nextbank()
                                pa, pb = self.ps(bi)
                                for kc in range(16):
                                    self.P(lambda kc=kc: nc.tensor.matmul(pa, w[:, kc, i * 128:(i + 1) * 128],
                                                                          hT[:, kc, tg * 512:(tg + 1) * 512],
                                                                          start=(kc == 0), stop=(kc == 15)),
                                           [bhT, bw], [pb], inc=(kc == 15))
                                oi += 1
                                oh, boh = obh[oi % NOB]
                                fn = {"u": AF.Gelu_apprx_tanh, "za": AF.Silu, "xbc": AF.Copy}[kind]
                                dst = {"u": self.UT, "za": self.ZAT, "xbc": self.XBCT}[kind]
                                cc = (col0 - {"u": 0, "za": 4096, "xbc": 8192}[kind]) // 128 + i
                                self.A(lambda: nc.scalar.activation(out=oh[:], in_=pa, func=fn), [pb], [boh])
                                t0 = c0 * 128 + tg * 512
                                self.stq(dst[cc, :, t0:t0 + 512], oh[:], [boh])
            self.cx.barrier()

    def stage_l0_front(self):
        nc = self.nc
        with ExitStack() as st:
            T = lambda shape, dt, nm: self.tile(st, shape, dt, nm)
            wsT, bwsT = T([128, 16, 128], BF16, "wsT")
            bsr, bbsr = T([1, D], BF16, "bsr")
            vg, bvg = T([128, D], F32, "vg")
            cw, bcw = T([128, 32, 5], F32, "cw")
            cbias, bcbias = T([128, 32], F32, "cbias")
            diag, bdiag = T([128, 160, 128], BF16, "diag")
            self.ld(wsT[:], self.wsT_d[:, :, :], [bwsT])
            self.ld(bsr[:], self.bsrow_d[:, :], [bbsr])
            self.ld(vg[:], self.vgain[:, :], [bvg])
            self.ld(cw[:], self.convw_d[:, :, :], [bcw])
            self.ld(cbias[:], self.convb_d[:, :], [bcbias])
            for kc in range(32):
                for j in range(5):
                    self.V(lambda kc=kc, j=j: nc.vector.tensor_scalar(diag[:, kc * 5 + j, :], self.identb,
                                                                       cw[:, kc, j:j + 1], None, ALU.mult),
                           [self.bcb, bcw], [bdiag])
            ones1 = self.cb[0:1, 5, :]
            xin = [T([128, 32, 260], BF16, "xin") for _ in range(1)]
            xcT = [T([128, 32, 256], BF16, "xcT") for _ in range(1)]
            ut = [T([128, 16, 256], BF16, "ut") for _ in range(1)]
            zat = [T([128, 16, 256], BF16, "zat") for _ in range(1)]
            vt = [T([128, D], BF16, "vt") for _ in range(2)]
            vn, bvn = T([128, D], BF16, "vn")
            tmp, btmp = T([128, D], F32, "tmp")
            mixa = [T([128, 16, 256], BF16, "mixa") for _ in range(1)]
            xtk = [T([128, 3072], BF16, "xtk") for _ in range(2)]
            st2 = [T([128, 4], F32, "st2") for _ in range(2)]
            for blk in range(12):
                t0 = blk * 256
                if blk < 8:
                    s0, s1 = 0, 2048
                else:
                    s0 = t0
                    s1 = t0 + 256
                xi, bxi = xin[0]
                xc, bxc = xcT[0]
                lo = max(s0, t0 - 2)
                hi = min(s1, t0 + 258)
                if lo > t0 - 2:
                    self.V(lambda: nc.vector.memset(xi[:, :, 0:2], 0.0), [], [bxi])
                if hi < t0 + 258:
                    self.V(lambda: nc.vector.memset(xi[:, :, 258:260], 0.0), [], [bxi])
                self.ld(xi[:, :, lo - (t0 - 2):hi - (t0 - 2)], self.XBCT[:, :, lo:hi].rearrange("k p t -> p k t"), [bxi])
                u, bu = ut[0]
                za, bza = zat[0]
                self.ld(u[:], self.UT[:, :, t0:t0 + 256].rearrange("k p t -> p k t"), [bu])
                self.ld(za[:], self.ZAT[:, :, t0:t0 + 256].rearrange("k p t -> p k t"), [bza])
                for kc in range(32):
                    bi = self.nextbank(4, 8)
                    pa, pb = self.ps(bi)
                    for j in range(5):
                        self.P(lambda kc=kc, j=j: nc.tensor.matmul(pa[:, 0:256], diag[:, kc * 5 + j, :], xi[:, kc, j:j + 256],
                                                                   start=(j == 0), stop=(j == 4)),
                               [bdiag, bxi], [pb], inc=(j == 4))
                    self.A(lambda kc=kc: nc.scalar.activation(out=xc[:, kc, :], in_=pa[:, 0:256], func=AF.Silu,
                                                              bias=cbias[:, kc:kc + 1]), [pb, bcbias], [bxc])
                self.stq(self.BCT[:, :, t0:t0 + 256].rearrange("k p t -> p k t"), xc[:, 16:32, :], [bxc])
                mx, bmx = mixa[0]
                for ch in range(2):
                    c = blk * 2 + ch
                    rows = slice(c * 128, (c + 1) * 128)
                    xk, bxk = xtk[c % 2]
                    for grp in range(3):
                        bi = self.nextbank(4, 8)
                        pa, pb = self.ps(bi)
                        pab = pa.bitcast(BF16)
                        for q in range(8):
                            kc = grp * 8 + q
                            self.P(lambda kc=kc, q=q: nc.tensor.transpose(pab[:, q * 128:(q + 1) * 128],
                                                                           xc[:, kc, ch * 128:(ch + 1) * 128], self.identb),
                                   [bxc, self.bcb], [pb], inc=(q == 7))
                        if grp == 1:
                            self.A(lambda: nc.scalar.copy(out=xk[:, grp * 1024:(grp + 1) * 1024], in_=pab), [pb], [bxk])
                        else:
                            self.V(lambda: nc.vector.tensor_copy(out=xk[:, grp * 1024:(grp + 1) * 1024], in_=pab), [pb], [bxk])
                    self.stq(self.XSB[rows, :], xk[:], [bxk])
                    v, bv = vt[c % 2]
                    s2, bs2 = st2[c % 2]
                    self.ld(v[:], self.Vs[rows, :], [bv])
                    self.A(lambda: nc.scalar.activation(out=tmp[:], in_=v[:], func=AF.Identity, accum_out=s2[:, 0:1]),
                           [bv], [btmp, bs2])
                    self.A(lambda: nc.scalar.activation(out=tmp[:], in_=v[:], func=AF.Square, accum_out=s2[:, 1:2]),
                           [bv], [btmp, bs2])
                    self.V(lambda: nc.vector.tensor_scalar(s2[:, 0:2], s2[:, 0:2], 1.0 / D, None, ALU.mult), [bs2], [bs2])
                    self.V(lambda: nc.vector.tensor_tensor(out=s2[:, 2:3], in0=s2[:, 0:1], in1=s2[:, 0:1], op=ALU.mult), [bs2], [bs2])
                    self.V(lambda: nc.vector.tensor_tensor(out=s2[:, 1:2], in0=s2[:, 1:2], in1=s2[:, 2:3], op=ALU.subtract), [bs2], [bs2])
                    self.rstd(st, s2[:, 1:2], bs2, 1.0)
                    self.V(lambda: nc.vector.scalar_tensor_tensor(out=s2[:, 3:4], in0=s2[:, 0:1], scalar=-1.0, in1=s2[:, 1:2],
                                                                  op0=ALU.mult, op1=ALU.mult), [bs2], [bs2])
                    self.A(lambda: nc.scalar.activation(out=tmp[:], in_=v[:], func=AF.Identity, scale=s2[:, 1:2],
                                                        bias=s2[:, 3:4]), [bv, bs2], [btmp])
                    self.V(lambda: nc.vector.tensor_tensor(out=vn[:], in0=tmp[:], in1=vg[:], op=ALU.mult), [btmp, bvg], [bvn])
                    for g in range(16):
                        bi = g // 4
                        pa, pb = self.ps(bi)
                        po = pa[:, (g % 4) * 128:(g % 4 + 1) * 128]
                        self.P(lambda g=g: nc.tensor.matmul(po, vn[:, g * 128:(g + 1) * 128], wsT[:, g, :], start=True, stop=False),
                               [bvn, bwsT], [pb], inc=False)
                        self.P(lambda g=g: nc.tensor.matmul(po, ones1, bsr[0:1, g * 128:(g + 1) * 128], start=False, stop=True),
                               [self.bcb, bbsr], [pb], inc=(g % 4 == 3))
                    pbs = [self.bbuf[i] for i in range(4)]
                    self.V(lambda: nc.vector.tensor_tensor(out=tmp[:].rearrange("p (g t) -> p g t", g=16),
                                                           in0=self.psA[:, :].rearrange("p (g t) -> p g t", g=16),
                                                           in1=u[:, :, ch * 128:(ch + 1) * 128], op=ALU.mult),
                           pbs + [bu], [btmp])
                    self.V(lambda: nc.vector.tensor_tensor(out=mx[:, :, ch * 128:(ch + 1) * 128],
                                                           in0=tmp[:].rearrange("p (g t) -> p g t", g=16),
                                                           in1=za[:, :, ch * 128:(ch + 1) * 128], op=ALU.mult),
                           [btmp, bza], [bmx])
                self.stq(self.MIXT[0:16, :, t0:t0 + 256].rearrange("k p t -> p k t"), mx[:], [bmx])
            self.cx.barrier()

    def load_state(self, H, bH, src, tmpf, btmpf):
        nc = self.nc
        self.ld(tmpf[:].rearrange("p (k n) -> p k n", k=16), src.rearrange("(k p) n -> p k n", p=128), [btmpf])
        for grp in range(4):
            bi = self.nextbank(0, 4)
            pa, pb = self.ps(bi)
            for q in range(4):
                k = grp * 4 + q
                self.P(lambda k=k, q=q: nc.tensor.transpose(pa[:, q * 128:(q + 1) * 128], tmpf[:, k * 128:(k + 1) * 128],
                                                             self.identf), [btmpf, self.bcf], [pb], inc=(q == 3))
            self.A(lambda: nc.scalar.copy(out=H[:, grp * 512:(grp + 1) * 512], in_=pa), [pb], [bH])

    def store_state(self, H, bH, dst, tmpf, btmpf):
        nc = self.nc
        for grp in range(4):
            bi = self.nextbank(0, 4)
            pa, pb = self.ps(bi)
            for q in range(4):
                k = grp * 4 + q
                self.P(lambda k=k, q=q: nc.tensor.transpose(pa[:, q * 128:(q + 1) * 128], H[:, k * 128:(k + 1) * 128],
                                                             self.identf), [bH, self.bcf], [pb], inc=(q == 3))
            self.A(lambda: nc.scalar.copy(out=tmpf[:, grp * 512:(grp + 1) * 512], in_=pa), [pb], [btmpf])
        self.stq(dst.rearrange("(k p) n -> p k n", p=128), tmpf[:].rearrange("p (k n) -> p k n", k=16), [btmpf])

    def state_update(self, H, bH, xw, bxw, xk, bxk, dec, be):
        nc = self.nc
        for g in range(8):
            pa, pb = self.ps(g // 2)
            self.P(lambda g=g: nc.tensor.matmul(pa[:, (g % 2) * 256:(g % 2 + 1) * 256],
                                                xk[:, 2048 + g * 128:2048 + (g + 1) * 128], xw[:, g * 256:(g + 1) * 256],
                                                start=True, stop=True), [bxk, bxw], [pb], inc=(g % 2 == 1))
        pbs = [self.bbuf[i] for i in range(4)]
        H3 = H[:].rearrange("p (h q) -> p h q", h=32)
        self.G(lambda: nc.gpsimd.tensor_tensor(out=H3, in0=H3, in1=dec.unsqueeze(2).to_broadcast([128, 32, 64]),
                                               op=ALU.mult), [bH, be], [bH])
        self.V(lambda: nc.vector.tensor_tensor(out=H[:], in0=H[:], in1=self.psA[:, :], op=ALU.add), [bH] + pbs, [bH])

    def stage_l0_ssd(self):
        nc = self.nc
        with ExitStack() as st:
            T = lambda shape, dt, nm: self.tile(st, shape, dt, nm)
            DT_, bDT = T([128, 24, 64], F32, "DTall")
            DA_, bDA = T([128, 24, 64], F32, "DAall")
            E_, bE = T([128, 24, 128], F32, "Eall")
            CU_, bCU = T([128, 24, 64], F32, "CUall")
            with ExitStack() as s2:
                T2 = lambda shape, dt, nm: self.tile(s2, shape, dt, nm)
                dtb, bdtb = T2([128, 64], F32, "dtb")
                a, ba = T2([128, 64], F32, "a")
                tl = [T2([128, 1536], F32, "tl") for _ in range(3)]
                self.ld(dtb[:], self.dtb_d[:, :], [bdtb])
                self.ld(a[:], self.alog_d[:, :], [ba])
                self.ld(DT_[:], self.DT.rearrange("(c p) k -> p c k", p=128), [bDT])
                self.A(lambda: nc.scalar.activation(out=a[:], in_=a[:], func=AF.Exp), [ba], [ba])
                self.V(lambda: nc.vector.tensor_scalar(a[:], a[:], -1.0, None, ALU.mult), [ba], [ba])
                self.V(lambda: nc.vector.tensor_tensor(out=DT_[:], in0=DT_[:], in1=dtb[:].unsqueeze(1).to_broadcast([128, 24, 64]),
                                                       op=ALU.add), [bDT, bdtb], [bDT])
                self.V(lambda: nc.vector.tensor_scalar(DT_[:], DT_[:], 40.0, None, ALU.min), [bDT], [bDT])
                self.A(lambda: nc.scalar.activation(out=DT_[:], in_=DT_[:], func=AF.Exp), [bDT], [bDT])
                self.A(lambda: nc.scalar.activation(out=DT_[:], in_=DT_[:], func=AF.Ln, bias=1.0), [bDT], [bDT])
                self.V(lambda: nc.vector.tensor_tensor(out=DA_[:], in0=DT_[:], in1=a[:].unsqueeze(1).to_broadcast([128, 24, 64]),
                                                       op=ALU.mult), [bDT, ba], [bDA])
                flat = DA_[:].rearrange("p c k -> p (c k)")
                for mi, (t_, bt_) in zip((1, 2, 5), tl):
                    for q in range(3):
                        pa, pb = self.ps(q)
                        self.P(lambda: nc.tensor.matmul(pa, self.cf[:, mi, :], flat[:, q * 512:(q + 1) * 512], start=True, stop=True),
                               [self.bcf, bDA], [pb])
                        self.A(lambda: nc.scalar.copy(out=t_[:, q * 512:(q + 1) * 512], in_=pa), [pb], [bt_])
                v3 = lambda t_: t_[:].rearrange("p (c k) -> p c k", c=24)
                self.V(lambda: nc.vector.tensor_copy(out=E_[:, :, 0:32], in_=v3(tl[0][0])[:, :, 0:32]), [tl[0][1]], [bE])
                self.V(lambda: nc.vector.tensor_copy(out=E_[:, :, 32:64], in_=v3(tl[1][0])[:, :, 32:64]), [tl[1][1]], [bE])
                self.V(lambda: nc.vector.tensor_copy(out=E_[:, :, 64:128], in_=v3(tl[2][0])), [tl[2][1]], [bE])
                self.V(lambda: nc.vector.tensor_tensor(out=CU_[:], in0=E_[:, :, 64:128], in1=E_[:, :, 0:64],
                                                       op=ALU.subtract), [bE], [bCU])
                self.A(lambda: nc.scalar.activation(out=CU_[:], in_=CU_[:], func=AF.Exp), [bCU], [bCU])
                self.V(lambda: nc.vector.tensor_tensor(out=CU_[:], in0=CU_[:], in1=DT_[:], op=ALU.mult),
                       [bCU, bDT], [bCU])
                self.A(lambda: nc.scalar.activation(out=E_[:], in_=E_[:], func=AF.Exp), [bE], [bE])
                self.cx.barrier()
            cst4 = (DT_, bDT, DA_, bDA, E_, bE, CU_, bCU)
            self.sweep_bwd(cst4)
            self.cx.barrier()
            self.sweep_fwd(cst4)
            self.cx.barrier()

    def sweep_bwd(self, cst4):
        nc = self.nc
        DT_, bDT, DA_, bDA, E_, bE, CU_, bCU = cst4
        with ExitStack() as st:
            T = lambda shape, dt, nm: self.tile(st, shape, dt, nm)
            H, bH = T([128, D], F32, "HB")
            tmpf, btmpf = T([128, D], F32, "tmpf")
            xtk = [T([128, 3072], BF16, "xtk") for _ in range(2)]
            xw = [T([128, D], BF16, "xw") for _ in range(2)]
            snap = [T([128, D], BF16, "snap") for _ in range(2)]
            seqs = [(0, 16, None)] + [(16 + 2 * p, 2, p) for p in range(4)]
            order = []
            for (c0, n, p) in seqs:
                for c in range(c0 + n - 1, c0 - 1, -1):
                    order.append((c, c == c0 + n - 1, c == c0, p))

            def loads(i):
                c = order[i][0]
                xk, bxk = xtk[i % 2]
                self.stq(xk[:], self.XSB[c * 128:(c + 1) * 128, :], writes=[bxk])
            loads(0)
            for i, (c, first, last, p) in enumerate(order):
                if i + 1 < len(order):
                    loads(i + 1)
                if first:
                    if p is None:
                        self.load_state(H, bH, self.sb_d, tmpf, btmpf)
                    else:
                        self.V(lambda: nc.vector.memset(H[:], 0.0), [], [bH])
                k = i % 2
                xk, bxk = xtk[k]
                sn, bsn = snap[k]
                self.A(lambda: nc.scalar.copy(out=sn[:], in_=H[:]), [bH], [bsn])
                self.stq(self.HBS[c, :, :], sn[:], [bsn])
                w_, bw_ = xw[k]
                self.G(lambda: nc.gpsimd.tensor_tensor(out=w_[:].rearrange("p (h q) -> p h q", h=32),
                                                       in0=xk[:, 0:D].rearrange("p (h q) -> p h q", h=32),
                                                       in1=CU_[:, c, 32:64].unsqueeze(2).to_broadcast([128, 32, 64]),
                                                       op=ALU.mult), [bxk, bCU], [bw_])
                self.state_update(H, bH, w_, bw_, xk, bxk, E_[:, c, 96:128], bE)
                if last and p is not None:
                    self.store_state(H, bH, self.nb_o[p, :, :], tmpf, btmpf)

    def sweep_fwd(self, cst4):
        nc = self.nc
        DT_, bDT, DA_, bDA, E_, bE, CU_, bCU = cst4
        with ExitStack() as st:
            T = lambda shape, dt, nm: self.tile(st, shape, dt, nm)
            H, bH = T([128, D], F32, "HF")
            tmpf, btmpf = T([128, D], F32, "tmpf")
            tmpg, btmpg = T([128, D], F32, "tmpg")
            yacc, byacc = T([128, D], F32, "yacc")
            Dbc, bDbc = T([128, D], F32, "Dbc")
            sn_, bsn_ = T([128, D], F32, "ssmn")
            self.ld(Dbc[:], self.dsk_d[:, 0, :], [bDbc])
            self.ld(tmpf[:], self.dsk_d[:, 1, :], [btmpf])
            self.ld(sn_[:], self.ssmn_d[:, :], [bsn_])
            self.V(lambda: nc.vector.tensor_tensor(out=Dbc[:], in0=Dbc[:], in1=tmpf[:], op=ALU.add), [bDbc, btmpf], [bDbc])
            xtk = [T([128, 3072], BF16, "xtk") for _ in range(3)]
            zb = [T([128, D], BF16, "zb") for _ in range(2)]
            hbb = [T([128, D], BF16, "hbb") for _ in range(2)]
            bct = [T([128, 16, 128], BF16, "bct") for _ in range(3)]
            hfb, bhfb = T([128, D], BF16, "hfb")
            R = [T([128, 32, 128], BF16, "R") for _ in range(1)]
            MmP = [[T([128, 32, 128], BF16, "Mm") for _ in range(2)] for _ in range(2)]
            CB = [T([128, 8, 128], BF16, "CB") for _ in range(2)]
            LT = [T([128, 512], BF16, "LT") for _ in range(2)]
            xdt = [T([128, D], BF16, "xdt") for _ in range(2)]
            xw_, bxw_ = T([128, D], BF16, "xwf")
            ybf, bybf = T([128, D], BF16, "ybf")
            ybT = [T([128, 16, 128], BF16, "ybT") for _ in range(1)]
            s8, bs8 = T([128, 8], F32, "s8")
            LEb = self.cb[:, 1, :]
            GEb = self.cb[:, 2, :]
            GTb = self.cb[:, 3, :]
            LTb = self.cb[:, 4, :]
            seqs = [(0, 16, None)] + [(16 + 2 * p, 2, p) for p in range(4)]
            order = []
            for (c0, n, p) in seqs:
                for c in range(c0, c0 + n):
                    order.append((c, c == c0, c == c0 + n - 1, p))

            def loads(i):
                c = order[i][0]
                k = i % 2
                rows = slice(c * 128, (c + 1) * 128)
                k3 = i % 3
                self.stq(bct[k3][0][:], self.BCT[:, :, c * 128:(c + 1) * 128].rearrange("k p t -> p k t"), writes=[bct[k3][1]])
                self.stq(xtk[k3][0][:], self.XSB[rows, :], writes=[xtk[k3][1]])

            def loadsB(i):
                c = order[i][0]
                k = i % 2
                rows = slice(c * 128, (c + 1) * 128)
                self.stq(zb[k][0][:], self.ZB[rows, :], writes=[zb[k][1]])
                self.stq(hbb[k][0][:], self.HBS[c, :, :], writes=[hbb[k][1]])
            def front(i):
                c = order[i][0]
                k = i % 2
                bc, bbc = bct[i % 3]
                Mm = MmP[k]
                pbs45 = [self.bbuf[4], self.bbuf[5]]
                for g in range(8):
                    pa, pb = self.ps(4 + g // 4)
                    self.P(lambda g=g: nc.tensor.matmul(pa[:, (g % 4) * 128:(g % 4 + 1) * 128], bc[:, g, :], bc[:, 8 + g, :],
                                                        start=True, stop=True), [bbc], [pb], inc=(g % 4 == 3))
                cbps = self.psB[:, 0:1024].rearrange("p (g l) -> p g l", g=8)
                self.V(lambda: nc.vector.tensor_tensor(out=CB[0][0][:], in0=cbps,
                                                       in1=LEb.unsqueeze(1).to_broadcast([128, 8, 128]), op=ALU.mult),
                       pbs45 + [self.bcb], [CB[0][1]])
                self.V(lambda: nc.vector.tensor_tensor(out=CB[1][0][:], in0=cbps,
                                                       in1=GEb.unsqueeze(1).to_broadcast([128, 8, 128]), op=ALU.mult),
                       pbs45 + [self.bcb], [CB[1][1]])
                for d in range(2):
                    Rt, bR = R[0]
                    Mt, bMm = Mm[d]
                    msk = LEb if d == 0 else GEb
                    lm = GTb if d == 0 else LTb
                    for h in range(32):
                        self.V(lambda h=h: nc.vector.tensor_scalar(Rt[:, h, :], msk, DA_[:, c, d * 32 + h:d * 32 + h + 1], None,
                                                                    ALU.mult), [self.bcb, bDA], [bR], so=(h > 0))
                    for q in range(8):
                        bi = 6 + (q % 2)
                        pa, pb = self.ps(bi)
                        self.P(lambda q=q: nc.tensor.matmul(pa, lm, Rt[:, 4 * q:4 * q + 4, :].rearrange("p h l -> p (h l)"),
                                                            start=True, stop=True), [self.bcb, bR], [pb])
                        lt, blt = LT[q % 2]
                        self.A(lambda: nc.scalar.activation(out=lt[:], in_=pa, func=AF.Exp), [pb], [blt])
                        self.V(lambda q=q: nc.vector.tensor_tensor(out=Mt[:, 4 * q:4 * q + 4, :],
                                                                   in0=lt[:].rearrange("p (h l) -> p h l", h=4),
                                                                   in1=CB[d][0][:, q:q + 1, :].to_broadcast([128, 4, 128]),
                                                                   op=ALU.mult), [blt, CB[d][1]], [bMm], so=(q > 0))
            loads(0)
            loads(1)
            loadsB(0)
            front(0)
            for i, (c, first, last, p) in enumerate(order):
                if i + 2 < len(order):
                    loads(i + 2)
                if i + 1 < len(order):
                    loadsB(i + 1)
                    front(i + 1)
                Mm = MmP[i % 2]
                if first:
                    if p is None:
                        self.load_state(H, bH, self.sf_d, tmpf, btmpf)
                    else:
                        self.V(lambda: nc.vector.memset(H[:], 0.0), [], [bH])
                k = i % 2
                xk, bxk = xtk[i % 3]
                z, bz = zb[k]
                hb_, bhb_ = hbb[k]
                bc, bbc = bct[i % 3]
                x3 = xk[:, 0:D].rearrange("p (h q) -> p h q", h=32)
                self.A(lambda: nc.scalar.copy(out=hfb[:], in_=H[:]), [bH], [bhfb])
                for d in range(2):
                    xd, bxd = xdt[d]
                    self.G(lambda d=d: nc.gpsimd.tensor_tensor(out=xd[:].rearrange("p (h q) -> p h q", h=32), in0=x3,
                                                               in1=DT_[:, c, d * 32:(d + 1) * 32].unsqueeze(2).to_broadcast([128, 32, 64]),
                                                               op=ALU.mult), [bxk, bDT], [bxd])
                self.G(lambda: nc.gpsimd.tensor_tensor(out=tmpg[:], in0=xk[:, 0:D], in1=Dbc[:], op=ALU.mult), [bxk, bDbc], [btmpg])
                self.G(lambda: nc.gpsimd.tensor_tensor(out=xw_[:].rearrange("p (h q) -> p h q", h=32), in0=x3,
                                                       in1=CU_[:, c, 0:32].unsqueeze(2).to_broadcast([128, 32, 64]),
                                                       op=ALU.mult), [bxk, bCU], [bxw_])
                for h in range(32):
                    pa, pb = self.ps(h // 8)
                    po = pa[:, (h % 8) * 64:(h % 8 + 1) * 64]
                    self.P(lambda h=h: nc.tensor.matmul(po, Mm[0][0][:, h, :], xdt[0][0][:, h * 64:(h + 1) * 64],
                                                        start=True, stop=False), [Mm[0][1], xdt[0][1]], [pb], inc=False)
                    self.P(lambda h=h: nc.tensor.matmul(po, Mm[1][0][:, h, :], xdt[1][0][:, h * 64:(h + 1) * 64],
                                                        start=False, stop=True), [Mm[1][1], xdt[1][1]], [pb],
                           inc=(h % 8 == 7))
                pbsA = [self.bbuf[i] for i in range(4)]
                self.V(lambda: nc.vector.tensor_tensor(out=yacc[:], in0=tmpg[:], in1=self.psA[:, :], op=ALU.add),
                       [btmpg] + pbsA, [byacc])
                for d in range(2):
                    Hs, bHs = (hfb, bhfb) if d == 0 else (hb_, bhb_)
                    tt, btt = (tmpf, btmpf) if d == 0 else (tmpg, btmpg)
                    for g in range(8):
                        pa, pb = self.ps(g // 2)
                        self.P(lambda g=g: nc.tensor.matmul(pa[:, (g % 2) * 256:(g % 2 + 1) * 256], bc[:, 8 + g, :],
                                                            Hs[:, g * 256:(g + 1) * 256], start=True, stop=True),
                               [bbc, bHs], [pb], inc=(g % 2 == 1))
                    self.V(lambda d=d: nc.vector.tensor_tensor(out=tt[:].rearrange("p (h q) -> p h q", h=32),
                                                               in0=self.psA[:, :].rearrange("p (h q) -> p h q", h=32),
                                                               in1=E_[:, c, d * 32:(d + 1) * 32].unsqueeze(2).to_broadcast([128, 32, 64]),
                                                               op=ALU.mult), pbsA + [bE], [btt])
                    self.V(lambda: nc.vector.tensor_tensor(out=yacc[:], in0=yacc[:], in1=tt[:], op=ALU.add),
                           [byacc, btt], [byacc])
                self.state_update(H, bH, xw_, bxw_, xk, bxk, E_[:, c, 64:96], bE)
                self.V(lambda: nc.vector.tensor_tensor(out=yacc[:], in0=yacc[:], in1=z[:], op=ALU.mult), [byacc, bz], [byacc])
                for g in range(8):
                    self.A(lambda g=g: nc.scalar.activation(out=ybf[:, g * 256:(g + 1) * 256], in_=yacc[:, g * 256:(g + 1) * 256],
                                                            func=AF.Square, accum_out=s8[:, g:g + 1]), [byacc], [bybf, bs8], so=(g > 0))
                self.rstd(st, s8[:], bs8, 1.0 / 256)
                for g in range(8):
                    self.V(lambda g=g: nc.vector.scalar_tensor_tensor(out=ybf[:, g * 256:(g + 1) * 256], in0=yacc[:, g * 256:(g + 1) * 256],
                                                                      scalar=s8[:, g:g + 1], in1=sn_[:, g * 256:(g + 1) * 256],
                                                                      op0=ALU.mult, op1=ALU.mult), [byacc, bs8, bsn_], [bybf], so=(g > 0))
                yT, byT = ybT[0]
                for half in range(2):
                    bi = 4 + half
                    pa, pb = self.ps(bi)
                    pab = pa.bitcast(BF16)
                    for q in range(8):
                        kc = half * 8 + q
                        self.P(lambda kc=kc, q=q: nc.tensor.transpose(pab[:, q * 128:(q + 1) * 128],
                                                                       ybf[:, kc * 128:(kc + 1) * 128], self.identb),
                               [bybf, self.bcb], [pb], inc=(q == 7))
                    self.A(lambda: nc.scalar.copy(out=yT[:, half * 8:(half + 1) * 8, :],
                                                  in_=pab.rearrange("p (q t) -> p q t", q=8)), [pb], [byT])
                self.stq(self.MIXT[16:32, :, c * 128:(c + 1) * 128].rearrange("k p t -> p k t"), yT[:], [byT])
                if last and p is not None:
                    self.store_state(H, bH, self.nf_o[p, :, :], tmpf, btmpf)

    def stage_out(self, layer):
        nc = self.nc
        KC = 32 if layer == 0 else 16
        W = self.w_out0 if layer == 0 else self.w_out1
        xsrc = self.xtok if layer == 0 else self.X1
        dst = self.X1 if layer == 0 else self.y_o
        Wv = W.rearrange("(kc p) n -> p kc n", p=128)
        TCH = 6 if layer == 0 else 4
        NB = 1 if layer == 0 else 2
        SW = 256 if KC == 32 else 512
        with ExitStack() as st:
            T = lambda shape, dt, nm: self.tile(st, shape, dt, nm)
            mTs = [T([128, KC, TCH * 128], BF16, "mT") for _ in range(NB)]
            wsl = [T([128, KC, SW], BF16, "wo") for _ in range(2)]
            G2 = [T([128, D], F32, "G2") for _ in range(2)]
            for c in range(2):
                self.ld(G2[c][0][:], self.MS[c, :, 2 * D:3 * D], [G2[c][1]])
            oaccs = [[T([128, D], F32, "oacc") for _ in range(TCH)] for _ in range(NB)]
            xb = [T([128, D], F32, "xb") for _ in range(2)]
            junk, bjunk = T([128, D], BF16, "junk")
            ss = [T([128, 1], F32, "ss") for _ in range(2)]
            NP = NCH // TCH

            def ldm(pp):
                t_, b_ = mTs[pp % NB]
                self.ld(t_[:], self.MIXT[0:KC, :, pp * TCH * 128:(pp + 1) * TCH * 128].rearrange("k p t -> p k t"), [b_])
            ldm(0)
            def epi(pp, j):
                c = pp * TCH + j
                cond = 0 if c < 16 else 1
                Mt, bM = G2[cond]
                o, bo = oaccs[pp % NB][j]
                x, bx = xb[c % 2]
                s1, bs1 = ss[c % 2]
                rows = slice(c * 128, (c + 1) * 128)
                self.stq(x[:], xsrc[rows, :], writes=[bx])
                self.A(lambda: nc.scalar.activation(out=junk[:], in_=o[:], func=AF.Square, accum_out=s1[:]),
                       [bo], [bjunk, bs1])
                self.rstd(st, s1[:], bs1, 1.0 / D)
                self.V(lambda: nc.vector.scalar_tensor_tensor(out=o[:], in0=o[:], scalar=s1[:], in1=Mt[:],
                                                              op0=ALU.mult, op1=ALU.mult), [bo, bs1, bM], [bo])
                self.V(lambda: nc.vector.tensor_tensor(out=o[:], in0=o[:], in1=x[:], op=ALU.add), [bo, bx], [bo])
                self.stq(dst[rows, :], o[:], [bo])

            NS = D // SW
            for pas in range(NP):
                c0 = pas * TCH
                mT, bmT = mTs[pas % NB]
                oacc = oaccs[pas % NB]
                if NB == 2 and pas + 1 < NP:
                    ldm(pas + 1)
                elif NB == 1 and pas > 0:
                    ldm(pas)
                for s in range(NS):
                    w, bw = wsl[s % 2]
                    self.ld(w[:], Wv[:, :, s * SW:(s + 1) * SW], [bw])
                    for j in range(TCH):
                        bi = self.nextbank()
                        pa, pb = self.ps(bi)
                        pa = pa[:, 0:SW]
                        for kc in range(KC):
                            self.P(lambda kc=kc: nc.tensor.matmul(pa, mT[:, kc, j * 128:(j + 1) * 128], w[:, kc, :],
                                                                  start=(kc == 0), stop=(kc == KC - 1)),
                                   [bmT, bw], [pb], inc=(kc == KC - 1))
                        o, bo = oacc[j]
                        self.A(lambda: nc.scalar.copy(out=o[:, s * SW:(s + 1) * SW], in_=pa), [pb], [bo])
                    if NB == 2 and pas > 0:
                        for j in range(s * TCH // NS, (s + 1) * TCH // NS):
                            epi(pas - 1, j)
                if NB == 1:
                    for j in range(TCH):
                        epi(pas, j)
            if NB == 2:
                for j in range(TCH):
                    epi(NP - 1, j)
            self.cx.barrier()

    def stage_attn(self):
        nc = self.nc
        SCALE = 128 ** -0.5
        with ExitStack() as st:
            T = lambda shape, dt, nm: self.tile(st, shape, dt, nm)
            kT, bkT = T([128, 4, 2560], BF16, "kT")
            va, bva = T([128, 20, 4, 132], BF16, "va")
            kt = [T([128, 512], BF16, "kt") for _ in range(2)]
            qt = [T([128, D], BF16, "qt") for _ in range(2)]
            qT, bqT = T([128, 16, 512], BF16, "qT")
            zt = [T([128, D], BF16, "zt") for _ in range(4)]
            pt = [T([128, 512], BF16, "pt") for _ in range(3)]
            mix = [T([128, D], BF16, "mix") for _ in range(4)]
            mxT = [T([128, 16, 128], BF16, "mxT") for _ in range(2)]
            rc = [T([128, 1], F32, "rc") for _ in range(4)]
            of = [T([128, 128], F32, "of") for _ in range(2)]
            self.V(lambda: nc.vector.memset(va[:], 1.0), [], [bva])
            seqs = [(0, 16, None)] + [(16 + 2 * p, 2, p) for p in range(4)]
            it = 0
            for (c0, n, p) in seqs:
                kch = ([NT // 128 + r for r in range(4)] if p is None else []) + list(range(c0, c0 + n))
                nk = len(kch)
                for i, kc_ in enumerate(kch):
                    it += 1
                    t, bt = kt[it % 2]
                    rows = slice(kc_ * 128, (kc_ + 1) * 128)
                    self.ld(t[:], self.Ks[rows, :], [bt])
                    self.ld(va[:, i, :, 0:128], self.V1[rows, :].rearrange("t (h d) -> t h d", h=4), [bva])
                    bi = self.nextbank(6, 8)
                    pa, pb = self.ps(bi)
                    pab = pa.bitcast(BF16)
                    for h in range(4):
                        self.P(lambda h=h: nc.tensor.transpose(pab[:, h * 128:(h + 1) * 128], t[:, h * 128:(h + 1) * 128],
                                                               self.identb), [bt, self.bcb], [pb], inc=(h == 3))
                    self.V(lambda: nc.vector.tensor_copy(out=kT[:, :, i * 128:(i + 1) * 128],
                                                         in_=pab[:, 0:512].rearrange("p (h t) -> p h t", h=4)), [pb], [bkT])
                nqc = min(4, n)
                for qt0 in range(c0, c0 + n, nqc):
                    nq = nqc * 128
                    for j in range(nqc):
                        c = qt0 + j
                        it += 1
                        q_, bq_ = qt[it % 2]
                        rows = slice(c * 128, (c + 1) * 128)
                        self.ld(q_[:], self.Qs[rows, :], [bq_])
                        self.ld(zt[j][0][:], self.Z1[rows, :], [zt[j][1]])
                        for half in range(2):
                            bi = self.nextbank(6, 8)
                            pa, pb = self.ps(bi)
                            pab = pa.bitcast(BF16)
                            for q in range(8):
                                hh = half * 8 + q
                                self.P(lambda hh=hh, q=q: nc.tensor.transpose(pab[:, q * 128:(q + 1) * 128],
                                                                               q_[:, hh * 128:(hh + 1) * 128], self.identb),
                                       [bq_, self.bcb], [pb], inc=(q == 7))
                            self.V(lambda: nc.vector.tensor_copy(out=qT[:, half * 8:(half + 1) * 8, j * 128:(j + 1) * 128],
                                                                 in_=pab.rearrange("p (q t) -> p q t", q=8)), [pb], [bqT])
                    iters = [(h, i) for h in range(16) for i in range(nk)]

                    def qk(t):
                        h, i = iters[t]
                        pa, pb = self.ps(4 + t % 2)
                        self.P(lambda: nc.tensor.matmul(pa[:, 0:nq], kT[:, h // 4, i * 128:(i + 1) * 128], qT[:, h, 0:nq],
                                                        start=True, stop=True), [bkT, bqT], [pb])
                    qk(0)
                    for t, (h, i) in enumerate(iters):
                        kh = h // 4
                        if t + 1 < len(iters):
                            qk(t + 1)
                        pa, pb = self.ps(4 + t % 2)
                        p_, bp_ = pt[t % 3]
                        self.A(lambda: nc.scalar.activation(out=p_[:, 0:nq], in_=pa[:, 0:nq], func=AF.Exp, scale=SCALE),
                               [pb], [bp_])
                        for j in range(nqc):
                            po, pob = self.ps(j)
                            self.P(lambda j=j, i=i: nc.tensor.matmul(po[:, 0:129], p_[:, j * 128:(j + 1) * 128],
                                                                      va[:, i, kh, 0:129], start=(i == 0), stop=(i == nk - 1)),
                                   [bp_, bva], [pob], inc=(i == nk - 1))
                        if i == nk - 1:
                            for j in range(nqc):
                                po, pob = self.ps(j)
                                r_, br_ = rc[j]
                                self.V(lambda: nc.vector.reciprocal(r_[:], po[:, 128:129]), [pob], [br_])
                                self.V(lambda j=j: nc.vector.scalar_tensor_tensor(out=mix[j][0][:, h * 128:(h + 1) * 128],
                                                                                  in0=po[:, 0:128], scalar=r_[:],
                                                                                  in1=zt[j][0][:, h * 128:(h + 1) * 128],
                                                                                  op0=ALU.mult, op1=ALU.mult),
                                       [pob, br_, zt[j][1]], [mix[j][1]])
                    for j in range(nqc):
                        c = qt0 + j
                        it += 1
                        yT, byT = mxT[it % 2]
                        for half in range(2):
                            bi = self.nextbank(6, 8)
                            pa, pb = self.ps(bi)
                            pab = pa.bitcast(BF16)
                            for q in range(8):
                                kc = half * 8 + q
                                self.P(lambda kc=kc, q=q: nc.tensor.transpose(pab[:, q * 128:(q + 1) * 128],
                                                                               mix[j][0][:, kc * 128:(kc + 1) * 128], self.identb),
                                       [mix[j][1], self.bcb], [pb], inc=(q == 7))
                            self.V(lambda: nc.vector.tensor_copy(out=yT[:, half * 8:(half + 1) * 8, :],
                                                                 in_=pab.rearrange("p (q t) -> p q t", q=8)), [pb], [byT])
                        self.stq(self.MIXT[0:16, :, c * 128:(c + 1) * 128].rearrange("k p t -> p k t"), yT[:], [byT])
            self.cx.barrier()


def _consts():
    k = np.arange(128)[:, None]
    m = np.arange(128)[None, :]
    cst = np.stack([np.eye(128), k <= m, k >= m, k > m, k < m, np.ones((128, 128))], axis=1).astype(np.float32)
    n = 2048
    rows = n // 64
    t_row = np.repeat(np.arange(rows), 64).astype(np.float32)
    t_col = np.tile(np.arange(64), rows).astype(np.float32)
    half = 64
    inv = (10000.0 ** (-np.arange(0, half, 2, dtype=np.float32) / half)).astype(np.float32)
    ar = t_row[:, None] * inv[None, :]
    ac = t_col[:, None] * inv[None, :]
    cos = np.concatenate([np.cos(ar), np.cos(ar), np.cos(ac), np.cos(ac)], axis=1).astype(np.float32)
    sin = np.concatenate([-np.sin(ar), np.sin(ar), -np.sin(ac), np.sin(ac)], axis=1).astype(np.float32)
    return np.ascontiguousarray(cst), cos, sin


def _bc(v, n=128):
    v = np.asarray(v, np.float32).reshape(1, -1)
    return np.ascontiguousarray(np.broadcast_to(v, (n, v.shape[1])))


def prep_core(inp, i, shared):
    f = lambda a: np.ascontiguousarray(np.asarray(a, np.float32))
    m = dict(shared)
    xs = np.asarray(inp["x_sample"][i], np.float32)
    xp = np.asarray(inp["x_prompt"][4 * i:4 * i + 4], np.float32).reshape(1024, D)
    m["xtok"] = np.ascontiguousarray(np.concatenate([xs, xp], axis=0))
    c = np.asarray(inp["c"][i], np.float32).reshape(16, 128).T
    cc = np.asarray(inp["c_ctx"], np.float32).reshape(16, 128).T
    m["condT"] = np.ascontiguousarray(np.concatenate([c, cc], axis=1))
    m["sf"] = f(np.asarray(inp["state_l0_ssm_fwd"][i]).reshape(D, 128))
    m["sb"] = f(np.asarray(inp["state_l0_ssm_bwd"][i]).reshape(D, 128))
    m["ck"] = f(np.asarray(inp["cache_l1_k"][i]).reshape(512, 512))
    m["cv"] = f(np.asarray(inp["cache_l1_v"][i]).reshape(512, 512))
    return m


def prep_shared(inp):
    f = lambda a: np.ascontiguousarray(np.asarray(a, np.float32))
    cst, cos, sin = _consts()
    s = {}
    s["mod_w0"] = f(inp["mod_w0"])
    s["mod_w1"] = f(inp["mod_w1"])
    s["mod_b0"] = _bc(inp["mod_b0"])
    s["mod_b1"] = _bc(inp["mod_b1"])
    s["npre0"] = _bc(inp["norm_pre0"])
    s["npre1"] = _bc(inp["norm_pre1"])
    s["npost0"] = _bc(inp["norm_post0"])
    s["npost1"] = _bc(inp["norm_post1"])
    s["l0_w_in"] = f(inp["l0_w_in"])
    s["l0_w_out"] = f(inp["l0_w_out"])
    s["l1_w_in"] = f(inp["l1_w_in"])
    s["l1_w_out"] = f(inp["l1_w_out"])
    s["vgain"] = _bc(inp["l0_v_gain"])
    s["wsT"] = f(np.transpose(np.asarray(inp["l0_w_s"], np.float32), (2, 0, 1)))
    s["bsrow"] = f(np.asarray(inp["l0_b_s"], np.float32).reshape(1, D))
    cw = np.asarray(inp["l0_conv_w"], np.float32)
    s["convw"] = f(np.transpose(cw.reshape(5, 32, 128), (2, 1, 0)))
    s["convb"] = f(np.asarray(inp["l0_conv_b"], np.float32).reshape(32, 128).T)
    s["dtb"] = _bc(np.asarray(inp["l0_dt_bias"], np.float32).reshape(-1))
    s["alog"] = _bc(np.asarray(inp["l0_a_log"], np.float32).reshape(-1))
    dsk = np.repeat(np.asarray(inp["l0_d_skip"], np.float32)[:, :, None], 64, axis=2).reshape(2, D)
    s["dsk"] = f(np.broadcast_to(dsk[None], (128, 2, D)))
    s["ssmn"] = _bc(inp["l0_ssm_norm"])
    qk = np.stack([np.asarray(inp["l1_q_norm"], np.float32), np.asarray(inp["l1_k_norm"], np.float32)], axis=0)
    s["qkn"] = f(np.broadcast_to(qk[None], (128, 2, 128)))
    s["ropec"] = cos
    s["ropes"] = sin
    s["cst"] = cst
    return s


def assemble(res, ncores=8):
    yp = np.zeros((4 * ncores, 256, D), np.float32)
    ys = np.zeros((ncores, 2048, D), np.float32)
    nf = np.zeros((4 * ncores, 32, 64, 128), np.float32)
    nb = np.zeros((4 * ncores, 32, 64, 128), np.float32)
    nk = np.zeros((4 * ncores, 256, 4, 128), np.float32)
    nv = np.zeros((4 * ncores, 256, 4, 128), np.float32)
    for i in range(ncores):
        r = res[i]
        y = np.asarray(r["y"])
        ys[i] = y[0:2048]
        yp[4 * i:4 * i + 4] = y[2048:].reshape(4, 256, D)
        nf[4 * i:4 * i + 4] = np.asarray(r["nfwd"]).reshape(4, 32, 64, 128)
        nb[4 * i:4 * i + 4] = np.asarray(r["nbwd"]).reshape(4, 32, 64, 128)
        nk[4 * i:4 * i + 4] = np.asarray(r["nk"]).reshape(4, 256, 4, 128)
        nv[4 * i:4 * i + 4] = np.asarray(r["nv"]).reshape(4, 256, 4, 128)
    return yp, ys, nf, nb, nk, nv


def kernel(**inputs):
    shared = prep_shared(inputs)
    in_maps = [prep_core(inputs, i, shared) for i in range(8)]
    nc = K().build()
    res = run_bass_kernel_spmd(nc, in_maps, core_ids=list(range(8)))
    return assemble(res.results, 8)
```
